# Optimizing a Trainium2 kernel written in Bass

```python
import jax, jax.numpy as jnp
from jax import lax
import numpy as np

D_MODEL = 1024
BATCH = 4
SEQ = 8192
DEPTH = 2

GRID_W = 64
CTX_LEN = 256
N_MIXERS = 2
D_FF = 4 * D_MODEL
CONV_WIDTH = 31
CONV_DIM = D_MODEL
MLSTM_INNER = 2 * D_MODEL
MLSTM_HEADS = 4
MLSTM_HEAD_DIM = MLSTM_INNER // MLSTM_HEADS
QKV_CONV_WIDTH = 5
CHUNK = 128
N_A = (DEPTH + 1) // 2
N_B = DEPTH // 2
ALPHA = (2 * DEPTH) ** 0.25
BETA = (8 * DEPTH) ** -0.25
LN_EPS = 1e-5

kernel_name = 'hybrid_conformer_mlstm_dit_block'


def layer_norm(x, g, b):
    xf = x.astype(jnp.float32)
    mu = jnp.mean(xf, axis=-1, keepdims=True)
    var = jnp.mean(jnp.square(xf - mu), axis=-1, keepdims=True)
    return ((xf - mu) * lax.rsqrt(var + LN_EPS) * g.astype(jnp.float32) + b.astype(jnp.float32)).astype(x.dtype)


def adaln(cvec, w, b):
    return jnp.split(jax.nn.silu(cvec) @ w + b, 6, axis=-1)


def modulate(x, shift, scale):
    return x * (1 + scale) + shift


def dwconv(x, w):
    return lax.conv_general_dilated(x, w[:, None, :], window_strides=(1,), padding='SAME',
                                    dimension_numbers=('NWC', 'WIO', 'NWC'),
                                    feature_group_count=x.shape[-1])


def sq_relu_mlp(u, w1, w2):
    return jnp.square(jax.nn.relu(u @ w1)) @ w2


def conformer_conv(u, rows, w_pw1, b_pw1, w_dw, b_dw, ln_g, ln_b, w_pw2, b_pw2):
    a = u @ w_pw1 + b_pw1
    a = a[..., :CONV_DIM] * jax.nn.sigmoid(a[..., CONV_DIM:])
    bsz, s, ch = a.shape
    if rows is None:
        a = dwconv(a, w_dw)
    else:
        a = dwconv(a.reshape(bsz * rows, GRID_W, ch), w_dw).reshape(bsz, s, ch)
    a = jax.nn.silu(layer_norm(a + b_dw, ln_g, ln_b))
    return a @ w_pw2 + b_pw2


def mlstm_chunk_scan(q, k, v, i_pre, f_pre, state):
    bsz, nh, s, dh = q.shape
    nc = s // CHUNK

    def to_chunks(t):
        return jnp.moveaxis(t.reshape(bsz, nh, nc, CHUNK, *t.shape[3:]), 2, 0)

    lf = jax.nn.log_sigmoid(f_pre)
    lower = jnp.tril(jnp.ones((CHUNK, CHUNK), dtype=bool))

    def step(carry, xs):
        C, n, m = carry
        qc, kc, vc, ic, lfc = xs
        bcum = jnp.cumsum(lfc, axis=-1)
        dlog = bcum[..., :, None] - bcum[..., None, :] + ic[..., None, :]
        dlog = jnp.where(lower, dlog, -jnp.inf)
        inter = bcum + m[..., None]
        m_row = jnp.maximum(inter, jnp.max(dlog, axis=-1))
        scores = jnp.einsum('bhtd,bhsd->bhts', qc, kc) * jnp.exp(dlog - m_row[..., None])
        w_inter = jnp.exp(inter - m_row)
        num = jnp.einsum('bhts,bhsd->bhtd', scores, vc) + w_inter[..., None] * jnp.einsum('bhtd,bhde->bhte', qc, C)
        den = jnp.sum(scores, axis=-1) + w_inter * jnp.einsum('bhtd,bhd->bht', qc, n)
        h = num / jnp.maximum(jnp.abs(den), jnp.exp(-m_row))[..., None]
        b_last = bcum[..., -1]
        dk = b_last[..., None] - bcum + ic
        m_new = jnp.maximum(b_last + m, jnp.max(dk, axis=-1))
        wk = jnp.exp(dk - m_new[..., None])[..., None] * kc
        decay = jnp.exp(b_last + m - m_new)
        C_new = decay[..., None, None] * C + jnp.einsum('bhsd,bhse->bhde', wk, vc)
        n_new = decay[..., None] * n + jnp.sum(wk, axis=2)
        return (C_new, n_new, m_new), h

    state, h = lax.scan(step, state, (to_chunks(q), to_chunks(k), to_chunks(v), to_chunks(i_pre), to_chunks(lf)))
    h = jnp.moveaxis(h, 0, 2).reshape(bsz, nh, s, dh)
    return h, state


def mlstm_features(u, w_up, w_conv, b_conv, w_q, w_k, w_v, w_gate, b_gate, w_o, b_o):
    up = u @ w_up
    xm, z = up[..., :MLSTM_INNER], up[..., MLSTM_INNER:]
    xc = jax.nn.silu(dwconv(xm, w_conv) + b_conv)
    q, k, v = xc @ w_q, xc @ w_k, xm @ w_v
    gates = jnp.concatenate([q, k, v], axis=-1) @ w_gate + b_gate
    o = jax.nn.sigmoid(xm @ w_o + b_o)
    return xc, z, q, k, v, gates, o


def to_heads(t):
    bsz, s, _ = t.shape
    return jnp.transpose(t.reshape(bsz, s, MLSTM_HEADS, MLSTM_HEAD_DIM), (0, 2, 1, 3)).astype(jnp.float32)


def bidir_scan(feats, states):
    _, _, q, k, v, gates, _ = feats
    qh = to_heads(q)
    kh = to_heads(k) * (MLSTM_HEAD_DIM ** -0.5)
    vh = to_heads(v)
    g = jnp.moveaxis(gates.astype(jnp.float32), -1, 1)
    i_f, f_f, i_b, f_b = jnp.split(g, 4, axis=1)
    h_f, st_f = mlstm_chunk_scan(qh, kh, vh, i_f, f_f, states[0])
    flip = lambda t: jnp.flip(t, axis=2)
    h_b, st_b = mlstm_chunk_scan(flip(qh), flip(kh), flip(vh), flip(i_b), flip(f_b), states[1])
    return h_f, flip(h_b), (st_f, st_b)


def mlstm_output(feats, h_f, h_b, gn_g, skip, w_down):
    xc, z, _, _, _, _, o = feats
    bsz, s, _ = o.shape
    from_heads = lambda t: jnp.transpose(t, (0, 2, 1, 3)).reshape(bsz, s, MLSTM_INNER)
    of = o.astype(jnp.float32)
    hsum = of[..., :MLSTM_INNER] * from_heads(h_f) + of[..., MLSTM_INNER:] * from_heads(h_b)
    hh = hsum.reshape(bsz, s, MLSTM_HEADS, MLSTM_HEAD_DIM)
    mu = jnp.mean(hh, axis=-1, keepdims=True)
    var = jnp.mean(jnp.square(hh - mu), axis=-1, keepdims=True)
    hn = ((hh - mu) * lax.rsqrt(var + LN_EPS)).reshape(bsz, s, MLSTM_INNER) * gn_g.astype(jnp.float32)
    y = (hn.astype(xc.dtype) + skip * xc) * jax.nn.silu(z)
    return y @ w_down


def mlstm_mixer(u, uc, need_ctx, w_up, w_conv, b_conv, w_q, w_k, w_v, w_gate, b_gate, w_o, b_o, gn_g, skip, w_down):
    p_in = (w_up, w_conv, b_conv, w_q, w_k, w_v, w_gate, b_gate, w_o, b_o)
    bsz = u.shape[0]
    zero_state = (jnp.zeros((bsz, MLSTM_HEADS, MLSTM_HEAD_DIM, MLSTM_HEAD_DIM), jnp.float32),
                  jnp.zeros((bsz, MLSTM_HEADS, MLSTM_HEAD_DIM), jnp.float32),
                  jnp.zeros((bsz, MLSTM_HEADS), jnp.float32))
    fc = mlstm_features(uc, *p_in)
    hf_c, hb_c, ctx_states = bidir_scan(fc, (zero_state, zero_state))
    fl = mlstm_features(u, *p_in)
    hf, hb, _ = bidir_scan(fl, ctx_states)
    y = mlstm_output(fl, hf, hb, gn_g, skip, w_down).astype(u.dtype)
    yc = mlstm_output(fc, hf_c, hb_c, gn_g, skip, w_down).astype(u.dtype) if need_ctx else None
    return y, yc


def setup_inputs(seed: int = 0) -> dict:
    key = jax.random.key(seed)
    ks = iter(jax.random.split(key, 48))

    def nrm(shape, scale):
        return jax.random.normal(next(ks), shape, jnp.float32) * scale

    D, E, H = D_MODEL, MLSTM_INNER, MLSTM_HEADS
    i_bias = nrm((N_B, 2, H), 0.1)
    f_bias = jnp.linspace(3.0, 6.0, H)[None, None, :] + nrm((N_B, 2, H), 0.1)
    b_gate = jnp.concatenate([i_bias[:, 0], f_bias[:, 0], i_bias[:, 1], f_bias[:, 1]], axis=-1)
    return {
        'x': nrm((BATCH, SEQ, D), 1.0),
        'c': nrm((BATCH, D), 1.0),
        'ctx': nrm((BATCH, CTX_LEN, D), 1.0),
        'c_ctx': nrm((D,), 1.0),
        'ada_w': nrm((DEPTH, D, 6 * D), 0.5 * D ** -0.5),
        'ada_b': nrm((DEPTH, 6 * D), 0.02),
        'ln1_g': 1.0 + nrm((DEPTH, D), 0.05),
        'ln1_b': nrm((DEPTH, D), 0.02),
        'ln2_g': 1.0 + nrm((DEPTH, D), 0.05),
        'ln2_b': nrm((DEPTH, D), 0.02),
        'mlp_w1': nrm((DEPTH, D, D_FF), D ** -0.5),
        'mlp_w2': nrm((DEPTH, D_FF, D), BETA * D_FF ** -0.5),
        'conv_w_pw1': nrm((N_A, D, 2 * CONV_DIM), D ** -0.5),
        'conv_b_pw1': nrm((N_A, 2 * CONV_DIM), 0.02),
        'conv_w_dw': nrm((N_A, CONV_WIDTH, CONV_DIM), CONV_WIDTH ** -0.5),
        'conv_b_dw': nrm((N_A, CONV_DIM), 0.02),
        'conv_ln_g': 1.0 + nrm((N_A, CONV_DIM), 0.05),
        'conv_ln_b': nrm((N_A, CONV_DIM), 0.02),
        'conv_w_pw2': nrm((N_A, CONV_DIM, D), BETA * CONV_DIM ** -0.5),
        'conv_b_pw2': nrm((N_A, D), 0.02),
        'ml_w_up': nrm((N_B, D, 2 * E), D ** -0.5),
        'ml_w_conv': nrm((N_B, QKV_CONV_WIDTH, E), QKV_CONV_WIDTH ** -0.5),
        'ml_b_conv': nrm((N_B, E), 0.02),
        'ml_w_q': nrm((N_B, E, E), E ** -0.5),
        'ml_w_k': nrm((N_B, E, E), E ** -0.5),
        'ml_w_v': nrm((N_B, E, E), E ** -0.5),
        'ml_w_gate': nrm((N_B, 3 * E, 4 * H), (3 * E) ** -0.5),
        'ml_b_gate': b_gate,
        'ml_w_o': nrm((N_B, E, 2 * E), E ** -0.5),
        'ml_b_o': nrm((N_B, 2 * E), 0.02),
        'ml_gn_g': 1.0 + nrm((N_B, E), 0.05),
        'ml_skip': 1.0 + nrm((N_B, E), 0.05),
        'ml_w_down': nrm((N_B, E, D), BETA * E ** -0.5),
    }


def reference(x, c, ctx, c_ctx, ada_w, ada_b, ln1_g, ln1_b, ln2_g, ln2_b, mlp_w1, mlp_w2,
              conv_w_pw1, conv_b_pw1, conv_w_dw, conv_b_dw, conv_ln_g, conv_ln_b, conv_w_pw2, conv_b_pw2,
              ml_w_up, ml_w_conv, ml_b_conv, ml_w_q, ml_w_k, ml_w_v, ml_w_gate, ml_b_gate, ml_w_o, ml_b_o,
              ml_gn_g, ml_skip, ml_w_down):
    rows = x.shape[1] // GRID_W
    h, hc = x, ctx
    for i in range(DEPTH):
        need_ctx = i < DEPTH - 1
        j = i // N_MIXERS
        sh1, sc1, g1, sh2, sc2, g2 = adaln(c[:, None, :], ada_w[i], ada_b[i])
        csh1, csc1, cg1, csh2, csc2, cg2 = adaln(c_ctx, ada_w[i], ada_b[i])
        u = modulate(h, sh1, sc1)
        uc = modulate(hc, csh1, csc1)
        if i % N_MIXERS == 0:
            cp = (conv_w_pw1[j], conv_b_pw1[j], conv_w_dw[j], conv_b_dw[j], conv_ln_g[j], conv_ln_b[j],
                  conv_w_pw2[j], conv_b_pw2[j])
            y = conformer_conv(u, rows, *cp)
            yc = conformer_conv(uc, None, *cp) if need_ctx else None
        else:
            y, yc = mlstm_mixer(u, uc, need_ctx, ml_w_up[j], ml_w_conv[j], ml_b_conv[j], ml_w_q[j], ml_w_k[j],
                                ml_w_v[j], ml_w_gate[j], ml_b_gate[j], ml_w_o[j], ml_b_o[j], ml_gn_g[j],
                                ml_skip[j], ml_w_down[j])
        h = layer_norm(ALPHA * h + g1 * y, ln1_g[i], ln1_b[i])
        h = layer_norm(ALPHA * h + g2 * sq_relu_mlp(modulate(h, sh2, sc2), mlp_w1[i], mlp_w2[i]), ln2_g[i], ln2_b[i])
        if need_ctx:
            hc = layer_norm(ALPHA * hc + cg1 * yc, ln1_g[i], ln1_b[i])
            hc = layer_norm(ALPHA * hc + cg2 * sq_relu_mlp(modulate(hc, csh2, csc2), mlp_w1[i], mlp_w2[i]),
                            ln2_g[i], ln2_b[i])
    return h
```

```python
import numpy as np
import ml_dtypes
import concourse.bass as bass
import concourse.mybir as mybir
from concourse.bass_utils import run_bass_kernel_spmd

F32 = mybir.dt.float32
BF16 = mybir.dt.bfloat16
AF = mybir.ActivationFunctionType
ALU = mybir.AluOpType
AX = mybir.AxisListType

D = 1024
E = 2048
NH = 4
DH = 512
DFF = 4096
ALPHA = 4.0 ** 0.25
LN_EPS = 1e-5
TT = 256
NS = 2
KSCALE = DH ** -0.5


class _Op:
    __slots__ = ("eng", "fn", "reads", "writes", "dma", "idx", "waits", "signal", "sig", "bar")

    def __init__(self, eng, fn, reads, writes, dma):
        self.eng = eng
        self.fn = fn
        self.reads = reads
        self.writes = writes
        self.dma = dma
        self.idx = -1
        self.waits = []
        self.signal = False
        self.sig = 0
        self.bar = False


class Sched:
    ENGS = ("pe", "act", "dve", "pool", "sp")

    def __init__(self):
        self.ops = []

    def op(self, eng, fn, reads=(), writes=(), dma=None):
        self.ops.append(_Op(eng, fn, tuple(reads), tuple(writes), dma))

    def barrier(self):
        for e in self.ENGS:
            op = _Op(e, lambda en: en.nop(), (), (), None)
            op.bar = True
            self.ops.append(op)

    def analyze(self):
        last_op = {}
        last_w = {}
        readers = {}
        eng_cnt = {e: 0 for e in self.ENGS}
        dma_cnt = {}
        known = {e: {} for e in self.ENGS}
        eng_ops = {e: [] for e in self.ENGS}
        for op in self.ops:
            deps = {}
            for k in op.reads:
                w = last_w.get(k)
                if w is not None:
                    deps[w] = True
            for k in op.writes:
                w = last_w.get(k)
                if w is not None:
                    deps[w] = True
                for r in readers.get(k, ()):
                    if r not in deps:
                        deps[r] = False
            need = {}
            for d, hard in deps.items():
                if d is op:
                    continue
                if d.dma is not None:
                    st = ("dma", d.dma)
                    tgt = dma_cnt[d.dma]
                else:
                    if d.eng == op.eng and op.dma is None:
                        if op.eng == "pe":
                            continue
                        if not hard:
                            continue
                    st = ("eng", d.eng)
                    tgt = d.idx
                if need.get(st, (-1,))[0] < tgt:
                    need[st] = (tgt, d)
            if op.bar:
                for f, d in last_op.items():
                    if f != op.eng:
                        need[("eng", f)] = (d.idx, d)
                for k, c in dma_cnt.items():
                    need[("dma", k)] = (c, None)
            kn = known[op.eng]
            for st, (tgt, d) in need.items():
                if kn.get(st, -1) >= tgt:
                    continue
                kn[st] = tgt
                if st[0] == "eng":
                    d.signal = True
                op.waits.append((st, tgt, d))
            if op.dma is not None:
                dma_cnt[op.dma] = dma_cnt.get(op.dma, 0) + 1
            else:
                op.idx = eng_cnt[op.eng]
                eng_cnt[op.eng] += 1
                last_op[op.eng] = op
            eng_ops[op.eng].append(op)
            for k in op.reads:
                readers.setdefault(k, []).append(op)
            for k in op.writes:
                last_w[k] = op
                readers[k] = []
        for e in self.ENGS:
            c = 0
            for op in eng_ops[e]:
                if op.dma is None and op.signal:
                    c += 1
                    op.sig = c
        self.eng_ops = eng_ops
        self.dma_cnt = dma_cnt

    def emit(self, nc, block, sems, dma_sems):
        def run(eng_name):
            def body(e):
                for op in self.eng_ops[eng_name]:
                    for st, tgt, d in op.waits:
                        if st[0] == "dma":
                            e.wait_ge(dma_sems[st[1]], 16 * tgt)
                        else:
                            e.wait_ge(sems[st[1]], d.sig)
                    ins = op.fn(e)
                    if op.dma is not None:
                        ins.then_inc(dma_sems[op.dma], 16)
                    elif op.signal:
                        ins.then_inc(sems[op.eng], 1)
                if eng_name == "sp":
                    for k, c in self.dma_cnt.items():
                        e.wait_ge(dma_sems[k], 16 * c)
            return body

        block.tensor(run("pe"))
        block.scalar(run("act"))
        block.vector(run("dve"))
        block.gpsimd(run("pool"))
        block.sync(run("sp"))


STAGES = [
    ("pw1", "conv_w_pw1", 1024, 0, 2048),
    ("pw2", "conv_w_pw2", 1024, 0, 1024),
    ("w1a", "mlp_w1_0", 1024, 0, 4096),
    ("w2a", "mlp_w2_0", 4096, 0, 1024),
    ("upx", "ml_w_up", 1024, 0, 2048),
    ("upz", "ml_w_up", 1024, 2048, 2048),
    ("wq", "ml_w_q", 2048, 0, 2048),
    ("wk", "ml_w_k", 2048, 0, 2048),
    ("wv", "ml_w_v", 2048, 0, 2048),
    ("wo", "ml_w_o", 2048, 0, 4096),
    ("down", "ml_w_down", 2048, 0, 1024),
    ("w1b", "mlp_w1_1", 1024, 0, 4096),
    ("w2b", "mlp_w2_1", 4096, 0, 1024),
]
STAGE = {s[0]: s for s in STAGES}


def stage_units(name):
    _, _, K, _, N = STAGE[name]
    return [(nb, ku) for nb in range(N // 512) for ku in range(K // 1024)]


UNIT_OFF = {}
_o = 0
for _s in STAGES:
    UNIT_OFF[_s[0]] = _o
    _o += len(stage_units(_s[0]))
N_UNITS = _o

COLS = {}
_c = 0
for _n, _w in [("ada_b0", 48), ("ada_b1", 48), ("b_pw1", 16), ("b_dw", 8), ("cln_g", 8), ("cln_b", 8),
               ("w_dw", 8 * 31), ("w_mc", 16 * 5), ("b_mc", 16), ("cT", 16), ("gn_g", 16), ("skip", 16)]:
    COLS[_n] = (_c, _w)
    _c += _w
NCOL = _c

ROWS = {}
_c = 0
for _n, _w in [("ln1_g0", D), ("ln1_b0", D), ("ln2_g0", D), ("ln2_b0", D),
               ("ln1_g1", D), ("ln1_b1", D), ("ln2_g1", D), ("ln2_b1", D),
               ("b_pw2", D), ("b_o", 2 * E), ("gn_g", E), ("skip", E), ("b_gate", 16)]:
    ROWS[_n] = (_c, _w)
    _c += _w
NROW = _c


class Cfg:
    def __init__(self, n_lat_tiles=16, phases=("p0", "p1", "p2", "p3"), debug=()):
        self.n_lat = n_lat_tiles
        self.n_tok = (n_lat_tiles + 1) * TT
        self.phases = phases
        self.debug = debug
        self.stop = 99
        self.p1_tiles = None


def build_program(cfg):
    nc = bass.Bass("TRN2", target_bir_lowering=False)
    S = Sched()
    NL = cfg.n_lat
    NTOK = cfg.n_tok

    def dram(name, shape, dt, kind="Internal"):
        return nc.dram_tensor(name, list(shape), dt, kind=kind).ap()

    xl = dram("xl", [NTOK, D], F32, "ExternalInput")
    cx = dram("cx", [TT, D], F32, "ExternalInput")
    colv = dram("colv", [128, NCOL], F32, "ExternalInput")
    rowv = dram("rowv", [1, NROW], F32, "ExternalInput")
    ada_w = dram("ada_w", [2, D, 6 * D], F32, "ExternalInput")
    wsrc = {}
    for nm, shp in [("conv_w_pw1", [D, 2 * D]), ("conv_w_pw2", [D, D]), ("mlp_w1_0", [D, DFF]),
                    ("mlp_w2_0", [DFF, D]), ("mlp_w1_1", [D, DFF]), ("mlp_w2_1", [DFF, D]),
                    ("ml_w_up", [D, 2 * E]), ("ml_w_q", [E, E]), ("ml_w_k", [E, E]), ("ml_w_v", [E, E]),
                    ("ml_w_o", [E, 2 * E]), ("ml_w_down", [E, D])]:
        wsrc[nm] = dram(nm, shp, F32, "ExternalInput")
    w_gate_in = dram("w_gate", [3 * E, 16], F32, "ExternalInput")

    dbg = set(cfg.debug)

    def scratch(name, shape, dt):
        return dram(name, shape, dt, "ExternalOutput" if name in dbg else "Internal")

    wbf = scratch("wbf", [N_UNITS, 128, 8 * 512], BF16)
    H1 = scratch("H1", [NTOK + TT, D], F32)
    XM = scratch("XM", [E, NTOK + TT], BF16)
    SZT = scratch("SZT", [E, NL * TT], BF16)
    NF = (NL + 1) * TT
    QT = scratch("QT", [E, NF], BF16)
    KT = scratch("KT", [E, NF], BF16)
    XC = scratch("XC", [E, NF], BF16)
    KTOK = scratch("KTOK", [NF, E], BF16)
    VTOK = scratch("VTOK", [NF, E], BF16)
    OAB = scratch("OAB", [NF, 2 * E], BF16)
    GS = scratch("GS", [NF, 32], F32)
    HOA = scratch("HOA", [NL * TT, E], F32)
    YT = scratch("YT", [E, NL * TT], BF16)
    state_in = dram("state_in", [128, 16 * 512 + 16], F32, "ExternalInput")
    state_out = dram("state_out", [128, 16 * 512 + 16], F32, "ExternalOutput")
    out_d = dram("out", [NL * TT, D], F32, "ExternalOutput")

    from contextlib import ExitStack
    es = ExitStack()

    def sb(name, shape, dt):
        return es.enter_context(nc.sbuf_tensor(name, list(shape), dt))

    colt = sb("colt", [128, NCOL], F32)
    ident_f = sb("ident_f", [128, 128], F32)
    ident_b = sb("ident_b", [128, 128], BF16)
    ones_f = sb("ones_f", [128, 128], F32)
    modc = sb("modc", [128, 2, 48, 2], F32)
    wring = sb("wring", [128, 6, 8, 512], BF16)
    NWR = 6
    psum = es.enter_context(nc.psum_tensor("psum", [128, 8, 512], F32))

    def PS(bank):
        return [("ps", bank, 0), ("ps", bank, 1)]

    def dump(name, ap, keys):
        if name not in dbg:
            return
        t = dram("dbg_" + name, list(ap.shape), ap.dtype, "ExternalOutput")
        S.op("sp", lambda e: e.dma_start(out=t, in_=ap), reads=keys, writes=[("dbg", name)], dma=("dbg", name))

    def col(name, i=0, n=1):
        o, w = COLS[name]
        return colt[:, o + i:o + i + n]

    S.op("sp", lambda e: e.dma_start(out=colt[:], in_=colv[:, :]), writes=["colt"], dma="const")
    S.op("pool", lambda e: e.memset(ones_f[:], 1.0), writes=["ones_f"])
    S.op("pool", lambda e: e.memset(ident_f[:], 0.0), writes=["ident_f"])
    S.op("pool", lambda e: e.affine_select(out=ident_f[:], in_=ident_f[:], pattern=[[-1, 128]],
                                           compare_op=ALU.not_equal, fill=1.0, base=0, channel_multiplier=1),
         reads=["ident_f"], writes=["ident_f"])
    S.op("dve", lambda e: e.tensor_copy(out=ident_b[:], in_=ident_f[:]), reads=["ident_f"], writes=["ident_b"])

    wring_uses = [0]

    if "p0" in cfg.phases:
        with nc.sbuf_tensor("cvin", [128, 2, 8, 512], F32) as cvin, \
                nc.sbuf_tensor("cvout", [128, 2, 8, 512], BF16) as cvout:
            n = 0
            for (nm, src, K, c0, N) in STAGES:
                for (nb, ku) in stage_units(nm):
                    u = UNIT_OFF[nm] + nb * (K // 1024) + ku
                    sl = n % 2
                    srcap = wsrc[src][ku * 1024:(ku + 1) * 1024, c0 + nb * 512:c0 + (nb + 1) * 512] \
                        .rearrange("(c p) n -> p c n", p=128)
                    S.op("sp", lambda e, sl=sl, srcap=srcap: e.dma_start(out=cvin[:, sl], in_=srcap),
                         writes=[("cvin", sl)], dma=("cvin", sl))
                    eng = ("dve", "act", "pool")[n % 3]
                    if eng == "act":
                        S.op("act", lambda e, sl=sl: e.activation(out=cvout[:, sl], in_=cvin[:, sl], func=AF.Copy),
                             reads=[("cvin", sl)], writes=[("cvout", sl)])
                    else:
                        S.op(eng, lambda e, sl=sl: e.tensor_copy(out=cvout[:, sl], in_=cvin[:, sl]),
                             reads=[("cvin", sl)], writes=[("cvout", sl)])
                    S.op("sp", lambda e, sl=sl, u=u: e.dma_start(
                        out=wbf[u].rearrange("p (c n) -> p c n", c=8), in_=cvout[:, sl]),
                         reads=[("cvout", sl)], writes=[("wbf", u)], dma=("cvout", sl))
                    n += 1
            S.barrier()

    if "p0" in cfg.phases:
        with nc.sbuf_tensor("adaw", [128, 2, 8, 512], F32) as adaw, \
                nc.sbuf_tensor("scT", [128, 8, 2], F32) as scT:
            cT0 = COLS["cT"][0]
            S.op("act", lambda e: e.activation(out=scT[:].rearrange("p c r -> p (c r)"),
                                               in_=colt[:, cT0:cT0 + 16], func=AF.Silu),
                 reads=["colt"], writes=["scT"])
            n = 0
            for l in range(2):
                for nb in range(12):
                    sl = n % 2
                    srcap = ada_w[l, :, nb * 512:(nb + 1) * 512].rearrange("(c p) n -> p c n", p=128)
                    S.op("sp", lambda e, sl=sl, srcap=srcap: e.dma_start(out=adaw[:, sl], in_=srcap),
                         writes=[("adaw", sl)], dma=("adaw", sl))
                    bank = n % 2
                    for j in range(4):
                        for kc in range(8):
                            S.op("pe", lambda e, sl=sl, j=j, kc=kc, bank=bank: e.matmul(
                                psum[:, bank, j * 2:j * 2 + 2], adaw[:, sl, kc, j * 128:(j + 1) * 128],
                                scT[:, kc, :], start=(kc == 0), stop=(kc == 7)),
                                 reads=[("adaw", sl), "scT"], writes=PS(bank))
                    ab = COLS["ada_b%d" % l][0]
                    S.op("dve", lambda e, l=l, nb=nb, bank=bank, ab=ab: e.tensor_tensor(
                        out=modc[:, l, nb * 4:(nb + 1) * 4, :],
                        in0=psum[:, bank, 0:8].rearrange("p (j r) -> p j r", r=2),
                        in1=colt[:, ab + nb * 4:ab + nb * 4 + 4].unsqueeze(2).to_broadcast([128, 4, 2]),
                        op=ALU.add),
                         reads=PS(bank) + ["colt"], writes=["modc"])
                    n += 1
            for l in range(2):
                for c0 in (8, 32):
                    S.op("dve", lambda e, l=l, c0=c0: e.tensor_scalar_add(
                        out=modc[:, l, c0:c0 + 8, :], in0=modc[:, l, c0:c0 + 8, :], scalar1=1.0),
                         reads=["modc"], writes=["modc"])
            S.barrier()

    def load_unit(stage, nb, ku):
        _, _, K, _, N = STAGE[stage]
        u = UNIT_OFF[stage] + nb * (K // 1024) + ku
        slot = wring_uses[0] % NWR
        wring_uses[0] += 1
        S.op("sp", lambda e, slot=slot, u=u: e.dma_start(
            out=wring[:, slot], in_=wbf[u].rearrange("p (c n) -> p c n", c=8)),
             reads=[("wbf", u)], writes=[("wr", slot)], dma=("wr", slot))
        return slot

    def linear_fm(stage, xT, xkey, nsub, evac, banks=(0, 1, 2, 3), xoff=0):
        _, _, K, _, N = STAGE[stage]
        T = nsub * 128
        nku = K // 1024
        for nb in range(N // 512):
            slots = [load_unit(stage, nb, ku) for ku in range(nku)]
            for j in range(4):
                bank = banks[j]
                pk = PS(bank)
                for ku in range(nku):
                    for kc in range(8):
                        S.op("pe", lambda e, bank=bank, slot=slots[ku], kc=kc, j=j, ku=ku: e.matmul(
                            psum[:, bank, 0:T], wring[:, slot, kc, j * 128:(j + 1) * 128],
                            xT[:, ku * 8 + kc, xoff:xoff + T], start=(ku == 0 and kc == 0), stop=(ku == nku - 1 and kc == 7)),
                             reads=[("wr", slots[ku]), xkey], writes=pk)
                evac(nb * 4 + j, psum[:, bank, 0:T], pk)

    def linear_tm(stage, xT, xkey, nsub, evac, banks=(0, 1, 2, 3), bias=None, xoff=0):
        _, _, K, _, N = STAGE[stage]
        nku = K // 1024
        for nb in range(N // 512):
            pair = (nb % 2) * 2
            slots = [load_unit(stage, nb, ku) for ku in range(nku)]
            for j in range(nsub):
                bank = banks[pair + j]
                pk = PS(bank)
                first = True
                if bias is not None:
                    bt, bkey, boff = bias
                    S.op("pe", lambda e, bank=bank, nb=nb, bt=bt, boff=boff: e.matmul(
                        psum[:, bank, :], ones_hl[:, :], bt[:, boff + nb * 512:boff + (nb + 1) * 512],
                        start=True, stop=False), reads=[bkey, "ones_hl"], writes=pk)
                    first = False
                for ku in range(nku):
                    for kc in range(8):
                        S.op("pe", lambda e, bank=bank, slot=slots[ku], kc=kc, j=j, ku=ku, first=first: e.matmul(
                            psum[:, bank, :], xT[:, ku * 8 + kc, xoff + j * 128:xoff + (j + 1) * 128], wring[:, slot, kc, :],
                            start=(first and ku == 0 and kc == 0), stop=(ku == nku - 1 and kc == 7)),
                             reads=[("wr", slots[ku]), xkey], writes=pk)
                evac(nb, j, psum[:, bank, :], pk)

    ones_hl = sb("ones_hl", [128, 128], BF16)
    S.op("pool", lambda e: e.memset(ones_hl[:], 1.0), writes=["ones_hl"])

    def hilo_rows(name, dst, dkey, width, tmpf, tmpb):
        o, w = ROWS[name]
        assert w == width
        S.op("pool", lambda e: e.memset(dst[:, 0:w], 0.0), writes=[dkey])
        S.op("sp", lambda e: e.dma_start(out=tmpf[0:1, 0:w], in_=rowv[0:1, o:o + w]), writes=["hl_tmpf"], dma="hl")
        S.op("sp", lambda e: e.dma_start(out=tmpf[1:2, 0:w], in_=rowv[0:1, o:o + w]), writes=["hl_tmpf"], dma="hl")
        S.op("dve", lambda e: e.tensor_copy(out=tmpb[0:2, 0:w], in_=tmpf[0:2, 0:w]), reads=["hl_tmpf"], writes=["hl_tmpb"])
        S.op("dve", lambda e: e.tensor_tensor(out=tmpf[0:2, 0:w], in0=tmpf[0:2, 0:w], in1=tmpb[0:2, 0:w], op=ALU.subtract),
             reads=["hl_tmpf", "hl_tmpb"], writes=["hl_tmpf"])
        S.op("dve", lambda e: e.tensor_copy(out=dst[0:2, 0:w], in_=tmpf[0:2, 0:w]), reads=["hl_tmpf"], writes=[dkey])
        S.op("dve", lambda e: e.tensor_copy(out=dst[0:1, 0:w], in_=tmpb[0:1, 0:w]), reads=["hl_tmpb", dkey], writes=[dkey])

    def bcast_row(name, dst, dkey):
        o, w = ROWS[name]
        S.op("sp", lambda e: e.dma_start(out=dst[:, 0:w], in_=rowv[0:1, o:o + w].partition_broadcast(128)),
             writes=[dkey], dma="const")

    def gate_bcast(l, c0, which, dst, dkey, gtmp):
        for c in range(8):
            S.op("dve", lambda e, c=c: e.tensor_scalar_mul(out=gtmp[:], in0=ones_f[:], scalar1=modc[:, l, c0 + c, which:which + 1]),
                 reads=["ones_f", "modc"], writes=["gtmp"])
            S.op("pe", lambda e, c=c: e.matmul(psum[:, 7, c % 4 * 128:(c % 4 + 1) * 128], gtmp[:], ident_f[:], start=True, stop=True),
                 reads=["gtmp", "ident_f"], writes=PS(7))
            S.op("dve", lambda e, c=c: e.tensor_copy(out=dst[:, c * 128:(c + 1) * 128], in_=psum[:, 7, c % 4 * 128:(c % 4 + 1) * 128]),
                 reads=PS(7), writes=[dkey])

    def modulate_T(h, hkey, l, sc_c0, sh_c0, which, uT, ukey, nsub):
        for c in range(8):
            bank = 4 + (c % 2)
            for j in range(nsub):
                S.op("pe", lambda e, c=c, j=j, bank=bank: e.transpose(
                    psum[:, bank, j * 128:(j + 1) * 128], h[:, j, c * 128:(c + 1) * 128], ident_f[:]),
                     reads=[hkey, "ident_f"], writes=PS(bank))
            eng = "act" if c % 2 == 0 else "dve"
            if eng == "act":
                S.op("act", lambda e, c=c, bank=bank: e.activation(
                    out=uT[:, c, 0:nsub * 128], in_=psum[:, bank, 0:nsub * 128], func=AF.Identity,
                    scale=modc[:, l, sc_c0 + c, which:which + 1], bias=modc[:, l, sh_c0 + c, which:which + 1]),
                     reads=PS(bank) + ["modc"], writes=[ukey])
            else:
                S.op("dve", lambda e, c=c, bank=bank: e.tensor_scalar(
                    out=uT[:, c, 0:nsub * 128], in0=psum[:, bank, 0:nsub * 128],
                    scalar1=modc[:, l, sc_c0 + c, which:which + 1], scalar2=modc[:, l, sh_c0 + c, which:which + 1],
                    op0=ALU.mult, op1=ALU.add),
                     reads=PS(bank) + ["modc"], writes=[ukey])

    def resid_ln(h, hkey, j, yb, ybkey, g_bc, gkey, lng, lngkey, lnb, lnbkey, stat, tmp):
        S.op("dve", lambda e: e.tensor_tensor(out=yb, in0=yb, in1=g_bc[:, :], op=ALU.mult),
             reads=[ybkey, gkey], writes=[ybkey])
        S.op("dve", lambda e: e.scalar_tensor_tensor(out=yb, in0=h[:, j, :], scalar=float(ALPHA), in1=yb,
                                                     op0=ALU.mult, op1=ALU.add),
             reads=[ybkey, hkey], writes=[ybkey])
        for q in range(2):
            S.op("dve", lambda e, q=q: e.bn_stats(out=stat[:, q * 6:(q + 1) * 6], in_=yb[:, q * 512:(q + 1) * 512]),
                 reads=[ybkey], writes=["ln_stat"])
        S.op("dve", lambda e: e.bn_aggr(out=stat[:, 12:14], in_=stat[:, 0:12]), reads=["ln_stat"], writes=["ln_mv"])
        S.op("act", lambda e: e.activation(out=stat[:, 14:15], in_=stat[:, 13:14], func=AF.Sqrt, bias=eps_t[:, 0:1], scale=1.0),
             reads=["ln_mv", "eps_t"], writes=["ln_sd"])
        S.op("dve", lambda e: e.reciprocal(out=stat[:, 15:16], in_=stat[:, 14:15]), reads=["ln_sd"], writes=["ln_rs"])
        S.op("dve", lambda e: e.tensor_scalar(out=yb, in0=yb, scalar1=stat[:, 12:13], scalar2=stat[:, 15:16],
                                              op0=ALU.subtract, op1=ALU.mult),
             reads=[ybkey, "ln_mv", "ln_rs"], writes=[ybkey])
        S.op("pool", lambda e: e.tensor_tensor(out=yb, in0=yb, in1=lng[:, :], op=ALU.mult),
             reads=[ybkey, lngkey], writes=[ybkey])
        S.op("pool", lambda e: e.tensor_tensor(out=h[:, j, :], in0=yb, in1=lnb[:, :], op=ALU.add),
             reads=[ybkey, lnbkey], writes=[hkey])

    eps_t = sb("eps_t", [128, 1], F32)
    S.op("pool", lambda e: e.memset(eps_t[:], LN_EPS), writes=["eps_t"])

    if "p1" in cfg.phases:
        p1 = ExitStack()

        def sb1(name, shape, dt):
            return p1.enter_context(nc.sbuf_tensor(name, list(shape), dt))

        bc = {}
        for nm in ("g1l", "g1c", "g2l", "g2c", "ln1g", "ln1b", "ln2g", "ln2b", "bpw2"):
            bc[nm] = sb1("bc_" + nm, [128, D], F32)
        gtmp = sb1("gtmp", [128, 128], F32)
        hbuf = sb1("hbuf", [128, 2, NS, D], F32)
        uT = sb1("uT", [128, 8, TT], BF16)
        sig = sb1("sig", [128, 8, TT], F32)
        glu = sb1("glu", [128, 8, TT], F32)
        acc = sb1("acc", [128, 8, TT], F32)
        sqt = sb1("sqt", [128, 2, TT], F32)
        lnt = sb1("lnt", [128, 4, TT], F32)
        sT = sb1("sT", [128, 8, TT], BF16)
        ybuf = sb1("ybuf", [128, 2, D], F32)
        stat = sb1("stat", [128, 16], F32)
        hidr = sb1("hidr", [128, 2, TT], BF16)
        hidT = sb1("hidT", [128, 32, TT], BF16)
        xst = sb1("xst", [128, 2, 4, TT], BF16)
        zst = sb1("zst", [128, 2, 4, TT], BF16)

        gate_bcast(0, 16, 0, bc["g1l"], "bc_g1l", gtmp)
        gate_bcast(0, 16, 1, bc["g1c"], "bc_g1c", gtmp)
        gate_bcast(0, 40, 0, bc["g2l"], "bc_g2l", gtmp)
        gate_bcast(0, 40, 1, bc["g2c"], "bc_g2c", gtmp)
        bcast_row("ln1_g0", bc["ln1g"], "bc_ln1g")
        bcast_row("ln1_b0", bc["ln1b"], "bc_ln1b")
        bcast_row("ln2_g0", bc["ln2g"], "bc_ln2g")
        bcast_row("ln2_b0", bc["ln2b"], "bc_ln2b")
        bcast_row("b_pw2", bc["bpw2"], "bc_bpw2")

        wdw0 = COLS["w_dw"][0]
        tiles = [("ctx", 0)] + [("lat", i) for i in range(NL + 1)]
        if cfg.p1_tiles is not None:
            tiles = tiles[:cfg.p1_tiles]
        for ti, (kind, idx) in enumerate(tiles):
            which = 1 if kind == "ctx" else 0
            hs = ti % 2
            h = hbuf[:, hs]
            hkey = ("h", hs)
            src = cx[:, :] if kind == "ctx" else xl[idx * TT:(idx + 1) * TT, :]
            S.op("sp", lambda e, hs=hs, src=src: e.dma_start(out=hbuf[:, hs], in_=src.rearrange("(j p) d -> p j d", p=128)),
                 writes=[hkey], dma=("h", hs))
            if cfg.stop <= 0:
                continue
            modulate_T(h, hkey, 0, 8, 0, which, uT, "uT", NS)

            def ev_pw1(ch, ps, pk):
                if ch >= 8:
                    c = ch - 8
                    if getattr(cfg, 'no_ev', None) == "act":
                        return
                    S.op("act", lambda e, c=c, ps=ps: e.activation(out=sig[:, c, :], in_=ps, func=(AF.Identity if getattr(cfg, "nosig", False) else AF.Sigmoid),
                                                                   bias=(0.0 if getattr(cfg, "nobias", False) else col("b_pw1", 8 + c)), scale=1.0),
                         reads=(pk if getattr(cfg, "nobias", False) else pk + ["colt"]), writes=[("sig", c)])
                else:
                    c = ch
                    if getattr(cfg, 'no_ev', None) == "dve":
                        return
                    S.op("dve", lambda e, c=c, ps=ps: e.scalar_tensor_tensor(
                        out=glu[:, c, :], in0=ps, scalar=col("b_pw1", c), in1=sig[:, c, :], op0=ALU.add, op1=ALU.mult),
                         reads=pk + ["colt", ("sig", c)], writes=[("glu", c)])

            if ti == 0:
                dump("modc", modc[:], ["modc"])
                dump("uT1", uT[:], ["uT"])
            if cfg.stop <= 1:
                continue
            _, _, K, _, N = STAGE["pw1"]
            for nb in (2, 0, 3, 1):
                slot = load_unit("pw1", nb, 0)
                for j in range(4):
                    bank = j
                    pk = PS(bank)
                    for kc in range(8):
                        S.op("pe", lambda e, bank=bank, slot=slot, kc=kc, j=j: e.matmul(
                            psum[:, bank, 0:TT], wring[:, slot, kc, j * 128:(j + 1) * 128],
                            uT[:, kc, :], start=(kc == 0), stop=(kc == 7)),
                             reads=[("wr", slot), "uT"], writes=pk)
                    if getattr(cfg, 'no_ev', None) is not True:
                        ev_pw1(nb * 4 + j, psum[:, bank, 0:TT], pk)
            if cfg.stop <= 2:
                continue
            RL = TT if kind == "ctx" else 64
            NR = TT // RL
            for c in range(8):
                gv = glu[:, c, :].rearrange("p (r t) -> p r t", t=RL)
                av = acc[:, c, :].rearrange("p (r t) -> p r t", t=RL)
                eng = "dve"
                S.op(eng, lambda e, c=c, gv=gv, av=av: e.tensor_scalar(
                    out=av, in0=gv, scalar1=colt[:, wdw0 + c * 31 + 15:wdw0 + c * 31 + 16], scalar2=col("b_dw", c),
                    op0=ALU.mult, op1=ALU.add), reads=[("glu", c), "colt"], writes=[("acc", c)])
                for k in range(31):
                    dlt = k - 15
                    if dlt == 0 or abs(dlt) >= RL:
                        continue
                    lo_o = max(0, -dlt)
                    hi_o = RL - max(0, dlt)
                    S.op(eng, lambda e, c=c, gv=gv, av=av, k=k, dlt=dlt, lo_o=lo_o, hi_o=hi_o: e.scalar_tensor_tensor(
                        out=av[:, :, lo_o:hi_o], in0=gv[:, :, lo_o + dlt:hi_o + dlt],
                        scalar=colt[:, wdw0 + c * 31 + k:wdw0 + c * 31 + k + 1], in1=av[:, :, lo_o:hi_o],
                        op0=ALU.mult, op1=ALU.add), reads=[("glu", c), ("acc", c), "colt"], writes=[("acc", c)])
            if ti == 0:
                dump("glu", glu[:], [("glu", c) for c in range(8)])
            if cfg.stop <= 3:
                continue
            for c in range(8):
                S.op("pe", lambda e, c=c: e.matmul(psum[:, 6, 0:TT], ones_f[:], acc[:, c, :], start=(c == 0), stop=(c == 7)),
                     reads=["ones_f", ("acc", c)], writes=PS(6))
            for c in range(8):
                S.op("act", lambda e, c=c: e.activation(out=sqt[:, c % 2, :], in_=acc[:, c, :], func=AF.Square),
                     reads=[("acc", c)], writes=[("sqt", c % 2)])
                S.op("pe", lambda e, c=c: e.matmul(psum[:, 7, 0:TT], ones_f[:], sqt[:, c % 2, :], start=(c == 0), stop=(c == 7)),
                     reads=["ones_f", ("sqt", c % 2)], writes=PS(7))
            S.op("act", lambda e: e.activation(out=lnt[:, 0, :], in_=psum[:, 6, 0:TT], func=AF.Copy, scale=1.0 / D),
                 reads=PS(6), writes=["lnt0"])
            S.op("dve", lambda e: e.tensor_tensor(out=lnt[:, 2, :], in0=lnt[:, 0, :], in1=lnt[:, 0, :], op=ALU.mult),
                 reads=["lnt0"], writes=["lnt2"])
            S.op("dve", lambda e: e.scalar_tensor_tensor(out=lnt[:, 2, :], in0=psum[:, 7, 0:TT], scalar=1.0 / D,
                                                         in1=lnt[:, 2, :], op0=ALU.mult, op1=ALU.subtract),
                 reads=PS(7) + ["lnt2"], writes=["lnt2"])
            S.op("act", lambda e: e.activation(out=lnt[:, 3, :], in_=lnt[:, 2, :], func=AF.Sqrt, bias=eps_t[:, 0:1], scale=1.0),
                 reads=["lnt2", "eps_t"], writes=["lnt3"])
            S.op("dve", lambda e: e.reciprocal(out=lnt[:, 1, :], in_=lnt[:, 3, :]), reads=["lnt3"], writes=["lnt1"])
            for c in range(8):
                eng = "dve" if c % 2 == 0 else "pool"
                S.op(eng, lambda e, c=c: e.tensor_tensor(out=acc[:, c, :], in0=acc[:, c, :], in1=lnt[:, 0, :], op=ALU.subtract),
                     reads=[("acc", c), "lnt0"], writes=[("acc", c)])
                S.op(eng, lambda e, c=c: e.tensor_tensor(out=acc[:, c, :], in0=acc[:, c, :], in1=lnt[:, 1, :], op=ALU.mult),
                     reads=[("acc", c), "lnt1"], writes=[("acc", c)])
                S.op("act", lambda e, c=c: e.activation(out=sT[:, c, :], in_=acc[:, c, :], func=AF.Silu,
                                                        scale=col("cln_g", c), bias=col("cln_b", c)),
                     reads=[("acc", c), "colt"], writes=["sT"])
            if ti == 0:
                dump("acc", acc[:], [("acc", c) for c in range(8)])
            if cfg.stop <= 4:
                continue
            g1 = ("g1c" if which else "g1l")
            g2 = ("g2c" if which else "g2l")

            def ev_tm(gname, lg, lb, bias_bc=None):
                def ev(nb, j, ps, pk):
                    if bias_bc is None:
                        S.op("act", lambda e, j=j, nb=nb, ps=ps: e.activation(out=ybuf[:, j, nb * 512:(nb + 1) * 512], in_=ps, func=AF.Copy),
                             reads=pk, writes=[("yb", j)])
                    else:
                        S.op("dve", lambda e, j=j, nb=nb, ps=ps: e.tensor_tensor(out=ybuf[:, j, nb * 512:(nb + 1) * 512], in0=ps,
                                                                                in1=bc[bias_bc][:, nb * 512:(nb + 1) * 512], op=ALU.add),
                             reads=pk + ["bc_" + bias_bc], writes=[("yb", j)])
                    if nb == 1:
                        resid_ln(h, hkey, j, ybuf[:, j, :], ("yb", j), bc[gname], "bc_" + gname,
                                 bc[lg], "bc_" + lg, bc[lb], "bc_" + lb, stat, None)
                return ev

            linear_tm("pw2", sT, "sT", NS, ev_tm(g1, "ln1g", "ln1b", "bpw2"))
            if ti == 0:
                dump("sT", sT[:], ["sT"])
                dump("h1mid", hbuf[:, hs], [hkey])
                dump("stat", stat[:], ["ln_stat", "ln_mv", "ln_sd", "ln_rs"])
                dump("yb", ybuf[:], [("yb", 0), ("yb", 1)])
            if cfg.stop <= 5:
                continue
            modulate_T(h, hkey, 0, 32, 24, which, uT, "uT", NS)

            def ev_w1(ch, ps, pk):
                S.op("act", lambda e, ch=ch, ps=ps: e.activation(out=hidr[:, ch % 2, :], in_=ps, func=AF.Relu),
                     reads=pk, writes=[("hidr", ch % 2)])
                S.op("pool", lambda e, ch=ch: e.tensor_tensor(out=hidT[:, ch, :], in0=hidr[:, ch % 2, :], in1=hidr[:, ch % 2, :], op=ALU.mult),
                     reads=[("hidr", ch % 2)], writes=["hidT"])

            linear_fm("w1a", uT, "uT", NS, ev_w1)
            linear_tm("w2a", hidT, "hidT", NS, ev_tm(g2, "ln2g", "ln2b"))
            if ti == 0:
                dump("hmid", hbuf[:, hs], [hkey])
            if cfg.stop <= 6:
                continue
            row0 = NTOK if kind == "ctx" else idx * TT
            S.op("pool", lambda e, hs=hs, row0=row0: e.dma_start(
                out=H1[row0:row0 + TT, :].rearrange("(j p) d -> p j d", p=128), in_=hbuf[:, hs]),
                 reads=[hkey], writes=[("H1", kind, idx)], dma=("hst", hs))
            if cfg.stop <= 7:
                continue
            modulate_T(h, hkey, 1, 8, 0, which, uT, "uT", NS)
            colbase = (NTOK if kind == "ctx" else idx * TT)

            def ev_upx(ch, ps, pk):
                s = (ch // 4) % 2
                S.op("act" if ch % 2 else "dve",
                     (lambda e, ch=ch, ps=ps, s=s: e.activation(out=xst[:, s, ch % 4, :], in_=ps, func=AF.Copy)) if ch % 2 else
                     (lambda e, ch=ch, ps=ps, s=s: e.tensor_copy(out=xst[:, s, ch % 4, :], in_=ps)),
                     reads=pk, writes=[("xst", s)])
                if ch % 4 == 3:
                    nb = ch // 4
                    S.op("pool", lambda e, s=s, nb=nb, colbase=colbase: e.dma_start(
                        out=XM[nb * 512:(nb + 1) * 512, colbase:colbase + TT].rearrange("(c p) t -> p c t", p=128),
                        in_=xst[:, s]), reads=[("xst", s)], writes=["XM"], dma=("xst", s))

            linear_fm("upx", uT, "uT", NS, ev_upx)
            if kind == "lat" and idx < NL:
                def ev_upz(ch, ps, pk):
                    sl = (ch // 4) % 2
                    S.op("act", lambda e, ch=ch, ps=ps, sl=sl: e.activation(out=zst[:, sl, ch % 4, :], in_=ps, func=AF.Silu),
                         reads=pk, writes=[("zst", sl)])
                    if ch % 4 == 3:
                        nb = ch // 4
                        S.op("pool", lambda e, sl=sl, nb=nb, idx=idx: e.dma_start(
                            out=SZT[nb * 512:(nb + 1) * 512, idx * TT:(idx + 1) * TT].rearrange("(c p) t -> p c t", p=128),
                            in_=zst[:, sl]), reads=[("zst", sl)], writes=[("SZT", idx)], dma=("zst", sl))

                linear_fm("upz", uT, "uT", NS, ev_upz)
        S.barrier()
        p1.close()


    maskA = sb("maskA", [128, 128], F32)
    maskB = sb("maskB", [128, 128], F32)
    ones_b = sb("ones_b", [128, 4], BF16)
    cst = sb("cst", [128, 4], F32)
    S.op("pool", lambda e: e.memset(maskA[:], 1.0), writes=["maskA"])
    S.op("pool", lambda e: e.affine_select(out=maskA[:], in_=maskA[:], pattern=[[1, 128]], compare_op=ALU.is_ge,
                                           fill=0.0, base=0, channel_multiplier=-1), reads=["maskA"], writes=["maskA"])
    S.op("pool", lambda e: e.memset(maskB[:], 1.0), writes=["maskB"])
    S.op("pool", lambda e: e.affine_select(out=maskB[:], in_=maskB[:], pattern=[[-1, 128]], compare_op=ALU.is_ge,
                                           fill=0.0, base=0, channel_multiplier=1), reads=["maskB"], writes=["maskB"])
    S.op("pool", lambda e: e.memset(ones_b[:], 1.0), writes=["ones_b"])
    S.op("pool", lambda e: e.memset(cst[:, 0:1], LN_EPS), writes=["cst"])
    S.op("pool", lambda e: e.memset(cst[:, 1:2], float(np.log(KSCALE))), writes=["cst"])
    S.op("pool", lambda e: e.memset(cst[:, 2:3], 1.0), writes=["cst"])
    S.op("pool", lambda e: e.memset(cst[:, 3:4], 0.0), writes=["cst"])

    ftiles = [("ctx", 0)] + [("lat", i) for i in range(NL)]

    if "p2" in cfg.phases:
        p2 = ExitStack()

        def sb2(name, shape, dt):
            return p2.enter_context(nc.sbuf_tensor(name, list(shape), dt))

        xmT = sb2("xmT", [128, 16, TT + 4], BF16)
        cacc = sb2("cacc", [128, 2, TT], F32)
        xcT = sb2("xcT", [128, 16, TT], BF16)
        qT = sb2("qT", [128, 16, TT], BF16)
        kT = sb2("kT", [128, 16, TT], BF16)
        vT = sb2("vT", [128, 16, TT], BF16)
        wg_f = sb2("wg_f", [128, 48, 16], F32)
        wg = sb2("wg", [128, 48, 16], BF16)
        bg_bc = sb2("bg_bc", [128, 16], F32)
        bo_bc = sb2("bo_bc", [128, 2 * E], F32)
        tst = sb2("tst", [128, 2, E], BF16)
        ost = sb2("ost", [128, 2, 2 * E], BF16)
        otmp = sb2("otmp", [128, 2, 512], F32)
        gsb = sb2("gsb", [128, 16], F32)
        gl1 = sb2("gl1", [128, 8], F32)
        gcum = sb2("gcum", [128, 16], F32)
        gtt = sb2("gtt", [128, 2, 8], F32)
        gso = sb2("gso", [128, 2, 32], F32)

        S.op("sp", lambda e: e.dma_start(out=wg_f[:], in_=w_gate_in.rearrange("(c p) n -> p c n", p=128)), writes=["wg_f"], dma="const2")
        S.op("dve", lambda e: e.tensor_copy(out=wg[:], in_=wg_f[:]), reads=["wg_f"], writes=["wg"])
        bcast_row("b_gate", bg_bc, "bg_bc")
        bcast_row("b_o", bo_bc, "bo_bc")
        wmc0 = COLS["w_mc"][0]

        for f, (kind, idx) in enumerate(ftiles):
            is_ctx = kind == "ctx"
            S.op("pool", lambda e: e.memset(xmT[:, :, 0:2], 0.0), writes=["xmT"])
            S.op("pool", lambda e: e.memset(xmT[:, :, TT + 2:TT + 4], 0.0), writes=["xmT"])
            if is_ctx:
                c0, c1, d0 = NTOK, NTOK + TT, 2
            else:
                c0 = max(0, idx * TT - 2)
                c1 = idx * TT + TT + 2
                d0 = 2 - (idx * TT - c0)
            S.op("sp", lambda e, c0=c0, c1=c1, d0=d0: e.dma_start(
                out=xmT[:, :, d0:d0 + (c1 - c0)], in_=XM[:, c0:c1].rearrange("(c p) t -> p c t", p=128)),
                 reads=["XM"], writes=["xmT"], dma="xmT")
            for c in range(16):
                cs = c % 2
                S.op("dve", lambda e, c=c, cs=cs: e.tensor_scalar(
                    out=cacc[:, cs, :], in0=xmT[:, c, 2:2 + TT], scalar1=colt[:, wmc0 + c * 5 + 2:wmc0 + c * 5 + 3],
                    scalar2=col("b_mc", c), op0=ALU.mult, op1=ALU.add), reads=["xmT", "colt"], writes=[("cacc", cs)])
                for k in (0, 1, 3, 4):
                    S.op("dve", lambda e, c=c, cs=cs, k=k: e.scalar_tensor_tensor(
                        out=cacc[:, cs, :], in0=xmT[:, c, k:k + TT], scalar=colt[:, wmc0 + c * 5 + k:wmc0 + c * 5 + k + 1],
                        in1=cacc[:, cs, :], op0=ALU.mult, op1=ALU.add), reads=["xmT", "colt", ("cacc", cs)], writes=[("cacc", cs)])
                S.op("act", lambda e, c=c, cs=cs: e.activation(out=xcT[:, c, :], in_=cacc[:, cs, :], func=AF.Silu),
                     reads=[("cacc", cs)], writes=["xcT"])
            S.op("pool", lambda e, f=f: e.dma_start(out=XC[:, f * TT:(f + 1) * TT].rearrange("(c p) t -> p c t", p=128), in_=xcT[:]),
                 reads=["xcT"], writes=[("XC", f)], dma="xcst")

            def ev_copy(dst, dkey):
                def ev(ch, ps, pk):
                    if ch % 2:
                        S.op("act", lambda e, ch=ch, ps=ps: e.activation(out=dst[:, ch, :], in_=ps, func=AF.Copy), reads=pk, writes=[dkey])
                    else:
                        S.op("dve", lambda e, ch=ch, ps=ps: e.tensor_copy(out=dst[:, ch, :], in_=ps), reads=pk, writes=[dkey])
                return ev

            linear_fm("wq", xcT, "xcT", NS, ev_copy(qT, "qT"))
            S.op("pool", lambda e, f=f: e.dma_start(out=QT[:, f * TT:(f + 1) * TT].rearrange("(c p) t -> p c t", p=128), in_=qT[:]),
                 reads=["qT"], writes=[("QT", f)], dma="qst")
            linear_fm("wk", xcT, "xcT", NS, ev_copy(kT, "kT"))
            S.op("pool", lambda e, f=f: e.dma_start(out=KT[:, f * TT:(f + 1) * TT].rearrange("(c p) t -> p c t", p=128), in_=kT[:]),
                 reads=["kT"], writes=[("KT", f)], dma="kst")
            linear_fm("wv", xmT, "xmT", NS, ev_copy(vT, "vT"), xoff=2)

            for j in range(NS):
                n = 0
                for (src, skey, base) in ((qT, "qT", 0), (kT, "kT", 16), (vT, "vT", 32)):
                    for c in range(16):
                        S.op("pe", lambda e, src=src, c=c, j=j, base=base, n=n: e.matmul(
                            psum[:, 6, 0:16], src[:, c, j * 128:(j + 1) * 128], wg[:, base + c, :], start=(n == 0), stop=(n == 47)),
                             reads=[skey, "wg"], writes=PS(6))
                        n += 1
                S.op("dve", lambda e: e.tensor_tensor(out=gsb[:], in0=psum[:, 6, 0:16], in1=bg_bc[:], op=ALU.add),
                     reads=PS(6) + ["bg_bc"], writes=["gsb"])
                gv = gsb[:].rearrange("p (d g h) -> p d g h", d=2, g=2)
                S.op("act", lambda e, gv=gv: e.activation(out=gl1[:].rearrange("p (d h) -> p d h", d=2), in_=gv[:, :, 1, :], func=AF.Exp, scale=-1.0),
                     reads=["gsb"], writes=["gl1"])
                S.op("act", lambda e: e.activation(out=gl1[:], in_=gl1[:], func=AF.Ln, bias=cst[:, 2:3], scale=1.0),
                     reads=["gl1", "cst"], writes=["gl1"])
                S.op("pe", lambda e: e.matmul(psum[:, 7, 0:4], maskA[:], gl1[:, 0:4], start=True, stop=True), reads=["maskA", "gl1"], writes=PS(7))
                S.op("pe", lambda e: e.matmul(psum[:, 7, 4:8], maskB[:], gl1[:, 4:8], start=True, stop=True), reads=["maskB", "gl1"], writes=PS(7))
                S.op("pe", lambda e: e.matmul(psum[:, 7, 8:16], ones_f[:], gl1[:, 0:8], start=True, stop=True), reads=["ones_f", "gl1"], writes=PS(7))
                S.op("dve", lambda e: e.tensor_copy(out=gcum[:], in_=psum[:, 7, 0:16]), reads=PS(7), writes=["gcum"])
                ncum = gcum[:, 0:8].rearrange("p (d h) -> p d h", d=2)
                ntot = gcum[:, 8:16].rearrange("p (d h) -> p d h", d=2)
                go = gso[:, j, :].rearrange("p (d k h) -> p d k h", d=2, k=4)
                S.op("act", lambda e, go=go, ncum=ncum: e.activation(out=go[:, :, 0, :], in_=ncum, func=AF.Exp, scale=-1.0),
                     reads=["gcum"], writes=[("gso", j)])
                S.op("dve", lambda e, gv=gv, ncum=ncum: e.tensor_tensor(out=gtt[:, :, 0:4], in0=gv[:, :, 0, :], in1=ncum, op=ALU.add),
                     reads=["gsb", "gcum"], writes=["gtt"])
                S.op("dve", lambda e, ntot=ntot: e.tensor_tensor(out=gtt[:, :, 4:8], in0=gtt[:, :, 0:4], in1=ntot, op=ALU.subtract),
                     reads=["gtt", "gcum"], writes=["gtt"])
                S.op("act", lambda e, go=go: e.activation(out=go[:, :, 1, :], in_=gtt[:, :, 0:4], func=AF.Exp, bias=cst[:, 1:2], scale=1.0),
                     reads=["gtt", "cst"], writes=[("gso", j)])
                S.op("act", lambda e, go=go: e.activation(out=go[:, :, 2, :], in_=gtt[:, :, 4:8], func=AF.Exp, bias=cst[:, 1:2], scale=1.0),
                     reads=["gtt", "cst"], writes=[("gso", j)])
                S.op("act", lambda e, go=go, ntot=ntot: e.activation(out=go[:, :, 3, :], in_=ntot, func=AF.Exp, scale=-1.0),
                     reads=["gcum"], writes=[("gso", j)])
                S.op("pool", lambda e, f=f, j=j: e.dma_start(out=GS[f * TT + j * 128:f * TT + (j + 1) * 128, :], in_=gso[:, j, :]),
                     reads=[("gso", j)], writes=[("GS", f)], dma=("gso", j))

            def ev_tok(DST, dname):
                def ev(nb, j, ps, pk):
                    if nb % 2:
                        S.op("act", lambda e, nb=nb, j=j, ps=ps: e.activation(out=tst[:, j, nb * 512:(nb + 1) * 512], in_=ps, func=AF.Copy),
                             reads=pk, writes=[("tst", j)])
                    else:
                        S.op("dve", lambda e, nb=nb, j=j, ps=ps: e.tensor_copy(out=tst[:, j, nb * 512:(nb + 1) * 512], in_=ps),
                             reads=pk, writes=[("tst", j)])
                    if nb == 3:
                        S.op("pool", lambda e, j=j, f=f: e.dma_start(out=DST[f * TT + j * 128:f * TT + (j + 1) * 128, :], in_=tst[:, j, :]),
                             reads=[("tst", j)], writes=[(dname, f)], dma=("tst", j))
                return ev

            linear_tm("wk", xcT, "xcT", NS, ev_tok(KTOK, "KTOK"))
            linear_tm("wv", xmT, "xmT", NS, ev_tok(VTOK, "VTOK"), xoff=2)
            if not is_ctx:
                def ev_o(nb, j, ps, pk):
                    S.op("dve", lambda e, nb=nb, j=j, ps=ps: e.tensor_tensor(out=otmp[:, j, :], in0=ps, in1=bo_bc[:, nb * 512:(nb + 1) * 512], op=ALU.add),
                         reads=pk + ["bo_bc"], writes=[("otmp", j)])
                    S.op("act", lambda e, nb=nb, j=j: e.activation(out=ost[:, j, nb * 512:(nb + 1) * 512], in_=otmp[:, j, :], func=AF.Sigmoid),
                         reads=[("otmp", j)], writes=[("ost", j)])
                    if nb == 7:
                        S.op("pool", lambda e, j=j, f=f: e.dma_start(out=OAB[f * TT + j * 128:f * TT + (j + 1) * 128, :], in_=ost[:, j, :]),
                             reads=[("ost", j)], writes=[("OAB", f)], dma=("ost", j))

                linear_tm("wo", xmT, "xmT", NS, ev_o, xoff=2)
        S.barrier()
        p2.close()

    if "p3" in cfg.phases:
        p3 = ExitStack()

        def sb3(name, shape, dt):
            return p3.enter_context(nc.sbuf_tensor(name, list(shape), dt))

        Cst = sb3("Cst", [128, 16, 512], F32)
        Cbf = sb3("Cbf", [128, 16, 512], BF16)
        nst = sb3("nst", [128, 16], F32)
        nbf = sb3("nbf", [128, 16], BF16)
        qS = sb3("qS", [128, 16, TT], BF16)
        kS = sb3("kS", [128, 16, TT], BF16)
        ktS = sb3("ktS", [128, 2, E], BF16)
        vtS = sb3("vtS", [128, 2, E], BF16)
        oS = sb3("oS", [128, 2, E], BF16)
        gS = sb3("gS", [128, 2, 32], F32)
        stS = sb3("stS", [128, 128], BF16)
        kgS = sb3("kgS", [128, 512], BF16)
        rS = sb3("rS", [128, 4], F32)
        hob = sb3("hob", [128, E], F32)
        hoa = sb3("hoa", [128, E], F32)
        xcS = sb3("xcS", [128, 16, TT], BF16)
        szS = sb3("szS", [128, 16, TT], BF16)
        yTs = sb3("yTs", [128, 16, TT], BF16)
        gst = sb3("gst", [128, 16], F32)
        ytmp = sb3("ytmp", [128, 2, 128], F32)
        ytm2 = sb3("ytm2", [128, 2, 128], F32)

        def state_init_zero():
            S.op("pool", lambda e: e.memset(Cst[:], 0.0), writes=["Cst"])
            S.op("pool", lambda e: e.memset(Cbf[:], 0.0), writes=["Cbf"])
            S.op("pool", lambda e: e.memset(nst[:], 0.0), writes=["nst"])
            S.op("pool", lambda e: e.memset(nbf[:], 0.0), writes=["nbf"])

        def scan_tile(dirn, f, is_ctx, lat_idx):
            do = dirn * 16
            S.op("sp", lambda e: e.dma_start(out=qS[:], in_=QT[:, f * TT:(f + 1) * TT].rearrange("(c p) t -> p c t", p=128)),
                 reads=[("QT", f)], writes=["qS"], dma="qS")
            S.op("sp", lambda e: e.dma_start(out=kS[:], in_=KT[:, f * TT:(f + 1) * TT].rearrange("(c p) t -> p c t", p=128)),
                 reads=[("KT", f)], writes=["kS"], dma="kS")
            S.op("sp", lambda e: e.dma_start(out=ktS[:], in_=KTOK[f * TT:(f + 1) * TT, :].rearrange("(j p) d -> p j d", p=128)),
                 reads=[("KTOK", f)], writes=["ktS"], dma="ktS")
            S.op("sp", lambda e: e.dma_start(out=vtS[:], in_=VTOK[f * TT:(f + 1) * TT, :].rearrange("(j p) d -> p j d", p=128)),
                 reads=[("VTOK", f)], writes=["vtS"], dma="vtS")
            S.op("sp", lambda e: e.dma_start(out=gS[:], in_=GS[f * TT:(f + 1) * TT, :].rearrange("(j p) d -> p j d", p=128)),
                 reads=[("GS", f)], writes=["gS"], dma="gS")
            if not is_ctx:
                S.op("sp", lambda e: e.dma_start(out=oS[:], in_=OAB[f * TT:(f + 1) * TT, dirn * E:(dirn + 1) * E].rearrange("(j p) d -> p j d", p=128)),
                     reads=[("OAB", f)], writes=["oS"], dma="oS")
                if dirn == 1:
                    S.op("sp", lambda e: e.dma_start(out=xcS[:], in_=XC[:, f * TT:(f + 1) * TT].rearrange("(c p) t -> p c t", p=128)),
                         reads=[("XC", f)], writes=["xcS"], dma="xcS")
                    S.op("sp", lambda e: e.dma_start(out=szS[:], in_=SZT[:, lat_idx * TT:(lat_idx + 1) * TT].rearrange("(c p) t -> p c t", p=128)),
                         reads=[("SZT", lat_idx)], writes=["szS"], dma="szS")
            mask = maskA if dirn == 0 else maskB
            mkey = "maskA" if dirn == 0 else "maskB"
            for j in ((0, 1) if dirn == 0 else (1, 0)):
                tok0 = lat_idx * TT + j * 128
                if (not is_ctx) and dirn == 1:
                    S.op("sp", lambda e, tok0=tok0: e.dma_start(out=hoa[:], in_=HOA[tok0:tok0 + 128, :]),
                         reads=[("HOA", lat_idx, j)], writes=["hoa"], dma="hoa")
                for h in range(NH):
                    tsl = slice(j * 128, (j + 1) * 128)
                    hsl = slice(h * 512, (h + 1) * 512)
                    gcol = lambda kind, h=h, j=j: gS[:, j, do + kind * 4 + h:do + kind * 4 + h + 1]
                    for dc in range(4):
                        S.op("pe", lambda e, h=h, dc=dc, tsl=tsl: e.matmul(psum[:, 0, 0:128], kS[:, h * 4 + dc, tsl], qS[:, h * 4 + dc, tsl],
                                                                           start=(dc == 0), stop=(dc == 3)),
                             reads=["kS", "qS"], writes=PS(0))
                    S.op("dve", lambda e, gcol=gcol: e.scalar_tensor_tensor(out=stS[:], in0=psum[:, 0, 0:128], scalar=gcol(1), in1=mask[:],
                                                                            op0=ALU.mult, op1=ALU.mult),
                         reads=PS(0) + ["gS", mkey], writes=["stS"])
                    if not is_ctx:
                        S.op("pe", lambda e, j=j, hsl=hsl: e.matmul(psum[:, 1, :], stS[:], vtS[:, j, hsl], start=True, stop=False),
                             reads=["stS", "vtS"], writes=PS(1))
                        for dc in range(4):
                            S.op("pe", lambda e, h=h, dc=dc, tsl=tsl: e.matmul(psum[:, 1, :], qS[:, h * 4 + dc, tsl], Cbf[:, h * 4 + dc, :],
                                                                               start=False, stop=(dc == 3)),
                                 reads=["qS", "Cbf"], writes=PS(1))
                        S.op("pe", lambda e: e.matmul(psum[:, 2, 0:1], stS[:], ones_b[:, 0:1], start=True, stop=False),
                             reads=["stS", "ones_b"], writes=PS(2))
                        for dc in range(4):
                            S.op("pe", lambda e, h=h, dc=dc, tsl=tsl: e.matmul(psum[:, 2, 0:1], qS[:, h * 4 + dc, tsl], nbf[:, h * 4 + dc:h * 4 + dc + 1],
                                                                               start=False, stop=(dc == 3)),
                                 reads=["qS", "nbf"], writes=PS(2))
                        S.op("dve", lambda e, gcol=gcol: e.tensor_tensor(out=rS[:, 0:1], in0=psum[:, 2, 0:1], in1=gcol(0), op=ALU.mult),
                             reads=PS(2) + ["gS"], writes=["rS0"])
                        S.op("act", lambda e: e.activation(out=rS[:, 1:2], in_=rS[:, 0:1], func=AF.Abs),
                             reads=["rS0"], writes=["rS1"])
                        S.op("dve", lambda e: e.tensor_scalar_max(out=rS[:, 1:2], in0=rS[:, 1:2], scalar1=1.0),
                             reads=["rS1"], writes=["rS1"])
                        S.op("dve", lambda e: e.reciprocal(out=rS[:, 2:3], in_=rS[:, 1:2]), reads=["rS1"], writes=["rS2"])
                        S.op("dve", lambda e, gcol=gcol: e.tensor_tensor(out=rS[:, 3:4], in0=rS[:, 2:3], in1=gcol(0), op=ALU.mult),
                             reads=["rS2", "gS"], writes=["rS3"])
                        S.op("dve", lambda e, j=j, hsl=hsl: e.scalar_tensor_tensor(out=hob[:, hsl], in0=psum[:, 1, :], scalar=rS[:, 3:4],
                                                                                   in1=oS[:, j, hsl], op0=ALU.mult, op1=ALU.mult),
                             reads=PS(1) + ["rS3", "oS"], writes=[("hob", h)])
                        if dirn == 1:
                            S.op("pool", lambda e, hsl=hsl: e.tensor_tensor(out=hob[:, hsl], in0=hob[:, hsl], in1=hoa[:, hsl], op=ALU.add),
                                 reads=[("hob", h), "hoa"], writes=[("hob", h)])
                    S.op("pool", lambda e, j=j, hsl=hsl, gcol=gcol: e.tensor_scalar_mul(out=kgS[:], in0=ktS[:, j, hsl], scalar1=gcol(2)),
                         reads=["ktS", "gS"], writes=["kgS"])
                    for dc in range(4):
                        S.op("pe", lambda e, j=j, hsl=hsl, dc=dc: e.matmul(psum[:, 3 + dc, :], kgS[:, dc * 128:(dc + 1) * 128], vtS[:, j, hsl],
                                                                           start=True, stop=True),
                             reads=["kgS", "vtS"], writes=PS(3 + dc))
                        S.op("dve", lambda e, h=h, dc=dc, gcol=gcol: e.scalar_tensor_tensor(
                            out=Cst[:, h * 4 + dc, :], in0=Cst[:, h * 4 + dc, :], scalar=gcol(3), in1=psum[:, 3 + dc, :],
                            op0=ALU.mult, op1=ALU.add), reads=PS(3 + dc) + ["gS", "Cst"], writes=["Cst"])
                        S.op("act", lambda e, h=h, dc=dc: e.activation(out=Cbf[:, h * 4 + dc, :], in_=Cst[:, h * 4 + dc, :], func=AF.Copy),
                             reads=["Cst"], writes=["Cbf"])
                    for dc in range(4):
                        S.op("pe", lambda e, dc=dc: e.matmul(psum[:, 7, dc:dc + 1], kgS[:, dc * 128:(dc + 1) * 128], ones_b[:, 0:1],
                                                             start=True, stop=True),
                             reads=["kgS", "ones_b"], writes=PS(7))
                    S.op("dve", lambda e, h=h, gcol=gcol: e.scalar_tensor_tensor(
                        out=nst[:, h * 4:h * 4 + 4], in0=nst[:, h * 4:h * 4 + 4], scalar=gcol(3), in1=psum[:, 7, 0:4],
                        op0=ALU.mult, op1=ALU.add), reads=PS(7) + ["gS", "nst"], writes=["nst"])
                    S.op("dve", lambda e, h=h: e.tensor_copy(out=nbf[:, h * 4:h * 4 + 4], in_=nst[:, h * 4:h * 4 + 4]),
                         reads=["nst"], writes=["nbf"])
                if is_ctx:
                    continue
                if dirn == 0:
                    S.op("pool", lambda e, tok0=tok0: e.dma_start(out=HOA[tok0:tok0 + 128, :], in_=hob[:]),
                         reads=[("hob", h) for h in range(NH)], writes=[("HOA", lat_idx, j)], dma="hob")
                else:
                    for h in range(NH):
                        hsl = slice(h * 512, (h + 1) * 512)
                        S.op("dve", lambda e, hsl=hsl: e.bn_stats(out=gst[:, 0:6], in_=hob[:, hsl]), reads=[("hob", h)], writes=["gst0"])
                        S.op("dve", lambda e: e.bn_aggr(out=gst[:, 6:8], in_=gst[:, 0:6]), reads=["gst0"], writes=["gst1"])
                        S.op("act", lambda e: e.activation(out=gst[:, 8:9], in_=gst[:, 7:8], func=AF.Sqrt, bias=cst[:, 0:1], scale=1.0),
                             reads=["gst1", "cst"], writes=["gst2"])
                        S.op("dve", lambda e: e.reciprocal(out=gst[:, 9:10], in_=gst[:, 8:9]), reads=["gst2"], writes=["gst3"])
                        S.op("dve", lambda e, hsl=hsl: e.tensor_scalar(out=hob[:, hsl], in0=hob[:, hsl], scalar1=gst[:, 6:7], scalar2=gst[:, 9:10],
                                                                       op0=ALU.subtract, op1=ALU.mult),
                             reads=[("hob", h), "gst1", "gst3"], writes=[("hob", h)])
                    for c in range(16):
                        pb = c % 4
                        S.op("pe", lambda e, c=c, pb=pb: e.transpose(psum[:, 7, pb * 128:(pb + 1) * 128], hob[:, c * 128:(c + 1) * 128], ident_f[:]),
                             reads=[("hob", c // 4), "ident_f"], writes=PS(7))
                        ys = c % 2
                        S.op("pool", lambda e, c=c, ys=ys, tsl=tsl: e.tensor_scalar_mul(out=ytmp[:, ys, :], in0=xcS[:, c, tsl], scalar1=col("skip", c)),
                             reads=["xcS", "colt"], writes=[("ytmp", ys)])
                        S.op("dve", lambda e, c=c, ys=ys, pb=pb: e.scalar_tensor_tensor(
                            out=ytm2[:, ys, :], in0=psum[:, 7, pb * 128:(pb + 1) * 128], scalar=col("gn_g", c), in1=ytmp[:, ys, :],
                            op0=ALU.mult, op1=ALU.add), reads=PS(7) + ["colt", ("ytmp", ys)], writes=[("ytm2", ys)])
                        S.op("pool", lambda e, c=c, ys=ys, tsl=tsl: e.tensor_tensor(out=yTs[:, c, tsl], in0=ytm2[:, ys, :], in1=szS[:, c, tsl], op=ALU.mult),
                             reads=[("ytm2", ys), "szS"], writes=["yTs"])
            if (not is_ctx) and dirn == 1:
                S.op("pool", lambda e: e.dma_start(out=YT[:, lat_idx * TT:(lat_idx + 1) * TT].rearrange("(c p) t -> p c t", p=128), in_=yTs[:]),
                     reads=["yTs"], writes=[("YT", lat_idx)], dma="yst")

        state_init_zero()
        for f, (kind, idx) in enumerate(ftiles):
            scan_tile(0, f, kind == "ctx", idx)
        S.op("pool", lambda e: e.dma_start(out=state_out[:, 0:16 * 512], in_=Cst[:].rearrange("p a b -> p (a b)")),
             reads=["Cst"], writes=["state_out"], dma="sto")
        S.op("pool", lambda e: e.dma_start(out=state_out[:, 16 * 512:16 * 512 + 16], in_=nst[:]),
             reads=["nst"], writes=["state_out"], dma="sto")
        S.op("sp", lambda e: e.dma_start(out=Cst[:].rearrange("p a b -> p (a b)"), in_=state_in[:, 0:16 * 512]),
             reads=["state_out"], writes=["Cst"], dma="sti")
        S.op("sp", lambda e: e.dma_start(out=nst[:], in_=state_in[:, 16 * 512:16 * 512 + 16]),
             reads=["state_out"], writes=["nst"], dma="sti")
        S.op("act", lambda e: e.activation(out=Cbf[:], in_=Cst[:], func=AF.Copy), reads=["Cst"], writes=["Cbf"])
        S.op("dve", lambda e: e.tensor_copy(out=nbf[:], in_=nst[:]), reads=["nst"], writes=["nbf"])
        for f in range(len(ftiles) - 1, 0, -1):
            scan_tile(1, f, False, ftiles[f][1])
        S.barrier()
        p3.close()

        p4 = ExitStack()

        def sb4(name, shape, dt):
            return p4.enter_context(nc.sbuf_tensor(name, list(shape), dt))

        bd = {}
        for nm in ("g1", "g2", "ln1g", "ln1b", "ln2g", "ln2b"):
            bd[nm] = sb4("bd_" + nm, [128, D], F32)
        gtmp4v = sb4("gtmp4", [128, 128], F32)
        hbuf4v = sb4("hbuf4", [128, 2, NS, D], F32)
        yin = sb4("yin", [128, 2, 16, TT], BF16)
        uT4v = sb4("uT4", [128, 8, TT], BF16)
        ybuf4v = sb4("ybuf4", [128, 2, D], F32)
        stat4v = sb4("stat4", [128, 16], F32)
        hidr4v = sb4("hidr4", [128, 2, TT], BF16)
        hidT4v = sb4("hidT4", [128, 32, TT], BF16)
        gate_bcast(1, 16, 0, bd["g1"], "bd_g1", gtmp4v)
        gate_bcast(1, 40, 0, bd["g2"], "bd_g2", gtmp4v)
        bcast_row("ln1_g1", bd["ln1g"], "bd_ln1g")
        bcast_row("ln1_b1", bd["ln1b"], "bd_ln1b")
        bcast_row("ln2_g1", bd["ln2g"], "bd_ln2g")
        bcast_row("ln2_b1", bd["ln2b"], "bd_ln2b")
        for i in range(NL):
            hs = i % 2
            h = hbuf4v[:, hs]
            hkey = ("h4", hs)
            S.op("sp", lambda e, hs=hs, i=i: e.dma_start(out=hbuf4v[:, hs], in_=H1[i * TT:(i + 1) * TT, :].rearrange("(j p) d -> p j d", p=128)),
                 reads=[("H1", "lat", i)], writes=[hkey], dma=("h4", hs))
            S.op("sp", lambda e, hs=hs, i=i: e.dma_start(out=yin[:, hs], in_=YT[:, i * TT:(i + 1) * TT].rearrange("(c p) t -> p c t", p=128)),
                 reads=[("YT", i)], writes=[("yin", hs)], dma=("yin", hs))

            def ev_tm4(gname, lg, lb):
                def ev(nb, j, ps, pk):
                    S.op("act", lambda e, j=j, nb=nb, ps=ps: e.activation(out=ybuf4v[:, j, nb * 512:(nb + 1) * 512], in_=ps, func=AF.Copy),
                         reads=pk, writes=[("yb4", j)])
                    if nb == 1:
                        resid_ln(h, hkey, j, ybuf4v[:, j, :], ("yb4", j), bd[gname], "bd_" + gname,
                                 bd[lg], "bd_" + lg, bd[lb], "bd_" + lb, stat4v, None)
                return ev

            linear_tm("down", yin[:, hs], ("yin", hs), NS, ev_tm4("g1", "ln1g", "ln1b"))
            modulate_T(h, hkey, 1, 32, 24, 0, uT4v, "uT4", NS)

            def ev_w14(ch, ps, pk):
                S.op("act", lambda e, ch=ch, ps=ps: e.activation(out=hidr4v[:, ch % 2, :], in_=ps, func=AF.Relu),
                     reads=pk, writes=[("hidr4", ch % 2)])
                S.op("pool", lambda e, ch=ch: e.tensor_tensor(out=hidT4v[:, ch, :], in0=hidr4v[:, ch % 2, :], in1=hidr4v[:, ch % 2, :], op=ALU.mult),
                     reads=[("hidr4", ch % 2)], writes=["hidT4"])

            linear_fm("w1b", uT4v, "uT4", NS, ev_w14)
            linear_tm("w2b", hidT4v, "hidT4", NS, ev_tm4("g2", "ln2g", "ln2b"))
            S.op("pool", lambda e, hs=hs, i=i: e.dma_start(out=out_d[i * TT:(i + 1) * TT, :].rearrange("(j p) d -> p j d", p=128), in_=hbuf4v[:, hs]),
                 reads=[hkey], writes=[("out", i)], dma=("ost4", hs))
        p4.close()

    S.analyze()
    dma_keys = list(S.dma_cnt.keys())
    with ExitStack() as ss:
        sems = {e: ss.enter_context(nc.semaphore("sem_" + e)) for e in Sched.ENGS}
        dma_sems = {k: ss.enter_context(nc.semaphore("dsem%d" % i)) for i, k in enumerate(dma_keys)}
        with nc.Block() as block:
            S.emit(nc, block, sems, dma_sems)
    es.close()
    return nc


def _colmajor(v, nchunk):
    return np.ascontiguousarray(v.reshape(nchunk, 128).T)


def make_core_inputs(inp, b, s, cfg):
    NTOK = cfg.n_tok
    own = cfg.n_lat * TT
    seq = inp["x"].shape[1]
    half = seq // 2
    x = inp["x"][b]
    ctx = inp["ctx"][b]
    if s == 0:
        xl = x[0:half + TT][:NTOK] if NTOK <= half + TT else None
        xl = x[0:NTOK]
        cxx = ctx
    else:
        xr = x[::-1]
        xl = xr[0:NTOK]
        cxx = ctx[::-1]
    flip = (s == 1)
    w_dw = inp["conv_w_dw"][0]
    w_mc = inp["ml_w_conv"][0]
    if flip:
        w_dw = w_dw[::-1]
        w_mc = w_mc[::-1]
    colv = np.zeros((128, NCOL), np.float32)

    def put(name, arr):
        o, w = COLS[name]
        assert arr.shape == (128, w), (name, arr.shape, w)
        colv[:, o:o + w] = arr

    put("ada_b0", _colmajor(inp["ada_b"][0], 48))
    put("ada_b1", _colmajor(inp["ada_b"][1], 48))
    put("b_pw1", _colmajor(inp["conv_b_pw1"][0], 16))
    put("b_dw", _colmajor(inp["conv_b_dw"][0], 8))
    put("cln_g", _colmajor(inp["conv_ln_g"][0], 8))
    put("cln_b", _colmajor(inp["conv_ln_b"][0], 8))
    put("w_dw", np.ascontiguousarray(w_dw.T.reshape(8, 128, 31).transpose(1, 0, 2)).reshape(128, 8 * 31))
    put("w_mc", np.ascontiguousarray(w_mc.T.reshape(16, 128, 5).transpose(1, 0, 2)).reshape(128, 16 * 5))
    put("b_mc", _colmajor(inp["ml_b_conv"][0], 16))
    cT = np.stack([_colmajor(inp["c"][b], 8), _colmajor(inp["c_ctx"], 8)], axis=-1)
    put("cT", cT.reshape(128, 16))
    put("gn_g", _colmajor(inp["ml_gn_g"][0], 16))
    put("skip", _colmajor(inp["ml_skip"][0], 16))

    rowv = np.zeros((1, NROW), np.float32)

    def putr(name, arr):
        o, w = ROWS[name]
        assert arr.shape == (w,), (name, arr.shape)
        rowv[0, o:o + w] = arr

    for l in range(2):
        putr("ln1_g%d" % l, inp["ln1_g"][l])
        putr("ln1_b%d" % l, inp["ln1_b"][l])
        putr("ln2_g%d" % l, inp["ln2_g"][l])
        putr("ln2_b%d" % l, inp["ln2_b"][l])
    putr("b_pw2", inp["conv_b_pw2"][0])
    w_o = inp["ml_w_o"][0]
    b_o = inp["ml_b_o"][0]
    w_g = inp["ml_w_gate"][0]
    b_g = inp["ml_b_gate"][0]
    if flip:
        w_o = np.concatenate([w_o[:, E:], w_o[:, :E]], axis=1)
        b_o = np.concatenate([b_o[E:], b_o[:E]])
        perm = np.r_[8:16, 0:8]
        w_g = w_g[:, perm]
        b_g = b_g[perm]
    putr("b_o", b_o)
    putr("gn_g", inp["ml_gn_g"][0])
    putr("skip", inp["ml_skip"][0])
    putr("b_gate", b_g)
    m = {
        "xl": np.ascontiguousarray(xl), "cx": np.ascontiguousarray(cxx), "colv": colv, "rowv": rowv,
        "ada_w": inp["ada_w"],
        "conv_w_pw1": inp["conv_w_pw1"][0], "conv_w_pw2": inp["conv_w_pw2"][0],
        "mlp_w1_0": inp["mlp_w1"][0], "mlp_w2_0": inp["mlp_w2"][0],
        "mlp_w1_1": inp["mlp_w1"][1], "mlp_w2_1": inp["mlp_w2"][1],
        "ml_w_up": inp["ml_w_up"][0], "ml_w_q": inp["ml_w_q"][0], "ml_w_k": inp["ml_w_k"][0],
        "ml_w_v": inp["ml_w_v"][0], "ml_w_o": np.ascontiguousarray(w_o), "ml_w_down": inp["ml_w_down"][0],
        "w_gate": np.ascontiguousarray(w_g),
        "state_in": np.zeros((128, 16 * 512 + 16), np.float32),
    }
    return {k: np.ascontiguousarray(v, dtype=np.float32) for k, v in m.items()}


def kernel(**inputs):
    inp = {k: np.asarray(v) for k, v in inputs.items()}
    B, SEQ = inp["x"].shape[0], inp["x"].shape[1]
    half = SEQ // 2
    cfg = Cfg(n_lat_tiles=half // TT)
    nc = build_program(cfg)
    cores = [(b, s) for b in range(B) for s in range(2)]
    ids = list(range(len(cores)))
    in_maps = [make_core_inputs(inp, b, s, cfg) for b, s in cores]
    res1 = run_bass_kernel_spmd(nc, in_maps, core_ids=ids)
    states = [np.array(res1.results[c]["state_out"], dtype=np.float32) for c in ids]
    for c in ids:
        in_maps[c]["state_in"] = states[c ^ 1]
    res2 = run_bass_kernel_spmd(nc, in_maps, core_ids=ids)
    out = np.zeros((B, SEQ, D), np.float32)
    for c, (b, s) in enumerate(cores):
        o = np.asarray(res2.results[c]["out"])
        if s == 0:
            out[b, :half] = o
        else:
            out[b, half:] = o[::-1]
    return out
```

```python
import numpy as np
import ml_dtypes
import concourse.bass as bass
import concourse.mybir as mybir
from concourse.bass_utils import run_bass_kernel_spmd

F32 = mybir.dt.float32
BF16 = mybir.dt.bfloat16
AF = mybir.ActivationFunctionType
ALU = mybir.AluOpType
AX = mybir.AxisListType

D = 1024
E = 2048
NH = 4
DH = 512
DFF = 4096
ALPHA = 4.0 ** 0.25
LN_EPS = 1e-5
TT = 256
NS = 2
KSCALE = DH ** -0.5


class _Op:
    __slots__ = ("eng", "fn", "reads", "writes", "dma", "idx", "waits", "signal", "sig", "bar")

    def __init__(self, eng, fn, reads, writes, dma):
        self.eng = eng
        self.fn = fn
        self.reads = reads
        self.writes = writes
        self.dma = dma
        self.idx = -1
        self.waits = []
        self.signal = False
        self.sig = 0
        self.bar = False


class Sched:
    ENGS = ("pe", "act", "dve", "pool", "sp")

    def __init__(self):
        self.ops = []

    def op(self, eng, fn, reads=(), writes=(), dma=None):
        self.ops.append(_Op(eng, fn, tuple(reads), tuple(writes), dma))

    def barrier(self):
        for e in self.ENGS:
            op = _Op(e, lambda en: en.nop(), (), (), None)
            op.bar = True
            self.ops.append(op)

    def analyze(self):
        last_op = {}
        last_w = {}
        readers = {}
        eng_cnt = {e: 0 for e in self.ENGS}
        dma_cnt = {}
        known = {e: {} for e in self.ENGS}
        eng_ops = {e: [] for e in self.ENGS}
        for op in self.ops:
            deps = {}
            for k in op.reads:
                w = last_w.get(k)
                if w is not None:
                    deps[w] = True
            for k in op.writes:
                w = last_w.get(k)
                if w is not None:
                    deps[w] = True
                for r in readers.get(k, ()):
                    if r not in deps:
                        deps[r] = False
            need = {}
            for d, hard in deps.items():
                if d is op:
                    continue
                if d.dma is not None:
                    st = ("dma", d.dma)
                    tgt = dma_cnt[d.dma]
                else:
                    if d.eng == op.eng and op.dma is None:
                        if op.eng == "pe":
                            continue
                        if not hard:
                            continue
                    st = ("eng", d.eng)
                    tgt = d.idx
                if need.get(st, (-1,))[0] < tgt:
                    need[st] = (tgt, d)
            if op.bar:
                for f, d in last_op.items():
                    if f != op.eng:
                        need[("eng", f)] = (d.idx, d)
                for k, c in dma_cnt.items():
                    need[("dma", k)] = (c, None)
            kn = known[op.eng]
            for st, (tgt, d) in need.items():
                if kn.get(st, -1) >= tgt:
                    continue
                kn[st] = tgt
                if st[0] == "eng":
                    d.signal = True
                op.waits.append((st, tgt, d))
            if op.dma is not None:
                dma_cnt[op.dma] = dma_cnt.get(op.dma, 0) + 1
            else:
                op.idx = eng_cnt[op.eng]
                eng_cnt[op.eng] += 1
                last_op[op.eng] = op
            eng_ops[op.eng].append(op)
            for k in op.reads:
                readers.setdefault(k, []).append(op)
            for k in op.writes:
                last_w[k] = op
                readers[k] = []
        for e in self.ENGS:
            c = 0
            for op in eng_ops[e]:
                if op.dma is None and op.signal:
                    c += 1
                    op.sig = c
        self.eng_ops = eng_ops
        self.dma_cnt = dma_cnt

    def emit(self, nc, block, sems, dma_sems):
        def run(eng_name):
            def body(e):
                for op in self.eng_ops[eng_name]:
                    for st, tgt, d in op.waits:
                        if st[0] == "dma":
                            e.wait_ge(dma_sems[st[1]], 16 * tgt)
                        else:
                            e.wait_ge(sems[st[1]], d.sig)
                    ins = op.fn(e)
                    if op.dma is not None:
                        ins.then_inc(dma_sems[op.dma], 16)
                    elif op.signal:
                        ins.then_inc(sems[op.eng], 1)
                if eng_name == "sp":
                    for k, c in self.dma_cnt.items():
                        e.wait_ge(dma_sems[k], 16 * c)
            return body

        block.tensor(run("pe"))
        block.scalar(run("act"))
        block.vector(run("dve"))
        block.gpsimd(run("pool"))
        block.sync(run("sp"))


STAGES = [
    ("pw1", "conv_w_pw1", 1024, 0, 2048),
    ("pw2", "conv_w_pw2", 1024, 0, 1024),
    ("w1a", "mlp_w1_0", 1024, 0, 4096),
    ("w2a", "mlp_w2_0", 4096, 0, 1024),
    ("upx", "ml_w_up", 1024, 0, 2048),
    ("upz", "ml_w_up", 1024, 2048, 2048),
    ("wq", "ml_w_q", 2048, 0, 2048),
    ("wk", "ml_w_k", 2048, 0, 2048),
    ("wv", "ml_w_v", 2048, 0, 2048),
    ("wo", "ml_w_o", 2048, 0, 4096),
    ("down", "ml_w_down", 2048, 0, 1024),
    ("w1b", "mlp_w1_1", 1024, 0, 4096),
    ("w2b", "mlp_w2_1", 4096, 0, 1024),
]
STAGE = {s[0]: s for s in STAGES}


def stage_units(name):
    _, _, K, _, N = STAGE[name]
    return [(nb, ku) for nb in range(N // 512) for ku in range(K // 1024)]


UNIT_OFF = {}
_o = 0
for _s in STAGES:
    UNIT_OFF[_s[0]] = _o
    _o += len(stage_units(_s[0]))
N_UNITS = _o

COLS = {}
_c = 0
for _n, _w in [("ada_b0", 48), ("ada_b1", 48), ("b_pw1", 16), ("b_dw", 8), ("cln_g", 8), ("cln_b", 8),
               ("w_dw", 8 * 31), ("w_mc", 16 * 5), ("b_mc", 16), ("cT", 16), ("gn_g", 16), ("skip", 16)]:
    COLS[_n] = (_c, _w)
    _c += _w
NCOL = _c

ROWS = {}
_c = 0
for _n, _w in [("ln1_g0", D), ("ln1_b0", D), ("ln2_g0", D), ("ln2_b0", D),
               ("ln1_g1", D), ("ln1_b1", D), ("ln2_g1", D), ("ln2_b1", D),
               ("b_pw2", D), ("b_o", 2 * E), ("gn_g", E), ("skip", E), ("b_gate", 16)]:
    ROWS[_n] = (_c, _w)
    _c += _w
NROW = _c


class Cfg:
    def __init__(self, n_lat_tiles=16, phases=("p0", "p1", "p2", "p3"), debug=()):
        self.n_lat = n_lat_tiles
        self.n_all = 2 * n_lat_tiles
        self.n_tok = self.n_all * TT
        self.phases = phases
        self.debug = debug
        self.stop = 99
        self.p1_tiles = None


def build_program(cfg):
    nc = bass.Bass("TRN2", target_bir_lowering=False)
    S = Sched()
    NL = cfg.n_lat
    NA = cfg.n_all
    NTOK = cfg.n_tok

    def dram(name, shape, dt, kind="Internal"):
        return nc.dram_tensor(name, list(shape), dt, kind=kind).ap()

    xl = dram("xl", [NTOK, D], F32, "ExternalInput")
    cx = dram("cx", [TT, D], F32, "ExternalInput")
    colv = dram("colv", [128, NCOL], F32, "ExternalInput")
    rowv = dram("rowv", [1, NROW], F32, "ExternalInput")
    ada_w = dram("ada_w", [2, D, 6 * D], F32, "ExternalInput")
    wsrc = {}
    for nm, shp in [("conv_w_pw1", [D, 2 * D]), ("conv_w_pw2", [D, D]), ("mlp_w1_0", [D, DFF]),
                    ("mlp_w2_0", [DFF, D]), ("mlp_w1_1", [D, DFF]), ("mlp_w2_1", [DFF, D]),
                    ("ml_w_up", [D, 2 * E]), ("ml_w_q", [E, E]), ("ml_w_k", [E, E]), ("ml_w_v", [E, E]),
                    ("ml_w_o", [E, 2 * E]), ("ml_w_down", [E, D])]:
        wsrc[nm] = dram(nm, shp, F32, "ExternalInput")
    w_gate_in = dram("w_gate", [3 * E, 16], F32, "ExternalInput")

    dbg = set(cfg.debug)

    def scratch(name, shape, dt):
        return dram(name, shape, dt, "ExternalOutput" if name in dbg else "Internal")

    wbf = scratch("wbf", [N_UNITS, 128, 8 * 512], BF16)
    H1 = scratch("H1", [NTOK + TT, D], F32)
    XM = scratch("XM", [E, NTOK + TT], BF16)
    SZT = scratch("SZT", [E, NL * TT], BF16)
    NF = (NA + 1) * TT
    QT = scratch("QT", [E, NF], BF16)
    KT = scratch("KT", [E, NF], BF16)
    XC = scratch("XC", [E, NF], BF16)
    KTOK = scratch("KTOK", [NF, E], BF16)
    VTOK = scratch("VTOK", [NF, E], BF16)
    OAB = scratch("OAB", [NF, 2 * E], BF16)
    GS = scratch("GS", [NF, 32], F32)
    HOA = scratch("HOA", [NL * TT, E], F32)
    YT = scratch("YT", [E, NL * TT], BF16)
    out_d = dram("out", [NL * TT, D], F32, "ExternalOutput")

    from contextlib import ExitStack
    es = ExitStack()

    def sb(name, shape, dt):
        return es.enter_context(nc.sbuf_tensor(name, list(shape), dt))

    colt = sb("colt", [128, NCOL], F32)
    ident_f = sb("ident_f", [128, 128], F32)
    ident_b = sb("ident_b", [128, 128], BF16)
    ones_f = sb("ones_f", [128, 128], F32)
    modc = sb("modc", [128, 2, 48, 2], F32)
    wring = sb("wring", [128, 6, 8, 512], BF16)
    NWR = 6
    psum = es.enter_context(nc.psum_tensor("psum", [128, 8, 512], F32))

    def PS(bank):
        return [("ps", bank, 0), ("ps", bank, 1)]

    def dump(name, ap, keys):
        if name not in dbg:
            return
        t = dram("dbg_" + name, list(ap.shape), ap.dtype, "ExternalOutput")
        S.op("sp", lambda e: e.dma_start(out=t, in_=ap), reads=keys, writes=[("dbg", name)], dma=("dbg", name))

    def col(name, i=0, n=1):
        o, w = COLS[name]
        return colt[:, o + i:o + i + n]

    S.op("sp", lambda e: e.dma_start(out=colt[:], in_=colv[:, :]), writes=["colt"], dma="const")
    S.op("pool", lambda e: e.memset(ones_f[:], 1.0), writes=["ones_f"])
    S.op("pool", lambda e: e.memset(ident_f[:], 0.0), writes=["ident_f"])
    S.op("pool", lambda e: e.affine_select(out=ident_f[:], in_=ident_f[:], pattern=[[-1, 128]],
                                           compare_op=ALU.not_equal, fill=1.0, base=0, channel_multiplier=1),
         reads=["ident_f"], writes=["ident_f"])
    S.op("dve", lambda e: e.tensor_copy(out=ident_b[:], in_=ident_f[:]), reads=["ident_f"], writes=["ident_b"])

    wring_uses = [0]

    if "p0" in cfg.phases:
        with nc.sbuf_tensor("cvin", [128, 2, 8, 512], F32) as cvin, \
                nc.sbuf_tensor("cvout", [128, 2, 8, 512], BF16) as cvout:
            n = 0
            for (nm, src, K, c0, N) in STAGES:
                for (nb, ku) in stage_units(nm):
                    u = UNIT_OFF[nm] + nb * (K // 1024) + ku
                    sl = n % 2
                    srcap = wsrc[src][ku * 1024:(ku + 1) * 1024, c0 + nb * 512:c0 + (nb + 1) * 512] \
                        .rearrange("(c p) n -> p c n", p=128)
                    S.op("sp", lambda e, sl=sl, srcap=srcap: e.dma_start(out=cvin[:, sl], in_=srcap),
                         writes=[("cvin", sl)], dma=("cvin", sl))
                    eng = ("dve", "act", "pool")[n % 3]
                    if eng == "act":
                        S.op("act", lambda e, sl=sl: e.activation(out=cvout[:, sl], in_=cvin[:, sl], func=AF.Copy),
                             reads=[("cvin", sl)], writes=[("cvout", sl)])
                    else:
                        S.op(eng, lambda e, sl=sl: e.tensor_copy(out=cvout[:, sl], in_=cvin[:, sl]),
                             reads=[("cvin", sl)], writes=[("cvout", sl)])
                    S.op("sp", lambda e, sl=sl, u=u: e.dma_start(
                        out=wbf[u].rearrange("p (c n) -> p c n", c=8), in_=cvout[:, sl]),
                         reads=[("cvout", sl)], writes=[("wbf", u)], dma=("cvout", sl))
                    n += 1
            S.barrier()

    if "p0" in cfg.phases:
        with nc.sbuf_tensor("adaw", [128, 2, 8, 512], F32) as adaw, \
                nc.sbuf_tensor("scT", [128, 8, 2], F32) as scT:
            cT0 = COLS["cT"][0]
            S.op("act", lambda e: e.activation(out=scT[:].rearrange("p c r -> p (c r)"),
                                               in_=colt[:, cT0:cT0 + 16], func=AF.Silu),
                 reads=["colt"], writes=["scT"])
            n = 0
            for l in range(2):
                for nb in range(12):
                    sl = n % 2
                    srcap = ada_w[l, :, nb * 512:(nb + 1) * 512].rearrange("(c p) n -> p c n", p=128)
                    S.op("sp", lambda e, sl=sl, srcap=srcap: e.dma_start(out=adaw[:, sl], in_=srcap),
                         writes=[("adaw", sl)], dma=("adaw", sl))
                    bank = n % 2
                    for j in range(4):
                        for kc in range(8):
                            S.op("pe", lambda e, sl=sl, j=j, kc=kc, bank=bank: e.matmul(
                                psum[:, bank, j * 2:j * 2 + 2], adaw[:, sl, kc, j * 128:(j + 1) * 128],
                                scT[:, kc, :], start=(kc == 0), stop=(kc == 7)),
                                 reads=[("adaw", sl), "scT"], writes=PS(bank))
                    ab = COLS["ada_b%d" % l][0]
                    S.op("dve", lambda e, l=l, nb=nb, bank=bank, ab=ab: e.tensor_tensor(
                        out=modc[:, l, nb * 4:(nb + 1) * 4, :],
                        in0=psum[:, bank, 0:8].rearrange("p (j r) -> p j r", r=2),
                        in1=colt[:, ab + nb * 4:ab + nb * 4 + 4].unsqueeze(2).to_broadcast([128, 4, 2]),
                        op=ALU.add),
                         reads=PS(bank) + ["colt"], writes=["modc"])
                    n += 1
            for l in range(2):
                for c0 in (8, 32):
                    S.op("dve", lambda e, l=l, c0=c0: e.tensor_scalar_add(
                        out=modc[:, l, c0:c0 + 8, :], in0=modc[:, l, c0:c0 + 8, :], scalar1=1.0),
                         reads=["modc"], writes=["modc"])
            S.barrier()

    def load_unit(stage, nb, ku):
        _, _, K, _, N = STAGE[stage]
        u = UNIT_OFF[stage] + nb * (K // 1024) + ku
        slot = wring_uses[0] % NWR
        wring_uses[0] += 1
        S.op("sp", lambda e, slot=slot, u=u: e.dma_start(
            out=wring[:, slot], in_=wbf[u].rearrange("p (c n) -> p c n", c=8)),
             reads=[("wbf", u)], writes=[("wr", slot)], dma=("wr", slot))
        return slot

    def linear_fm(stage, xT, xkey, nsub, evac, banks=(0, 1, 2, 3), xoff=0):
        _, _, K, _, N = STAGE[stage]
        T = nsub * 128
        nku = K // 1024
        for nb in range(N // 512):
            slots = [load_unit(stage, nb, ku) for ku in range(nku)]
            for j in range(4):
                bank = banks[j]
                pk = PS(bank)
                for ku in range(nku):
                    for kc in range(8):
                        S.op("pe", lambda e, bank=bank, slot=slots[ku], kc=kc, j=j, ku=ku: e.matmul(
                            psum[:, bank, 0:T], wring[:, slot, kc, j * 128:(j + 1) * 128],
                            xT[:, ku * 8 + kc, xoff:xoff + T], start=(ku == 0 and kc == 0), stop=(ku == nku - 1 and kc == 7)),
                             reads=[("wr", slots[ku]), xkey], writes=pk)
                evac(nb * 4 + j, psum[:, bank, 0:T], pk)

    def linear_tm(stage, xT, xkey, nsub, evac, banks=(0, 1, 2, 3), bias=None, xoff=0):
        _, _, K, _, N = STAGE[stage]
        nku = K // 1024
        for nb in range(N // 512):
            pair = (nb % 2) * 2
            slots = [load_unit(stage, nb, ku) for ku in range(nku)]
            for j in range(nsub):
                bank = banks[pair + j]
                pk = PS(bank)
                first = True
                if bias is not None:
                    bt, bkey, boff = bias
                    S.op("pe", lambda e, bank=bank, nb=nb, bt=bt, boff=boff: e.matmul(
                        psum[:, bank, :], ones_hl[:, :], bt[:, boff + nb * 512:boff + (nb + 1) * 512],
                        start=True, stop=False), reads=[bkey, "ones_hl"], writes=pk)
                    first = False
                for ku in range(nku):
                    for kc in range(8):
                        S.op("pe", lambda e, bank=bank, slot=slots[ku], kc=kc, j=j, ku=ku, first=first: e.matmul(
                            psum[:, bank, :], xT[:, ku * 8 + kc, xoff + j * 128:xoff + (j + 1) * 128], wring[:, slot, kc, :],
                            start=(first and ku == 0 and kc == 0), stop=(ku == nku - 1 and kc == 7)),
                             reads=[("wr", slots[ku]), xkey], writes=pk)
                evac(nb, j, psum[:, bank, :], pk)

    ones_hl = sb("ones_hl", [128, 128], BF16)
    S.op("pool", lambda e: e.memset(ones_hl[:], 1.0), writes=["ones_hl"])

    def hilo_rows(name, dst, dkey, width, tmpf, tmpb):
        o, w = ROWS[name]
        assert w == width
        S.op("pool", lambda e: e.memset(dst[:, 0:w], 0.0), writes=[dkey])
        S.op("sp", lambda e: e.dma_start(out=tmpf[0:1, 0:w], in_=rowv[0:1, o:o + w]), writes=["hl_tmpf"], dma="hl")
        S.op("sp", lambda e: e.dma_start(out=tmpf[1:2, 0:w], in_=rowv[0:1, o:o + w]), writes=["hl_tmpf"], dma="hl")
        S.op("dve", lambda e: e.tensor_copy(out=tmpb[0:2, 0:w], in_=tmpf[0:2, 0:w]), reads=["hl_tmpf"], writes=["hl_tmpb"])
        S.op("dve", lambda e: e.tensor_tensor(out=tmpf[0:2, 0:w], in0=tmpf[0:2, 0:w], in1=tmpb[0:2, 0:w], op=ALU.subtract),
             reads=["hl_tmpf", "hl_tmpb"], writes=["hl_tmpf"])
        S.op("dve", lambda e: e.tensor_copy(out=dst[0:2, 0:w], in_=tmpf[0:2, 0:w]), reads=["hl_tmpf"], writes=[dkey])
        S.op("dve", lambda e: e.tensor_copy(out=dst[0:1, 0:w], in_=tmpb[0:1, 0:w]), reads=["hl_tmpb", dkey], writes=[dkey])

    def bcast_row(name, dst, dkey):
        o, w = ROWS[name]
        S.op("sp", lambda e: e.dma_start(out=dst[:, 0:w], in_=rowv[0:1, o:o + w].partition_broadcast(128)),
             writes=[dkey], dma="const")

    def gate_bcast(l, c0, which, dst, dkey, gtmp):
        for c in range(8):
            S.op("dve", lambda e, c=c: e.tensor_scalar_mul(out=gtmp[:], in0=ones_f[:], scalar1=modc[:, l, c0 + c, which:which + 1]),
                 reads=["ones_f", "modc"], writes=["gtmp"])
            S.op("pe", lambda e, c=c: e.matmul(psum[:, 7, c % 4 * 128:(c % 4 + 1) * 128], gtmp[:], ident_f[:], start=True, stop=True),
                 reads=["gtmp", "ident_f"], writes=PS(7))
            S.op("dve", lambda e, c=c: e.tensor_copy(out=dst[:, c * 128:(c + 1) * 128], in_=psum[:, 7, c % 4 * 128:(c % 4 + 1) * 128]),
                 reads=PS(7), writes=[dkey])

    def modulate_T(h, hkey, l, sc_c0, sh_c0, which, uT, ukey, nsub):
        for c in range(8):
            bank = 4 + (c % 2)
            for j in range(nsub):
                S.op("pe", lambda e, c=c, j=j, bank=bank: e.transpose(
                    psum[:, bank, j * 128:(j + 1) * 128], h[:, j, c * 128:(c + 1) * 128], ident_f[:]),
                     reads=[hkey, "ident_f"], writes=PS(bank))
            eng = "act" if c % 2 == 0 else "dve"
            if eng == "act":
                S.op("act", lambda e, c=c, bank=bank: e.activation(
                    out=uT[:, c, 0:nsub * 128], in_=psum[:, bank, 0:nsub * 128], func=AF.Identity,
                    scale=modc[:, l, sc_c0 + c, which:which + 1], bias=modc[:, l, sh_c0 + c, which:which + 1]),
                     reads=PS(bank) + ["modc"], writes=[ukey])
            else:
                S.op("dve", lambda e, c=c, bank=bank: e.tensor_scalar(
                    out=uT[:, c, 0:nsub * 128], in0=psum[:, bank, 0:nsub * 128],
                    scalar1=modc[:, l, sc_c0 + c, which:which + 1], scalar2=modc[:, l, sh_c0 + c, which:which + 1],
                    op0=ALU.mult, op1=ALU.add),
                     reads=PS(bank) + ["modc"], writes=[ukey])

    def resid_ln(h, hkey, j, yb, ybkey, g_bc, gkey, lng, lngkey, lnb, lnbkey, stat, tmp):
        S.op("dve", lambda e: e.tensor_tensor(out=yb, in0=yb, in1=g_bc[:, :], op=ALU.mult),
             reads=[ybkey, gkey], writes=[ybkey])
        S.op("dve", lambda e: e.scalar_tensor_tensor(out=yb, in0=h[:, j, :], scalar=float(ALPHA), in1=yb,
                                                     op0=ALU.mult, op1=ALU.add),
             reads=[ybkey, hkey], writes=[ybkey])
        for q in range(2):
            S.op("dve", lambda e, q=q: e.bn_stats(out=stat[:, q * 6:(q + 1) * 6], in_=yb[:, q * 512:(q + 1) * 512]),
                 reads=[ybkey], writes=["ln_stat"])
        S.op("dve", lambda e: e.bn_aggr(out=stat[:, 12:14], in_=stat[:, 0:12]), reads=["ln_stat"], writes=["ln_mv"])
        S.op("act", lambda e: e.activation(out=stat[:, 14:15], in_=stat[:, 13:14], func=AF.Sqrt, bias=eps_t[:, 0:1], scale=1.0),
             reads=["ln_mv", "eps_t"], writes=["ln_sd"])
        S.op("dve", lambda e: e.reciprocal(out=stat[:, 15:16], in_=stat[:, 14:15]), reads=["ln_sd"], writes=["ln_rs"])
        S.op("dve", lambda e: e.tensor_scalar(out=yb, in0=yb, scalar1=stat[:, 12:13], scalar2=stat[:, 15:16],
                                              op0=ALU.subtract, op1=ALU.mult),
             reads=[ybkey, "ln_mv", "ln_rs"], writes=[ybkey])
        S.op("pool", lambda e: e.tensor_tensor(out=yb, in0=yb, in1=lng[:, :], op=ALU.mult),
             reads=[ybkey, lngkey], writes=[ybkey])
        S.op("pool", lambda e: e.tensor_tensor(out=h[:, j, :], in0=yb, in1=lnb[:, :], op=ALU.add),
             reads=[ybkey, lnbkey], writes=[hkey])

    eps_t = sb("eps_t", [128, 1], F32)
    S.op("pool", lambda e: e.memset(eps_t[:], LN_EPS), writes=["eps_t"])

    if "p1" in cfg.phases:
        p1 = ExitStack()

        def sb1(name, shape, dt):
            return p1.enter_context(nc.sbuf_tensor(name, list(shape), dt))

        bc = {}
        for nm in ("g1l", "g1c", "g2l", "g2c", "ln1g", "ln1b", "ln2g", "ln2b", "bpw2"):
            bc[nm] = sb1("bc_" + nm, [128, D], F32)
        gtmp = sb1("gtmp", [128, 128], F32)
        hbuf = sb1("hbuf", [128, 2, NS, D], F32)
        uT = sb1("uT", [128, 8, TT], BF16)
        sig = sb1("sig", [128, 8, TT], F32)
        glu = sb1("glu", [128, 8, TT], F32)
        acc = sb1("acc", [128, 8, TT], F32)
        sqt = sb1("sqt", [128, 2, TT], F32)
        lnt = sb1("lnt", [128, 4, TT], F32)
        sT = sb1("sT", [128, 8, TT], BF16)
        ybuf = sb1("ybuf", [128, 2, D], F32)
        stat = sb1("stat", [128, 16], F32)
        hidr = sb1("hidr", [128, 2, TT], BF16)
        hidT = sb1("hidT", [128, 32, TT], BF16)
        xst = sb1("xst", [128, 2, 4, TT], BF16)
        zst = sb1("zst", [128, 2, 4, TT], BF16)

        gate_bcast(0, 16, 0, bc["g1l"], "bc_g1l", gtmp)
        gate_bcast(0, 16, 1, bc["g1c"], "bc_g1c", gtmp)
        gate_bcast(0, 40, 0, bc["g2l"], "bc_g2l", gtmp)
        gate_bcast(0, 40, 1, bc["g2c"], "bc_g2c", gtmp)
        bcast_row("ln1_g0", bc["ln1g"], "bc_ln1g")
        bcast_row("ln1_b0", bc["ln1b"], "bc_ln1b")
        bcast_row("ln2_g0", bc["ln2g"], "bc_ln2g")
        bcast_row("ln2_b0", bc["ln2b"], "bc_ln2b")
        bcast_row("b_pw2", bc["bpw2"], "bc_bpw2")

        wdw0 = COLS["w_dw"][0]
        tiles = [("ctx", 0)] + [("lat", i) for i in range(NA)]
        if cfg.p1_tiles is not None:
            tiles = tiles[:cfg.p1_tiles]
        for ti, (kind, idx) in enumerate(tiles):
            which = 1 if kind == "ctx" else 0
            hs = ti % 2
            h = hbuf[:, hs]
            hkey = ("h", hs)
            src = cx[:, :] if kind == "ctx" else xl[idx * TT:(idx + 1) * TT, :]
            S.op("sp", lambda e, hs=hs, src=src: e.dma_start(out=hbuf[:, hs], in_=src.rearrange("(j p) d -> p j d", p=128)),
                 writes=[hkey], dma=("h", hs))
            if cfg.stop <= 0:
                continue
            modulate_T(h, hkey, 0, 8, 0, which, uT, "uT", NS)

            def ev_pw1(ch, ps, pk):
                if ch >= 8:
                    c = ch - 8
                    if getattr(cfg, 'no_ev', None) == "act":
                        return
                    S.op("act", lambda e, c=c, ps=ps: e.activation(out=sig[:, c, :], in_=ps, func=(AF.Identity if getattr(cfg, "nosig", False) else AF.Sigmoid),
                                                                   bias=(0.0 if getattr(cfg, "nobias", False) else col("b_pw1", 8 + c)), scale=1.0),
                         reads=(pk if getattr(cfg, "nobias", False) else pk + ["colt"]), writes=[("sig", c)])
                else:
                    c = ch
                    if getattr(cfg, 'no_ev', None) == "dve":
                        return
                    S.op("dve", lambda e, c=c, ps=ps: e.scalar_tensor_tensor(
                        out=glu[:, c, :], in0=ps, scalar=col("b_pw1", c), in1=sig[:, c, :], op0=ALU.add, op1=ALU.mult),
                         reads=pk + ["colt", ("sig", c)], writes=[("glu", c)])

            if ti == 0:
                dump("modc", modc[:], ["modc"])
                dump("uT1", uT[:], ["uT"])
            if cfg.stop <= 1:
                continue
            _, _, K, _, N = STAGE["pw1"]
            for nb in (2, 0, 3, 1):
                slot = load_unit("pw1", nb, 0)
                for j in range(4):
                    bank = j
                    pk = PS(bank)
                    for kc in range(8):
                        S.op("pe", lambda e, bank=bank, slot=slot, kc=kc, j=j: e.matmul(
                            psum[:, bank, 0:TT], wring[:, slot, kc, j * 128:(j + 1) * 128],
                            uT[:, kc, :], start=(kc == 0), stop=(kc == 7)),
                             reads=[("wr", slot), "uT"], writes=pk)
                    if getattr(cfg, 'no_ev', None) is not True:
                        ev_pw1(nb * 4 + j, psum[:, bank, 0:TT], pk)
            if cfg.stop <= 2:
                continue
            RL = TT if kind == "ctx" else 64
            NR = TT // RL
            for c in range(8):
                gv = glu[:, c, :].rearrange("p (r t) -> p r t", t=RL)
                av = acc[:, c, :].rearrange("p (r t) -> p r t", t=RL)
                eng = "dve"
                S.op(eng, lambda e, c=c, gv=gv, av=av: e.tensor_scalar(
                    out=av, in0=gv, scalar1=colt[:, wdw0 + c * 31 + 15:wdw0 + c * 31 + 16], scalar2=col("b_dw", c),
                    op0=ALU.mult, op1=ALU.add), reads=[("glu", c), "colt"], writes=[("acc", c)])
                for k in range(31):
                    dlt = k - 15
                    if dlt == 0 or abs(dlt) >= RL:
                        continue
                    lo_o = max(0, -dlt)
                    hi_o = RL - max(0, dlt)
                    S.op(eng, lambda e, c=c, gv=gv, av=av, k=k, dlt=dlt, lo_o=lo_o, hi_o=hi_o: e.scalar_tensor_tensor(
                        out=av[:, :, lo_o:hi_o], in0=gv[:, :, lo_o + dlt:hi_o + dlt],
                        scalar=colt[:, wdw0 + c * 31 + k:wdw0 + c * 31 + k + 1], in1=av[:, :, lo_o:hi_o],
                        op0=ALU.mult, op1=ALU.add), reads=[("glu", c), ("acc", c), "colt"], writes=[("acc", c)])
            if ti == 0:
                dump("glu", glu[:], [("glu", c) for c in range(8)])
            if cfg.stop <= 3:
                continue
            for c in range(8):
                S.op("pe", lambda e, c=c: e.matmul(psum[:, 6, 0:TT], ones_f[:], acc[:, c, :], start=(c == 0), stop=(c == 7)),
                     reads=["ones_f", ("acc", c)], writes=PS(6))
            for c in range(8):
                S.op("act", lambda e, c=c: e.activation(out=sqt[:, c % 2, :], in_=acc[:, c, :], func=AF.Square),
                     reads=[("acc", c)], writes=[("sqt", c % 2)])
                S.op("pe", lambda e, c=c: e.matmul(psum[:, 7, 0:TT], ones_f[:], sqt[:, c % 2, :], start=(c == 0), stop=(c == 7)),
                     reads=["ones_f", ("sqt", c % 2)], writes=PS(7))
            S.op("act", lambda e: e.activation(out=lnt[:, 0, :], in_=psum[:, 6, 0:TT], func=AF.Copy, scale=1.0 / D),
                 reads=PS(6), writes=["lnt0"])
            S.op("dve", lambda e: e.tensor_tensor(out=lnt[:, 2, :], in0=lnt[:, 0, :], in1=lnt[:, 0, :], op=ALU.mult),
                 reads=["lnt0"], writes=["lnt2"])
            S.op("dve", lambda e: e.scalar_tensor_tensor(out=lnt[:, 2, :], in0=psum[:, 7, 0:TT], scalar=1.0 / D,
                                                         in1=lnt[:, 2, :], op0=ALU.mult, op1=ALU.subtract),
                 reads=PS(7) + ["lnt2"], writes=["lnt2"])
            S.op("act", lambda e: e.activation(out=lnt[:, 3, :], in_=lnt[:, 2, :], func=AF.Sqrt, bias=eps_t[:, 0:1], scale=1.0),
                 reads=["lnt2", "eps_t"], writes=["lnt3"])
            S.op("dve", lambda e: e.reciprocal(out=lnt[:, 1, :], in_=lnt[:, 3, :]), reads=["lnt3"], writes=["lnt1"])
            for c in range(8):
                eng = "dve" if c % 2 == 0 else "pool"
                S.op(eng, lambda e, c=c: e.tensor_tensor(out=acc[:, c, :], in0=acc[:, c, :], in1=lnt[:, 0, :], op=ALU.subtract),
                     reads=[("acc", c), "lnt0"], writes=[("acc", c)])
                S.op(eng, lambda e, c=c: e.tensor_tensor(out=acc[:, c, :], in0=acc[:, c, :], in1=lnt[:, 1, :], op=ALU.mult),
                     reads=[("acc", c), "lnt1"], writes=[("acc", c)])
                S.op("act", lambda e, c=c: e.activation(out=sT[:, c, :], in_=acc[:, c, :], func=AF.Silu,
                                                        scale=col("cln_g", c), bias=col("cln_b", c)),
                     reads=[("acc", c), "colt"], writes=["sT"])
            if ti == 0:
                dump("acc", acc[:], [("acc", c) for c in range(8)])
            if cfg.stop <= 4:
                continue
            g1 = ("g1c" if which else "g1l")
            g2 = ("g2c" if which else "g2l")

            def ev_tm(gname, lg, lb, bias_bc=None):
                def ev(nb, j, ps, pk):
                    if bias_bc is None:
                        S.op("act", lambda e, j=j, nb=nb, ps=ps: e.activation(out=ybuf[:, j, nb * 512:(nb + 1) * 512], in_=ps, func=AF.Copy),
                             reads=pk, writes=[("yb", j)])
                    else:
                        S.op("dve", lambda e, j=j, nb=nb, ps=ps: e.tensor_tensor(out=ybuf[:, j, nb * 512:(nb + 1) * 512], in0=ps,
                                                                                in1=bc[bias_bc][:, nb * 512:(nb + 1) * 512], op=ALU.add),
                             reads=pk + ["bc_" + bias_bc], writes=[("yb", j)])
                    if nb == 1:
                        resid_ln(h, hkey, j, ybuf[:, j, :], ("yb", j), bc[gname], "bc_" + gname,
                                 bc[lg], "bc_" + lg, bc[lb], "bc_" + lb, stat, None)
                return ev

            linear_tm("pw2", sT, "sT", NS, ev_tm(g1, "ln1g", "ln1b", "bpw2"))
            if ti == 0:
                dump("sT", sT[:], ["sT"])
                dump("h1mid", hbuf[:, hs], [hkey])
                dump("stat", stat[:], ["ln_stat", "ln_mv", "ln_sd", "ln_rs"])
                dump("yb", ybuf[:], [("yb", 0), ("yb", 1)])
            if cfg.stop <= 5:
                continue
            modulate_T(h, hkey, 0, 32, 24, which, uT, "uT", NS)

            def ev_w1(ch, ps, pk):
                S.op("act", lambda e, ch=ch, ps=ps: e.activation(out=hidr[:, ch % 2, :], in_=ps, func=AF.Relu),
                     reads=pk, writes=[("hidr", ch % 2)])
                S.op("pool", lambda e, ch=ch: e.tensor_tensor(out=hidT[:, ch, :], in0=hidr[:, ch % 2, :], in1=hidr[:, ch % 2, :], op=ALU.mult),
                     reads=[("hidr", ch % 2)], writes=["hidT"])

            linear_fm("w1a", uT, "uT", NS, ev_w1)
            linear_tm("w2a", hidT, "hidT", NS, ev_tm(g2, "ln2g", "ln2b"))
            if ti == 0:
                dump("hmid", hbuf[:, hs], [hkey])
            if cfg.stop <= 6:
                continue
            row0 = NTOK if kind == "ctx" else idx * TT
            if kind == "lat" and idx < NL:
                S.op("pool", lambda e, hs=hs, row0=row0: e.dma_start(
                    out=H1[row0:row0 + TT, :].rearrange("(j p) d -> p j d", p=128), in_=hbuf[:, hs]),
                     reads=[hkey], writes=[("H1", kind, idx)], dma=("hst", hs))
            if cfg.stop <= 7:
                continue
            modulate_T(h, hkey, 1, 8, 0, which, uT, "uT", NS)
            colbase = (NTOK if kind == "ctx" else idx * TT)

            def ev_upx(ch, ps, pk):
                s = (ch // 4) % 2
                S.op("act" if ch % 2 else "dve",
                     (lambda e, ch=ch, ps=ps, s=s: e.activation(out=xst[:, s, ch % 4, :], in_=ps, func=AF.Copy)) if ch % 2 else
                     (lambda e, ch=ch, ps=ps, s=s: e.tensor_copy(out=xst[:, s, ch % 4, :], in_=ps)),
                     reads=pk, writes=[("xst", s)])
                if ch % 4 == 3:
                    nb = ch // 4
                    S.op("pool", lambda e, s=s, nb=nb, colbase=colbase: e.dma_start(
                        out=XM[nb * 512:(nb + 1) * 512, colbase:colbase + TT].rearrange("(c p) t -> p c t", p=128),
                        in_=xst[:, s]), reads=[("xst", s)], writes=["XM"], dma=("xst", s))

            linear_fm("upx", uT, "uT", NS, ev_upx)
            if kind == "lat" and idx < NL:
                def ev_upz(ch, ps, pk):
                    sl = (ch // 4) % 2
                    S.op("act", lambda e, ch=ch, ps=ps, sl=sl: e.activation(out=zst[:, sl, ch % 4, :], in_=ps, func=AF.Silu),
                         reads=pk, writes=[("zst", sl)])
                    if ch % 4 == 3:
                        nb = ch // 4
                        S.op("pool", lambda e, sl=sl, nb=nb, idx=idx: e.dma_start(
                            out=SZT[nb * 512:(nb + 1) * 512, idx * TT:(idx + 1) * TT].rearrange("(c p) t -> p c t", p=128),
                            in_=zst[:, sl]), reads=[("zst", sl)], writes=[("SZT", idx)], dma=("zst", sl))

                linear_fm("upz", uT, "uT", NS, ev_upz)
        S.barrier()
        p1.close()


    maskA = sb("maskA", [128, 128], F32)
    maskB = sb("maskB", [128, 128], F32)
    ones_b = sb("ones_b", [128, 4], BF16)
    cst = sb("cst", [128, 4], F32)
    S.op("pool", lambda e: e.memset(maskA[:], 1.0), writes=["maskA"])
    S.op("pool", lambda e: e.affine_select(out=maskA[:], in_=maskA[:], pattern=[[1, 128]], compare_op=ALU.is_ge,
                                           fill=0.0, base=0, channel_multiplier=-1), reads=["maskA"], writes=["maskA"])
    S.op("pool", lambda e: e.memset(maskB[:], 1.0), writes=["maskB"])
    S.op("pool", lambda e: e.affine_select(out=maskB[:], in_=maskB[:], pattern=[[-1, 128]], compare_op=ALU.is_ge,
                                           fill=0.0, base=0, channel_multiplier=1), reads=["maskB"], writes=["maskB"])
    S.op("pool", lambda e: e.memset(ones_b[:], 1.0), writes=["ones_b"])
    S.op("pool", lambda e: e.memset(cst[:, 0:1], LN_EPS), writes=["cst"])
    S.op("pool", lambda e: e.memset(cst[:, 1:2], float(np.log(KSCALE))), writes=["cst"])
    S.op("pool", lambda e: e.memset(cst[:, 2:3], 1.0), writes=["cst"])
    S.op("pool", lambda e: e.memset(cst[:, 3:4], 0.0), writes=["cst"])

    ftiles = [("ctx", 0)] + [("lat", i) for i in range(NA)]

    if "p2" in cfg.phases:
        p2 = ExitStack()

        def sb2(name, shape, dt):
            return p2.enter_context(nc.sbuf_tensor(name, list(shape), dt))

        xmT = sb2("xmT", [128, 16, TT + 4], BF16)
        cacc = sb2("cacc", [128, 2, TT], F32)
        xcT = sb2("xcT", [128, 16, TT], BF16)
        qT = sb2("qT", [128, 16, TT], BF16)
        kT = sb2("kT", [128, 16, TT], BF16)
        vT = sb2("vT", [128, 16, TT], BF16)
        wg_f = sb2("wg_f", [128, 48, 16], F32)
        wg = sb2("wg", [128, 48, 16], BF16)
        bg_bc = sb2("bg_bc", [128, 16], F32)
        bo_bc = sb2("bo_bc", [128, 2 * E], F32)
        tst = sb2("tst", [128, 2, E], BF16)
        ost = sb2("ost", [128, 2, 2 * E], BF16)
        otmp = sb2("otmp", [128, 2, 512], F32)
        gsb = sb2("gsb", [128, 16], F32)
        gl1 = sb2("gl1", [128, 8], F32)
        gcum = sb2("gcum", [128, 16], F32)
        gtt = sb2("gtt", [128, 2, 8], F32)
        gso = sb2("gso", [128, 2, 32], F32)

        S.op("sp", lambda e: e.dma_start(out=wg_f[:], in_=w_gate_in.rearrange("(c p) n -> p c n", p=128)), writes=["wg_f"], dma="const2")
        S.op("dve", lambda e: e.tensor_copy(out=wg[:], in_=wg_f[:]), reads=["wg_f"], writes=["wg"])
        bcast_row("b_gate", bg_bc, "bg_bc")
        bcast_row("b_o", bo_bc, "bo_bc")
        wmc0 = COLS["w_mc"][0]

        for f, (kind, idx) in enumerate(ftiles):
            is_ctx = kind == "ctx"
            own = (kind == "lat" and idx < NL)
            S.op("pool", lambda e: e.memset(xmT[:, :, 0:2], 0.0), writes=["xmT"])
            S.op("pool", lambda e: e.memset(xmT[:, :, TT + 2:TT + 4], 0.0), writes=["xmT"])
            if is_ctx:
                c0, c1, d0 = NTOK, NTOK + TT, 2
            else:
                c0 = max(0, idx * TT - 2)
                c1 = min(idx * TT + TT + 2, NTOK)
                d0 = 2 - (idx * TT - c0)
            S.op("sp", lambda e, c0=c0, c1=c1, d0=d0: e.dma_start(
                out=xmT[:, :, d0:d0 + (c1 - c0)], in_=XM[:, c0:c1].rearrange("(c p) t -> p c t", p=128)),
                 reads=["XM"], writes=["xmT"], dma="xmT")
            for c in range(16):
                cs = c % 2
                S.op("dve", lambda e, c=c, cs=cs: e.tensor_scalar(
                    out=cacc[:, cs, :], in0=xmT[:, c, 2:2 + TT], scalar1=colt[:, wmc0 + c * 5 + 2:wmc0 + c * 5 + 3],
                    scalar2=col("b_mc", c), op0=ALU.mult, op1=ALU.add), reads=["xmT", "colt"], writes=[("cacc", cs)])
                for k in (0, 1, 3, 4):
                    S.op("dve", lambda e, c=c, cs=cs, k=k: e.scalar_tensor_tensor(
                        out=cacc[:, cs, :], in0=xmT[:, c, k:k + TT], scalar=colt[:, wmc0 + c * 5 + k:wmc0 + c * 5 + k + 1],
                        in1=cacc[:, cs, :], op0=ALU.mult, op1=ALU.add), reads=["xmT", "colt", ("cacc", cs)], writes=[("cacc", cs)])
                S.op("act", lambda e, c=c, cs=cs: e.activation(out=xcT[:, c, :], in_=cacc[:, cs, :], func=AF.Silu),
                     reads=[("cacc", cs)], writes=["xcT"])
            if own:
                S.op("pool", lambda e, f=f: e.dma_start(out=XC[:, f * TT:(f + 1) * TT].rearrange("(c p) t -> p c t", p=128), in_=xcT[:]),
                     reads=["xcT"], writes=[("XC", f)], dma="xcst")

            def ev_copy(dst, dkey):
                def ev(ch, ps, pk):
                    if ch % 2:
                        S.op("act", lambda e, ch=ch, ps=ps: e.activation(out=dst[:, ch, :], in_=ps, func=AF.Copy), reads=pk, writes=[dkey])
                    else:
                        S.op("dve", lambda e, ch=ch, ps=ps: e.tensor_copy(out=dst[:, ch, :], in_=ps), reads=pk, writes=[dkey])
                return ev

            linear_fm("wq", xcT, "xcT", NS, ev_copy(qT, "qT"))
            if own:
                S.op("pool", lambda e, f=f: e.dma_start(out=QT[:, f * TT:(f + 1) * TT].rearrange("(c p) t -> p c t", p=128), in_=qT[:]),
                     reads=["qT"], writes=[("QT", f)], dma="qst")
            linear_fm("wk", xcT, "xcT", NS, ev_copy(kT, "kT"))
            if own:
                S.op("pool", lambda e, f=f: e.dma_start(out=KT[:, f * TT:(f + 1) * TT].rearrange("(c p) t -> p c t", p=128), in_=kT[:]),
                     reads=["kT"], writes=[("KT", f)], dma="kst")
            linear_fm("wv", xmT, "xmT", NS, ev_copy(vT, "vT"), xoff=2)

            for j in range(NS):
                n = 0
                for (src, skey, base) in ((qT, "qT", 0), (kT, "kT", 16), (vT, "vT", 32)):
                    for c in range(16):
                        S.op("pe", lambda e, src=src, c=c, j=j, base=base, n=n: e.matmul(
                            psum[:, 6, 0:16], src[:, c, j * 128:(j + 1) * 128], wg[:, base + c, :], start=(n == 0), stop=(n == 47)),
                             reads=[skey, "wg"], writes=PS(6))
                        n += 1
                S.op("dve", lambda e: e.tensor_tensor(out=gsb[:], in0=psum[:, 6, 0:16], in1=bg_bc[:], op=ALU.add),
                     reads=PS(6) + ["bg_bc"], writes=["gsb"])
                gv = gsb[:].rearrange("p (d g h) -> p d g h", d=2, g=2)
                S.op("act", lambda e, gv=gv: e.activation(out=gl1[:].rearrange("p (d h) -> p d h", d=2), in_=gv[:, :, 1, :], func=AF.Exp, scale=-1.0),
                     reads=["gsb"], writes=["gl1"])
                S.op("act", lambda e: e.activation(out=gl1[:], in_=gl1[:], func=AF.Ln, bias=cst[:, 2:3], scale=1.0),
                     reads=["gl1", "cst"], writes=["gl1"])
                S.op("pe", lambda e: e.matmul(psum[:, 7, 0:4], maskA[:], gl1[:, 0:4], start=True, stop=True), reads=["maskA", "gl1"], writes=PS(7))
                S.op("pe", lambda e: e.matmul(psum[:, 7, 4:8], maskB[:], gl1[:, 4:8], start=True, stop=True), reads=["maskB", "gl1"], writes=PS(7))
                S.op("pe", lambda e: e.matmul(psum[:, 7, 8:16], ones_f[:], gl1[:, 0:8], start=True, stop=True), reads=["ones_f", "gl1"], writes=PS(7))
                S.op("dve", lambda e: e.tensor_copy(out=gcum[:], in_=psum[:, 7, 0:16]), reads=PS(7), writes=["gcum"])
                ncum = gcum[:, 0:8].rearrange("p (d h) -> p d h", d=2)
                ntot = gcum[:, 8:16].rearrange("p (d h) -> p d h", d=2)
                go = gso[:, j, :].rearrange("p (d k h) -> p d k h", d=2, k=4)
                S.op("act", lambda e, go=go, ncum=ncum: e.activation(out=go[:, :, 0, :], in_=ncum, func=AF.Exp, scale=-1.0),
                     reads=["gcum"], writes=[("gso", j)])
                S.op("dve", lambda e, gv=gv, ncum=ncum: e.tensor_tensor(out=gtt[:, :, 0:4], in0=gv[:, :, 0, :], in1=ncum, op=ALU.add),
                     reads=["gsb", "gcum"], writes=["gtt"])
                S.op("dve", lambda e, ntot=ntot: e.tensor_tensor(out=gtt[:, :, 4:8], in0=gtt[:, :, 0:4], in1=ntot, op=ALU.subtract),
                     reads=["gtt", "gcum"], writes=["gtt"])
                S.op("act", lambda e, go=go: e.activation(out=go[:, :, 1, :], in_=gtt[:, :, 0:4], func=AF.Exp, bias=cst[:, 1:2], scale=1.0),
                     reads=["gtt", "cst"], writes=[("gso", j)])
                S.op("act", lambda e, go=go: e.activation(out=go[:, :, 2, :], in_=gtt[:, :, 4:8], func=AF.Exp, bias=cst[:, 1:2], scale=1.0),
                     reads=["gtt", "cst"], writes=[("gso", j)])
                S.op("act", lambda e, go=go, ntot=ntot: e.activation(out=go[:, :, 3, :], in_=ntot, func=AF.Exp, scale=-1.0),
                     reads=["gcum"], writes=[("gso", j)])
                S.op("pool", lambda e, f=f, j=j: e.dma_start(out=GS[f * TT + j * 128:f * TT + (j + 1) * 128, :], in_=gso[:, j, :]),
                     reads=[("gso", j)], writes=[("GS", f)], dma=("gso", j))

            def ev_tok(DST, dname):
                def ev(nb, j, ps, pk):
                    if nb % 2:
                        S.op("act", lambda e, nb=nb, j=j, ps=ps: e.activation(out=tst[:, j, nb * 512:(nb + 1) * 512], in_=ps, func=AF.Copy),
                             reads=pk, writes=[("tst", j)])
                    else:
                        S.op("dve", lambda e, nb=nb, j=j, ps=ps: e.tensor_copy(out=tst[:, j, nb * 512:(nb + 1) * 512], in_=ps),
                             reads=pk, writes=[("tst", j)])
                    if nb == 3:
                        S.op("pool", lambda e, j=j, f=f: e.dma_start(out=DST[f * TT + j * 128:f * TT + (j + 1) * 128, :], in_=tst[:, j, :]),
                             reads=[("tst", j)], writes=[(dname, f)], dma=("tst", j))
                return ev

            linear_tm("wk", xcT, "xcT", NS, ev_tok(KTOK, "KTOK"))
            linear_tm("wv", xmT, "xmT", NS, ev_tok(VTOK, "VTOK"), xoff=2)
            if own:
                def ev_o(nb, j, ps, pk):
                    S.op("dve", lambda e, nb=nb, j=j, ps=ps: e.tensor_tensor(out=otmp[:, j, :], in0=ps, in1=bo_bc[:, nb * 512:(nb + 1) * 512], op=ALU.add),
                         reads=pk + ["bo_bc"], writes=[("otmp", j)])
                    S.op("act", lambda e, nb=nb, j=j: e.activation(out=ost[:, j, nb * 512:(nb + 1) * 512], in_=otmp[:, j, :], func=AF.Sigmoid),
                         reads=[("otmp", j)], writes=[("ost", j)])
                    if nb == 7:
                        S.op("pool", lambda e, j=j, f=f: e.dma_start(out=OAB[f * TT + j * 128:f * TT + (j + 1) * 128, :], in_=ost[:, j, :]),
                             reads=[("ost", j)], writes=[("OAB", f)], dma=("ost", j))

                linear_tm("wo", xmT, "xmT", NS, ev_o, xoff=2)
        S.barrier()
        p2.close()

    if "p3" in cfg.phases:
        p3 = ExitStack()

        def sb3(name, shape, dt):
            return p3.enter_context(nc.sbuf_tensor(name, list(shape), dt))

        Cst = sb3("Cst", [128, 16, 512], F32)
        Cbf = sb3("Cbf", [128, 16, 512], BF16)
        nst = sb3("nst", [128, 16], F32)
        nbf = sb3("nbf", [128, 16], BF16)
        qS = sb3("qS", [128, 16, TT], BF16)
        kS = sb3("kS", [128, 16, TT], BF16)
        ktS = sb3("ktS", [128, 2, E], BF16)
        vtS = sb3("vtS", [128, 2, E], BF16)
        oS = sb3("oS", [128, 2, E], BF16)
        gS = sb3("gS", [128, 2, 32], F32)
        stS = sb3("stS", [128, 128], BF16)
        kgS = sb3("kgS", [128, 512], BF16)
        rS = sb3("rS", [128, 4], F32)
        hob = sb3("hob", [128, E], F32)
        hoa = sb3("hoa", [128, E], F32)
        xcS = sb3("xcS", [128, 16, TT], BF16)
        szS = sb3("szS", [128, 16, TT], BF16)
        yTs = sb3("yTs", [128, 16, TT], BF16)
        gst = sb3("gst", [128, 16], F32)
        ytmp = sb3("ytmp", [128, 2, 128], F32)
        ytm2 = sb3("ytm2", [128, 2, 128], F32)

        def state_init_zero():
            S.op("pool", lambda e: e.memset(Cst[:], 0.0), writes=["Cst"])
            S.op("pool", lambda e: e.memset(Cbf[:], 0.0), writes=["Cbf"])
            S.op("pool", lambda e: e.memset(nst[:], 0.0), writes=["nst"])
            S.op("pool", lambda e: e.memset(nbf[:], 0.0), writes=["nbf"])

        def scan_tile(dirn, f, is_ctx, lat_idx):
            do = dirn * 16
            if not is_ctx:
                S.op("sp", lambda e: e.dma_start(out=qS[:], in_=QT[:, f * TT:(f + 1) * TT].rearrange("(c p) t -> p c t", p=128)),
                     reads=[("QT", f)], writes=["qS"], dma="qS")
                S.op("sp", lambda e: e.dma_start(out=kS[:], in_=KT[:, f * TT:(f + 1) * TT].rearrange("(c p) t -> p c t", p=128)),
                     reads=[("KT", f)], writes=["kS"], dma="kS")
            S.op("sp", lambda e: e.dma_start(out=ktS[:], in_=KTOK[f * TT:(f + 1) * TT, :].rearrange("(j p) d -> p j d", p=128)),
                 reads=[("KTOK", f)], writes=["ktS"], dma="ktS")
            S.op("sp", lambda e: e.dma_start(out=vtS[:], in_=VTOK[f * TT:(f + 1) * TT, :].rearrange("(j p) d -> p j d", p=128)),
                 reads=[("VTOK", f)], writes=["vtS"], dma="vtS")
            S.op("sp", lambda e: e.dma_start(out=gS[:], in_=GS[f * TT:(f + 1) * TT, :].rearrange("(j p) d -> p j d", p=128)),
                 reads=[("GS", f)], writes=["gS"], dma="gS")
            if not is_ctx:
                S.op("sp", lambda e: e.dma_start(out=oS[:], in_=OAB[f * TT:(f + 1) * TT, dirn * E:(dirn + 1) * E].rearrange("(j p) d -> p j d", p=128)),
                     reads=[("OAB", f)], writes=["oS"], dma="oS")
                if dirn == 1:
                    S.op("sp", lambda e: e.dma_start(out=xcS[:], in_=XC[:, f * TT:(f + 1) * TT].rearrange("(c p) t -> p c t", p=128)),
                         reads=[("XC", f)], writes=["xcS"], dma="xcS")
                    S.op("sp", lambda e: e.dma_start(out=szS[:], in_=SZT[:, lat_idx * TT:(lat_idx + 1) * TT].rearrange("(c p) t -> p c t", p=128)),
                         reads=[("SZT", lat_idx)], writes=["szS"], dma="szS")
            mask = maskA if dirn == 0 else maskB
            mkey = "maskA" if dirn == 0 else "maskB"
            for j in ((0, 1) if dirn == 0 else (1, 0)):
                tok0 = lat_idx * TT + j * 128
                if (not is_ctx) and dirn == 1:
                    S.op("sp", lambda e, tok0=tok0: e.dma_start(out=hoa[:], in_=HOA[tok0:tok0 + 128, :]),
                         reads=[("HOA", lat_idx, j)], writes=["hoa"], dma="hoa")
                for h in range(NH):
                    tsl = slice(j * 128, (j + 1) * 128)
                    hsl = slice(h * 512, (h + 1) * 512)
                    gcol = lambda kind, h=h, j=j: gS[:, j, do + kind * 4 + h:do + kind * 4 + h + 1]
                    if not is_ctx:
                        for dc in range(4):
                            S.op("pe", lambda e, h=h, dc=dc, tsl=tsl: e.matmul(psum[:, 0, 0:128], kS[:, h * 4 + dc, tsl], qS[:, h * 4 + dc, tsl],
                                                                               start=(dc == 0), stop=(dc == 3)),
                                 reads=["kS", "qS"], writes=PS(0))
                        S.op("dve", lambda e, gcol=gcol: e.scalar_tensor_tensor(out=stS[:], in0=psum[:, 0, 0:128], scalar=gcol(1), in1=mask[:],
                                                                                op0=ALU.mult, op1=ALU.mult),
                             reads=PS(0) + ["gS", mkey], writes=["stS"])
                        S.op("pe", lambda e, j=j, hsl=hsl: e.matmul(psum[:, 1, :], stS[:], vtS[:, j, hsl], start=True, stop=False),
                             reads=["stS", "vtS"], writes=PS(1))
                        for dc in range(4):
                            S.op("pe", lambda e, h=h, dc=dc, tsl=tsl: e.matmul(psum[:, 1, :], qS[:, h * 4 + dc, tsl], Cbf[:, h * 4 + dc, :],
                                                                               start=False, stop=(dc == 3)),
                                 reads=["qS", "Cbf"], writes=PS(1))
                        S.op("pe", lambda e: e.matmul(psum[:, 2, 0:1], stS[:], ones_b[:, 0:1], start=True, stop=False),
                             reads=["stS", "ones_b"], writes=PS(2))
                        for dc in range(4):
                            S.op("pe", lambda e, h=h, dc=dc, tsl=tsl: e.matmul(psum[:, 2, 0:1], qS[:, h * 4 + dc, tsl], nbf[:, h * 4 + dc:h * 4 + dc + 1],
                                                                               start=False, stop=(dc == 3)),
                                 reads=["qS", "nbf"], writes=PS(2))
                        S.op("dve", lambda e, gcol=gcol: e.tensor_tensor(out=rS[:, 0:1], in0=psum[:, 2, 0:1], in1=gcol(0), op=ALU.mult),
                             reads=PS(2) + ["gS"], writes=["rS0"])
                        S.op("act", lambda e: e.activation(out=rS[:, 1:2], in_=rS[:, 0:1], func=AF.Abs),
                             reads=["rS0"], writes=["rS1"])
                        S.op("dve", lambda e: e.tensor_scalar_max(out=rS[:, 1:2], in0=rS[:, 1:2], scalar1=1.0),
                             reads=["rS1"], writes=["rS1"])
                        S.op("dve", lambda e: e.reciprocal(out=rS[:, 2:3], in_=rS[:, 1:2]), reads=["rS1"], writes=["rS2"])
                        S.op("dve", lambda e, gcol=gcol: e.tensor_tensor(out=rS[:, 3:4], in0=rS[:, 2:3], in1=gcol(0), op=ALU.mult),
                             reads=["rS2", "gS"], writes=["rS3"])
                        S.op("dve", lambda e, j=j, hsl=hsl: e.scalar_tensor_tensor(out=hob[:, hsl], in0=psum[:, 1, :], scalar=rS[:, 3:4],
                                                                                   in1=oS[:, j, hsl], op0=ALU.mult, op1=ALU.mult),
                             reads=PS(1) + ["rS3", "oS"], writes=[("hob", h)])
                        if dirn == 1:
                            S.op("pool", lambda e, hsl=hsl: e.tensor_tensor(out=hob[:, hsl], in0=hob[:, hsl], in1=hoa[:, hsl], op=ALU.add),
                                 reads=[("hob", h), "hoa"], writes=[("hob", h)])
                    S.op("pool", lambda e, j=j, hsl=hsl, gcol=gcol: e.tensor_scalar_mul(out=kgS[:], in0=ktS[:, j, hsl], scalar1=gcol(2)),
                         reads=["ktS", "gS"], writes=["kgS"])
                    for dc in range(4):
                        S.op("pe", lambda e, j=j, hsl=hsl, dc=dc: e.matmul(psum[:, 3 + dc, :], kgS[:, dc * 128:(dc + 1) * 128], vtS[:, j, hsl],
                                                                           start=True, stop=True),
                             reads=["kgS", "vtS"], writes=PS(3 + dc))
                        S.op("dve", lambda e, h=h, dc=dc, gcol=gcol: e.scalar_tensor_tensor(
                            out=Cst[:, h * 4 + dc, :], in0=Cst[:, h * 4 + dc, :], scalar=gcol(3), in1=psum[:, 3 + dc, :],
                            op0=ALU.mult, op1=ALU.add), reads=PS(3 + dc) + ["gS", "Cst"], writes=["Cst"])
                        S.op("act", lambda e, h=h, dc=dc: e.activation(out=Cbf[:, h * 4 + dc, :], in_=Cst[:, h * 4 + dc, :], func=AF.Copy),
                             reads=["Cst"], writes=["Cbf"])
                    for dc in range(4):
                        S.op("pe", lambda e, dc=dc: e.matmul(psum[:, 7, dc:dc + 1], kgS[:, dc * 128:(dc + 1) * 128], ones_b[:, 0:1],
                                                             start=True, stop=True),
                             reads=["kgS", "ones_b"], writes=PS(7))
                    S.op("dve", lambda e, h=h, gcol=gcol: e.scalar_tensor_tensor(
                        out=nst[:, h * 4:h * 4 + 4], in0=nst[:, h * 4:h * 4 + 4], scalar=gcol(3), in1=psum[:, 7, 0:4],
                        op0=ALU.mult, op1=ALU.add), reads=PS(7) + ["gS", "nst"], writes=["nst"])
                    S.op("dve", lambda e, h=h: e.tensor_copy(out=nbf[:, h * 4:h * 4 + 4], in_=nst[:, h * 4:h * 4 + 4]),
                         reads=["nst"], writes=["nbf"])
                if is_ctx:
                    continue
                if dirn == 0:
                    S.op("pool", lambda e, tok0=tok0: e.dma_start(out=HOA[tok0:tok0 + 128, :], in_=hob[:]),
                         reads=[("hob", h) for h in range(NH)], writes=[("HOA", lat_idx, j)], dma="hob")
                else:
                    for h in range(NH):
                        hsl = slice(h * 512, (h + 1) * 512)
                        S.op("dve", lambda e, hsl=hsl: e.bn_stats(out=gst[:, 0:6], in_=hob[:, hsl]), reads=[("hob", h)], writes=["gst0"])
                        S.op("dve", lambda e: e.bn_aggr(out=gst[:, 6:8], in_=gst[:, 0:6]), reads=["gst0"], writes=["gst1"])
                        S.op("act", lambda e: e.activation(out=gst[:, 8:9], in_=gst[:, 7:8], func=AF.Sqrt, bias=cst[:, 0:1], scale=1.0),
                             reads=["gst1", "cst"], writes=["gst2"])
                        S.op("dve", lambda e: e.reciprocal(out=gst[:, 9:10], in_=gst[:, 8:9]), reads=["gst2"], writes=["gst3"])
                        S.op("dve", lambda e, hsl=hsl: e.tensor_scalar(out=hob[:, hsl], in0=hob[:, hsl], scalar1=gst[:, 6:7], scalar2=gst[:, 9:10],
                                                                       op0=ALU.subtract, op1=ALU.mult),
                             reads=[("hob", h), "gst1", "gst3"], writes=[("hob", h)])
                    for c in range(16):
                        pb = c % 4
                        S.op("pe", lambda e, c=c, pb=pb: e.transpose(psum[:, 7, pb * 128:(pb + 1) * 128], hob[:, c * 128:(c + 1) * 128], ident_f[:]),
                             reads=[("hob", c // 4), "ident_f"], writes=PS(7))
                        ys = c % 2
                        S.op("pool", lambda e, c=c, ys=ys, tsl=tsl: e.tensor_scalar_mul(out=ytmp[:, ys, :], in0=xcS[:, c, tsl], scalar1=col("skip", c)),
                             reads=["xcS", "colt"], writes=[("ytmp", ys)])
                        S.op("dve", lambda e, c=c, ys=ys, pb=pb: e.scalar_tensor_tensor(
                            out=ytm2[:, ys, :], in0=psum[:, 7, pb * 128:(pb + 1) * 128], scalar=col("gn_g", c), in1=ytmp[:, ys, :],
                            op0=ALU.mult, op1=ALU.add), reads=PS(7) + ["colt", ("ytmp", ys)], writes=[("ytm2", ys)])
                        S.op("pool", lambda e, c=c, ys=ys, tsl=tsl: e.tensor_tensor(out=yTs[:, c, tsl], in0=ytm2[:, ys, :], in1=szS[:, c, tsl], op=ALU.mult),
                             reads=[("ytm2", ys), "szS"], writes=["yTs"])
            if (not is_ctx) and dirn == 1:
                S.op("pool", lambda e: e.dma_start(out=YT[:, lat_idx * TT:(lat_idx + 1) * TT].rearrange("(c p) t -> p c t", p=128), in_=yTs[:]),
                     reads=["yTs"], writes=[("YT", lat_idx)], dma="yst")

        state_init_zero()
        for f in range(0, NL + 1):
            scan_tile(0, f, f == 0, ftiles[f][1])
        state_init_zero()
        scan_tile(1, 0, True, 0)
        for f in range(NA, NL, -1):
            scan_tile(1, f, True, ftiles[f][1])
        for f in range(NL, 0, -1):
            scan_tile(1, f, False, ftiles[f][1])
        S.barrier()
        p3.close()

        p4 = ExitStack()

        def sb4(name, shape, dt):
            return p4.enter_context(nc.sbuf_tensor(name, list(shape), dt))

        bd = {}
        for nm in ("g1", "g2", "ln1g", "ln1b", "ln2g", "ln2b"):
            bd[nm] = sb4("bd_" + nm, [128, D], F32)
        gtmp4v = sb4("gtmp4", [128, 128], F32)
        hbuf4v = sb4("hbuf4", [128, 2, NS, D], F32)
        yin = sb4("yin", [128, 2, 16, TT], BF16)
        uT4v = sb4("uT4", [128, 8, TT], BF16)
        ybuf4v = sb4("ybuf4", [128, 2, D], F32)
        stat4v = sb4("stat4", [128, 16], F32)
        hidr4v = sb4("hidr4", [128, 2, TT], BF16)
        hidT4v = sb4("hidT4", [128, 32, TT], BF16)
        gate_bcast(1, 16, 0, bd["g1"], "bd_g1", gtmp4v)
        gate_bcast(1, 40, 0, bd["g2"], "bd_g2", gtmp4v)
        bcast_row("ln1_g1", bd["ln1g"], "bd_ln1g")
        bcast_row("ln1_b1", bd["ln1b"], "bd_ln1b")
        bcast_row("ln2_g1", bd["ln2g"], "bd_ln2g")
        bcast_row("ln2_b1", bd["ln2b"], "bd_ln2b")
        for i in range(NL):
            hs = i % 2
            h = hbuf4v[:, hs]
            hkey = ("h4", hs)
            S.op("sp", lambda e, hs=hs, i=i: e.dma_start(out=hbuf4v[:, hs], in_=H1[i * TT:(i + 1) * TT, :].rearrange("(j p) d -> p j d", p=128)),
                 reads=[("H1", "lat", i)], writes=[hkey], dma=("h4", hs))
            S.op("sp", lambda e, hs=hs, i=i: e.dma_start(out=yin[:, hs], in_=YT[:, i * TT:(i + 1) * TT].rearrange("(c p) t -> p c t", p=128)),
                 reads=[("YT", i)], writes=[("yin", hs)], dma=("yin", hs))

            def ev_tm4(gname, lg, lb):
                def ev(nb, j, ps, pk):
                    S.op("act", lambda e, j=j, nb=nb, ps=ps: e.activation(out=ybuf4v[:, j, nb * 512:(nb + 1) * 512], in_=ps, func=AF.Copy),
                         reads=pk, writes=[("yb4", j)])
                    if nb == 1:
                        resid_ln(h, hkey, j, ybuf4v[:, j, :], ("yb4", j), bd[gname], "bd_" + gname,
                                 bd[lg], "bd_" + lg, bd[lb], "bd_" + lb, stat4v, None)
                return ev

            linear_tm("down", yin[:, hs], ("yin", hs), NS, ev_tm4("g1", "ln1g", "ln1b"))
            modulate_T(h, hkey, 1, 32, 24, 0, uT4v, "uT4", NS)

            def ev_w14(ch, ps, pk):
                S.op("act", lambda e, ch=ch, ps=ps: e.activation(out=hidr4v[:, ch % 2, :], in_=ps, func=AF.Relu),
                     reads=pk, writes=[("hidr4", ch % 2)])
                S.op("pool", lambda e, ch=ch: e.tensor_tensor(out=hidT4v[:, ch, :], in0=hidr4v[:, ch % 2, :], in1=hidr4v[:, ch % 2, :], op=ALU.mult),
                     reads=[("hidr4", ch % 2)], writes=["hidT4"])

            linear_fm("w1b", uT4v, "uT4", NS, ev_w14)
            linear_tm("w2b", hidT4v, "hidT4", NS, ev_tm4("g2", "ln2g", "ln2b"))
            S.op("pool", lambda e, hs=hs, i=i: e.dma_start(out=out_d[i * TT:(i + 1) * TT, :].rearrange("(j p) d -> p j d", p=128), in_=hbuf4v[:, hs]),
                 reads=[hkey], writes=[("out", i)], dma=("ost4", hs))
        p4.close()

    S.analyze()
    dma_keys = list(S.dma_cnt.keys())
    with ExitStack() as ss:
        sems = {e: ss.enter_context(nc.semaphore("sem_" + e)) for e in Sched.ENGS}
        dma_sems = {k: ss.enter_context(nc.semaphore("dsem%d" % i)) for i, k in enumerate(dma_keys)}
        with nc.Block() as block:
            S.emit(nc, block, sems, dma_sems)
    es.close()
    return nc


def _colmajor(v, nchunk):
    return np.ascontiguousarray(v.reshape(nchunk, 128).T)


def make_core_inputs(inp, b, s, cfg):
    NTOK = cfg.n_tok
    own = cfg.n_lat * TT
    seq = inp["x"].shape[1]
    half = seq // 2
    x = inp["x"][b]
    ctx = inp["ctx"][b]
    if s == 0:
        xl = x[0:half + TT][:NTOK] if NTOK <= half + TT else None
        xl = x[0:NTOK]
        cxx = ctx
    else:
        xr = x[::-1]
        xl = xr[0:NTOK]
        cxx = ctx[::-1]
    flip = (s == 1)
    w_dw = inp["conv_w_dw"][0]
    w_mc = inp["ml_w_conv"][0]
    if flip:
        w_dw = w_dw[::-1]
        w_mc = w_mc[::-1]
    colv = np.zeros((128, NCOL), np.float32)

    def put(name, arr):
        o, w = COLS[name]
        assert arr.shape == (128, w), (name, arr.shape, w)
        colv[:, o:o + w] = arr

    put("ada_b0", _colmajor(inp["ada_b"][0], 48))
    put("ada_b1", _colmajor(inp["ada_b"][1], 48))
    put("b_pw1", _colmajor(inp["conv_b_pw1"][0], 16))
    put("b_dw", _colmajor(inp["conv_b_dw"][0], 8))
    put("cln_g", _colmajor(inp["conv_ln_g"][0], 8))
    put("cln_b", _colmajor(inp["conv_ln_b"][0], 8))
    put("w_dw", np.ascontiguousarray(w_dw.T.reshape(8, 128, 31).transpose(1, 0, 2)).reshape(128, 8 * 31))
    put("w_mc", np.ascontiguousarray(w_mc.T.reshape(16, 128, 5).transpose(1, 0, 2)).reshape(128, 16 * 5))
    put("b_mc", _colmajor(inp["ml_b_conv"][0], 16))
    cT = np.stack([_colmajor(inp["c"][b], 8), _colmajor(inp["c_ctx"], 8)], axis=-1)
    put("cT", cT.reshape(128, 16))
    put("gn_g", _colmajor(inp["ml_gn_g"][0], 16))
    put("skip", _colmajor(inp["ml_skip"][0], 16))

    rowv = np.zeros((1, NROW), np.float32)

    def putr(name, arr):
        o, w = ROWS[name]
        assert arr.shape == (w,), (name, arr.shape)
        rowv[0, o:o + w] = arr

    for l in range(2):
        putr("ln1_g%d" % l, inp["ln1_g"][l])
        putr("ln1_b%d" % l, inp["ln1_b"][l])
        putr("ln2_g%d" % l, inp["ln2_g"][l])
        putr("ln2_b%d" % l, inp["ln2_b"][l])
    putr("b_pw2", inp["conv_b_pw2"][0])
    w_o = inp["ml_w_o"][0]
    b_o = inp["ml_b_o"][0]
    w_g = inp["ml_w_gate"][0]
    b_g = inp["ml_b_gate"][0]
    if flip:
        w_o = np.concatenate([w_o[:, E:], w_o[:, :E]], axis=1)
        b_o = np.concatenate([b_o[E:], b_o[:E]])
        perm = np.r_[8:16, 0:8]
        w_g = w_g[:, perm]
        b_g = b_g[perm]
    putr("b_o", b_o)
    putr("gn_g", inp["ml_gn_g"][0])
    putr("skip", inp["ml_skip"][0])
    putr("b_gate", b_g)
    m = {
        "xl": np.ascontiguousarray(xl), "cx": np.ascontiguousarray(cxx), "colv": colv, "rowv": rowv,
        "ada_w": inp["ada_w"],
        "conv_w_pw1": inp["conv_w_pw1"][0], "conv_w_pw2": inp["conv_w_pw2"][0],
        "mlp_w1_0": inp["mlp_w1"][0], "mlp_w2_0": inp["mlp_w2"][0],
        "mlp_w1_1": inp["mlp_w1"][1], "mlp_w2_1": inp["mlp_w2"][1],
        "ml_w_up": inp["ml_w_up"][0], "ml_w_q": inp["ml_w_q"][0], "ml_w_k": inp["ml_w_k"][0],
        "ml_w_v": inp["ml_w_v"][0], "ml_w_o": np.ascontiguousarray(w_o), "ml_w_down": inp["ml_w_down"][0],
        "w_gate": np.ascontiguousarray(w_g),
    }
    return {k: np.ascontiguousarray(v, dtype=np.float32) for k, v in m.items()}


def kernel(**inputs):
    inp = {k: np.asarray(v) for k, v in inputs.items()}
    B, SEQ = inp["x"].shape[0], inp["x"].shape[1]
    half = SEQ // 2
    cfg = Cfg(n_lat_tiles=half // TT)
    nc = build_program(cfg)
    cores = [(b, s) for b in range(B) for s in range(2)]
    ids = list(range(len(cores)))
    in_maps = [make_core_inputs(inp, b, s, cfg) for b, s in cores]
    res2 = run_bass_kernel_spmd(nc, in_maps, core_ids=ids)
    out = np.zeros((B, SEQ, D), np.float32)
    for c, (b, s) in enumerate(cores):
        o = np.asarray(res2.results[c]["out"])
        if s == 0:
            out[b, :half] = o
        else:
            out[b, half:] = o[::-1]
    return out
```

```python
import numpy as np
import ml_dtypes
import concourse.bass as bass
import concourse.mybir as mybir
from concourse.bass_utils import run_bass_kernel_spmd

F32 = mybir.dt.float32
BF16 = mybir.dt.bfloat16
AF = mybir.ActivationFunctionType
ALU = mybir.AluOpType
AX = mybir.AxisListType

D = 1024
E = 2048
NH = 4
DH = 512
DFF = 4096
ALPHA = 4.0 ** 0.25
LN_EPS = 1e-5
TT = 256
NS = 2
KSCALE = DH ** -0.5


class _Op:
    __slots__ = ("eng", "fn", "reads", "writes", "dma", "idx", "waits", "signal", "sig", "bar")

    def __init__(self, eng, fn, reads, writes, dma):
        self.eng = eng
        self.fn = fn
        self.reads = reads
        self.writes = writes
        self.dma = dma
        self.idx = -1
        self.waits = []
        self.signal = False
        self.sig = 0
        self.bar = False


class Sched:
    ENGS = ("pe", "act", "dve", "pool", "sp")

    def __init__(self):
        self.ops = []

    def op(self, eng, fn, reads=(), writes=(), dma=None):
        self.ops.append(_Op(eng, fn, tuple(reads), tuple(writes), dma))

    def barrier(self):
        for e in self.ENGS:
            op = _Op(e, lambda en: en.nop(), (), (), None)
            op.bar = True
            self.ops.append(op)

    def analyze(self):
        last_op = {}
        last_w = {}
        readers = {}
        eng_cnt = {e: 0 for e in self.ENGS}
        dma_cnt = {}
        known = {e: {} for e in self.ENGS}
        eng_ops = {e: [] for e in self.ENGS}
        for op in self.ops:
            deps = {}
            for k in op.reads:
                w = last_w.get(k)
                if w is not None:
                    deps[w] = True
            for k in op.writes:
                w = last_w.get(k)
                if w is not None:
                    deps[w] = True
                for r in readers.get(k, ()):
                    if r not in deps:
                        deps[r] = False
            need = {}
            for d, hard in deps.items():
                if d is op:
                    continue
                if d.dma is not None:
                    st = ("dma", d.dma)
                    tgt = dma_cnt[d.dma]
                else:
                    if d.eng == op.eng and op.dma is None:
                        if op.eng == "pe":
                            continue
                        if not hard:
                            continue
                    st = ("eng", d.eng)
                    tgt = d.idx
                if need.get(st, (-1,))[0] < tgt:
                    need[st] = (tgt, d)
            if op.bar:
                for f, d in last_op.items():
                    if f != op.eng:
                        need[("eng", f)] = (d.idx, d)
                for k, c in dma_cnt.items():
                    need[("dma", k)] = (c, None)
            kn = known[op.eng]
            for st, (tgt, d) in need.items():
                if kn.get(st, -1) >= tgt:
                    continue
                kn[st] = tgt
                if st[0] == "eng":
                    d.signal = True
                op.waits.append((st, tgt, d))
            if op.dma is not None:
                dma_cnt[op.dma] = dma_cnt.get(op.dma, 0) + 1
            else:
                op.idx = eng_cnt[op.eng]
                eng_cnt[op.eng] += 1
                last_op[op.eng] = op
            eng_ops[op.eng].append(op)
            for k in op.reads:
                readers.setdefault(k, []).append(op)
            for k in op.writes:
                last_w[k] = op
                readers[k] = []
        for e in self.ENGS:
            c = 0
            for op in eng_ops[e]:
                if op.dma is None and op.signal:
                    c += 1
                    op.sig = c
        self.eng_ops = eng_ops
        self.dma_cnt = dma_cnt

    def emit(self, nc, block, sems, dma_sems):
        def run(eng_name):
            def body(e):
                for op in self.eng_ops[eng_name]:
                    for st, tgt, d in op.waits:
                        if st[0] == "dma":
                            e.wait_ge(dma_sems[st[1]], 16 * tgt)
                        else:
                            e.wait_ge(sems[st[1]], d.sig)
                    ins = op.fn(e)
                    if op.dma is not None:
                        ins.then_inc(dma_sems[op.dma], 16)
                    elif op.signal:
                        ins.then_inc(sems[op.eng], 1)
                if eng_name == "sp":
                    for k, c in self.dma_cnt.items():
                        e.wait_ge(dma_sems[k], 16 * c)
            return body

        block.tensor(run("pe"))
        block.scalar(run("act"))
        block.vector(run("dve"))
        block.gpsimd(run("pool"))
        block.sync(run("sp"))


STAGES = [
    ("pw1", "conv_w_pw1", 1024, 0, 2048),
    ("pw2", "conv_w_pw2", 1024, 0, 1024),
    ("w1a", "mlp_w1_0", 1024, 0, 4096),
    ("w2a", "mlp_w2_0", 4096, 0, 1024),
    ("upx", "ml_w_up", 1024, 0, 2048),
    ("upz", "ml_w_up", 1024, 2048, 2048),
    ("wq", "ml_w_q", 2048, 0, 2048),
    ("wk", "ml_w_k", 2048, 0, 2048),
    ("wv", "ml_w_v", 2048, 0, 2048),
    ("wo", "ml_w_o", 2048, 0, 4096),
    ("down", "ml_w_down", 2048, 0, 1024),
    ("w1b", "mlp_w1_1", 1024, 0, 4096),
    ("w2b", "mlp_w2_1", 4096, 0, 1024),
]
STAGE = {s[0]: s for s in STAGES}


def stage_units(name):
    _, _, K, _, N = STAGE[name]
    return [(nb, ku) for nb in range(N // 512) for ku in range(K // 1024)]


UNIT_OFF = {}
_o = 0
for _s in STAGES:
    UNIT_OFF[_s[0]] = _o
    _o += len(stage_units(_s[0]))
N_UNITS = _o

COLS = {}
_c = 0
for _n, _w in [("ada_b0", 48), ("ada_b1", 48), ("b_pw1", 16), ("b_dw", 8), ("cln_g", 8), ("cln_b", 8),
               ("w_dw", 8 * 31), ("w_mc", 16 * 5), ("b_mc", 16), ("cT", 16), ("gn_g", 16), ("skip", 16)]:
    COLS[_n] = (_c, _w)
    _c += _w
NCOL = _c

ROWS = {}
_c = 0
for _n, _w in [("ln1_g0", D), ("ln1_b0", D), ("ln2_g0", D), ("ln2_b0", D),
               ("ln1_g1", D), ("ln1_b1", D), ("ln2_g1", D), ("ln2_b1", D),
               ("b_pw2", D), ("b_o", 2 * E), ("gn_g", E), ("skip", E), ("b_gate", 16)]:
    ROWS[_n] = (_c, _w)
    _c += _w
NROW = _c


class Cfg:
    def __init__(self, n_lat_tiles=16, phases=("p0", "p1", "p2", "p3"), debug=()):
        self.n_lat = n_lat_tiles
        self.n_all = 2 * n_lat_tiles
        self.n_tok = self.n_all * TT
        self.phases = phases
        self.debug = debug
        self.stop = 99
        self.p1_tiles = None


def build_program(cfg):
    nc = bass.Bass("TRN2", target_bir_lowering=False)
    S = Sched()
    NL = cfg.n_lat
    NA = cfg.n_all
    NTOK = cfg.n_tok

    def dram(name, shape, dt, kind="Internal"):
        return nc.dram_tensor(name, list(shape), dt, kind=kind).ap()

    xl = dram("xl", [NTOK, D], F32, "ExternalInput")
    cx = dram("cx", [TT, D], F32, "ExternalInput")
    colv = dram("colv", [128, NCOL], F32, "ExternalInput")
    rowv = dram("rowv", [1, NROW], F32, "ExternalInput")
    ada_w = dram("ada_w", [2, D, 6 * D], F32, "ExternalInput")
    wsrc = {}
    for nm, shp in [("conv_w_pw1", [D, 2 * D]), ("conv_w_pw2", [D, D]), ("mlp_w1_0", [D, DFF]),
                    ("mlp_w2_0", [DFF, D]), ("mlp_w1_1", [D, DFF]), ("mlp_w2_1", [DFF, D]),
                    ("ml_w_up", [D, 2 * E]), ("ml_w_q", [E, E]), ("ml_w_k", [E, E]), ("ml_w_v", [E, E]),
                    ("ml_w_o", [E, 2 * E]), ("ml_w_down", [E, D])]:
        wsrc[nm] = dram(nm, shp, F32, "ExternalInput")
    w_gate_in = dram("w_gate", [3 * E, 16], F32, "ExternalInput")

    dbg = set(cfg.debug)

    def scratch(name, shape, dt):
        return dram(name, shape, dt, "ExternalOutput" if name in dbg else "Internal")

    wbf = scratch("wbf", [N_UNITS, 128, 8 * 512], BF16)
    H1 = scratch("H1", [NTOK + TT, D], F32)
    XM = scratch("XM", [E, NTOK + TT], BF16)
    SZT = scratch("SZT", [E, NL * TT], BF16)
    NF = (NA + 1) * TT
    QT = scratch("QT", [E, NF], BF16)
    KT = scratch("KT", [E, NF], BF16)
    XC = scratch("XC", [E, NF], BF16)
    KTOK = scratch("KTOK", [NF, E], BF16)
    VTOK = scratch("VTOK", [NF, E], BF16)
    OAB = scratch("OAB", [NF, 2 * E], BF16)
    GS = scratch("GS", [NF, 32], F32)
    HOA = scratch("HOA", [NL * TT, E], F32)
    YT = scratch("YT", [E, NL * TT], BF16)
    out_d = dram("out", [NL * TT, D], F32, "ExternalOutput")

    from contextlib import ExitStack
    es = ExitStack()

    def sb(name, shape, dt):
        return es.enter_context(nc.sbuf_tensor(name, list(shape), dt))

    colt = sb("colt", [128, NCOL], F32)
    ident_f = sb("ident_f", [128, 128], F32)
    ident_b = sb("ident_b", [128, 128], BF16)
    ones_f = sb("ones_f", [128, 128], F32)
    modc = sb("modc", [128, 2, 48, 2], F32)
    wring = sb("wring", [128, 8, 8, 512], BF16)
    NWR = 8
    psum = es.enter_context(nc.psum_tensor("psum", [128, 8, 512], F32))

    def PS(bank):
        return [("ps", bank, 0), ("ps", bank, 1)]

    def dump(name, ap, keys):
        if name not in dbg:
            return
        t = dram("dbg_" + name, list(ap.shape), ap.dtype, "ExternalOutput")
        S.op("sp", lambda e: e.dma_start(out=t, in_=ap), reads=keys, writes=[("dbg", name)], dma=("dbg", name))

    def col(name, i=0, n=1):
        o, w = COLS[name]
        return colt[:, o + i:o + i + n]

    S.op("sp", lambda e: e.dma_start(out=colt[:], in_=colv[:, :]), writes=["colt"], dma="const")
    S.op("pool", lambda e: e.memset(ones_f[:], 1.0), writes=["ones_f"])
    S.op("pool", lambda e: e.memset(ident_f[:], 0.0), writes=["ident_f"])
    S.op("pool", lambda e: e.affine_select(out=ident_f[:], in_=ident_f[:], pattern=[[-1, 128]],
                                           compare_op=ALU.not_equal, fill=1.0, base=0, channel_multiplier=1),
         reads=["ident_f"], writes=["ident_f"])
    S.op("dve", lambda e: e.tensor_copy(out=ident_b[:], in_=ident_f[:]), reads=["ident_f"], writes=["ident_b"])

    wring_uses = [0]

    if "p0" in cfg.phases:
        with nc.sbuf_tensor("cvin", [128, 2, 8, 512], F32) as cvin, \
                nc.sbuf_tensor("cvout", [128, 2, 8, 512], BF16) as cvout:
            n = 0
            for (nm, src, K, c0, N) in STAGES:
                for (nb, ku) in stage_units(nm):
                    u = UNIT_OFF[nm] + nb * (K // 1024) + ku
                    sl = n % 2
                    srcap = wsrc[src][ku * 1024:(ku + 1) * 1024, c0 + nb * 512:c0 + (nb + 1) * 512] \
                        .rearrange("(c p) n -> p c n", p=128)
                    S.op("sp", lambda e, sl=sl, srcap=srcap: e.dma_start(out=cvin[:, sl], in_=srcap),
                         writes=[("cvin", sl)], dma=("cvin", sl))
                    eng = ("dve", "act", "pool")[n % 3]
                    if eng == "act":
                        S.op("act", lambda e, sl=sl: e.activation(out=cvout[:, sl], in_=cvin[:, sl], func=AF.Copy),
                             reads=[("cvin", sl)], writes=[("cvout", sl)])
                    else:
                        S.op(eng, lambda e, sl=sl: e.tensor_copy(out=cvout[:, sl], in_=cvin[:, sl]),
                             reads=[("cvin", sl)], writes=[("cvout", sl)])
                    S.op("pool", lambda e, sl=sl, u=u: e.dma_start(
                        out=wbf[u].rearrange("p (c n) -> p c n", c=8), in_=cvout[:, sl]),
                         reads=[("cvout", sl)], writes=[("wbf", u)], dma=("cvout", sl))
                    n += 1
            S.barrier()

    if "p0" in cfg.phases:
        with nc.sbuf_tensor("adaw", [128, 2, 8, 512], F32) as adaw, \
                nc.sbuf_tensor("scT", [128, 8, 2], F32) as scT:
            cT0 = COLS["cT"][0]
            S.op("act", lambda e: e.activation(out=scT[:].rearrange("p c r -> p (c r)"),
                                               in_=colt[:, cT0:cT0 + 16], func=AF.Silu),
                 reads=["colt"], writes=["scT"])
            n = 0
            for l in range(2):
                for nb in range(12):
                    sl = n % 2
                    srcap = ada_w[l, :, nb * 512:(nb + 1) * 512].rearrange("(c p) n -> p c n", p=128)
                    S.op("sp", lambda e, sl=sl, srcap=srcap: e.dma_start(out=adaw[:, sl], in_=srcap),
                         writes=[("adaw", sl)], dma=("adaw", sl))
                    bank = n % 2
                    for j in range(4):
                        for kc in range(8):
                            S.op("pe", lambda e, sl=sl, j=j, kc=kc, bank=bank: e.matmul(
                                psum[:, bank, j * 2:j * 2 + 2], adaw[:, sl, kc, j * 128:(j + 1) * 128],
                                scT[:, kc, :], start=(kc == 0), stop=(kc == 7)),
                                 reads=[("adaw", sl), "scT"], writes=PS(bank))
                    ab = COLS["ada_b%d" % l][0]
                    S.op("dve", lambda e, l=l, nb=nb, bank=bank, ab=ab: e.tensor_tensor(
                        out=modc[:, l, nb * 4:(nb + 1) * 4, :],
                        in0=psum[:, bank, 0:8].rearrange("p (j r) -> p j r", r=2),
                        in1=colt[:, ab + nb * 4:ab + nb * 4 + 4].unsqueeze(2).to_broadcast([128, 4, 2]),
                        op=ALU.add),
                         reads=PS(bank) + ["colt"], writes=["modc"])
                    n += 1
            for l in range(2):
                for c0 in (8, 32):
                    S.op("dve", lambda e, l=l, c0=c0: e.tensor_scalar_add(
                        out=modc[:, l, c0:c0 + 8, :], in0=modc[:, l, c0:c0 + 8, :], scalar1=1.0),
                         reads=["modc"], writes=["modc"])
            S.barrier()

    def load_unit(stage, nb, ku):
        _, _, K, _, N = STAGE[stage]
        u = UNIT_OFF[stage] + nb * (K // 1024) + ku
        slot = wring_uses[0] % NWR
        wring_uses[0] += 1
        S.op("sp", lambda e, slot=slot, u=u: e.dma_start(
            out=wring[:, slot], in_=wbf[u].rearrange("p (c n) -> p c n", c=8)),
             reads=[("wbf", u)], writes=[("wr", slot)], dma=("wr", slot))
        return slot

    def linear_fm(stage, xT, xkey, nsub, evac, banks=(0, 1, 2, 3), xoff=0):
        _, _, K, _, N = STAGE[stage]
        T = nsub * 128
        nku = K // 1024
        for nb in range(N // 512):
            slots = [load_unit(stage, nb, ku) for ku in range(nku)]
            for j in range(4):
                bank = banks[j]
                pk = PS(bank)
                for ku in range(nku):
                    for kc in range(8):
                        S.op("pe", lambda e, bank=bank, slot=slots[ku], kc=kc, j=j, ku=ku: e.matmul(
                            psum[:, bank, 0:T], wring[:, slot, kc, j * 128:(j + 1) * 128],
                            xT[:, ku * 8 + kc, xoff:xoff + T], start=(ku == 0 and kc == 0), stop=(ku == nku - 1 and kc == 7)),
                             reads=[("wr", slots[ku]), xkey], writes=pk)
                evac(nb * 4 + j, psum[:, bank, 0:T], pk)

    def linear_tm(stage, xT, xkey, nsub, evac, banks=(0, 1, 2, 3), bias=None, xoff=0):
        _, _, K, _, N = STAGE[stage]
        nku = K // 1024
        for nb in range(N // 512):
            pair = (nb % 2) * 2
            slots = [load_unit(stage, nb, ku) for ku in range(nku)]
            for j in range(nsub):
                bank = banks[pair + j]
                pk = PS(bank)
                first = True
                if bias is not None:
                    bt, bkey, boff = bias
                    S.op("pe", lambda e, bank=bank, nb=nb, bt=bt, boff=boff: e.matmul(
                        psum[:, bank, :], ones_hl[:, :], bt[:, boff + nb * 512:boff + (nb + 1) * 512],
                        start=True, stop=False), reads=[bkey, "ones_hl"], writes=pk)
                    first = False
                for ku in range(nku):
                    for kc in range(8):
                        S.op("pe", lambda e, bank=bank, slot=slots[ku], kc=kc, j=j, ku=ku, first=first: e.matmul(
                            psum[:, bank, :], xT[:, ku * 8 + kc, xoff + j * 128:xoff + (j + 1) * 128], wring[:, slot, kc, :],
                            start=(first and ku == 0 and kc == 0), stop=(ku == nku - 1 and kc == 7)),
                             reads=[("wr", slots[ku]), xkey], writes=pk)
                evac(nb, j, psum[:, bank, :], pk)

    ones_hl = sb("ones_hl", [128, 128], BF16)
    S.op("pool", lambda e: e.memset(ones_hl[:], 1.0), writes=["ones_hl"])

    def hilo_rows(name, dst, dkey, width, tmpf, tmpb):
        o, w = ROWS[name]
        assert w == width
        S.op("pool", lambda e: e.memset(dst[:, 0:w], 0.0), writes=[dkey])
        S.op("sp", lambda e: e.dma_start(out=tmpf[0:1, 0:w], in_=rowv[0:1, o:o + w]), writes=["hl_tmpf"], dma="hl")
        S.op("sp", lambda e: e.dma_start(out=tmpf[1:2, 0:w], in_=rowv[0:1, o:o + w]), writes=["hl_tmpf"], dma="hl")
        S.op("dve", lambda e: e.tensor_copy(out=tmpb[0:2, 0:w], in_=tmpf[0:2, 0:w]), reads=["hl_tmpf"], writes=["hl_tmpb"])
        S.op("dve", lambda e: e.tensor_tensor(out=tmpf[0:2, 0:w], in0=tmpf[0:2, 0:w], in1=tmpb[0:2, 0:w], op=ALU.subtract),
             reads=["hl_tmpf", "hl_tmpb"], writes=["hl_tmpf"])
        S.op("dve", lambda e: e.tensor_copy(out=dst[0:2, 0:w], in_=tmpf[0:2, 0:w]), reads=["hl_tmpf"], writes=[dkey])
        S.op("dve", lambda e: e.tensor_copy(out=dst[0:1, 0:w], in_=tmpb[0:1, 0:w]), reads=["hl_tmpb", dkey], writes=[dkey])

    def bcast_row(name, dst, dkey):
        o, w = ROWS[name]
        S.op("sp", lambda e: e.dma_start(out=dst[:, 0:w], in_=rowv[0:1, o:o + w].partition_broadcast(128)),
             writes=[dkey], dma="const")

    def gate_bcast(l, c0, which, dst, dkey, gtmp):
        for c in range(8):
            S.op("dve", lambda e, c=c: e.tensor_scalar_mul(out=gtmp[:], in0=ones_f[:], scalar1=modc[:, l, c0 + c, which:which + 1]),
                 reads=["ones_f", "modc"], writes=["gtmp"])
            S.op("pe", lambda e, c=c: e.matmul(psum[:, 7, c % 4 * 128:(c % 4 + 1) * 128], gtmp[:], ident_f[:], start=True, stop=True),
                 reads=["gtmp", "ident_f"], writes=PS(7))
            S.op("dve", lambda e, c=c: e.tensor_copy(out=dst[:, c * 128:(c + 1) * 128], in_=psum[:, 7, c % 4 * 128:(c % 4 + 1) * 128]),
                 reads=PS(7), writes=[dkey])

    def modulate_T(h, hkey, l, sc_c0, sh_c0, which, uT, ukey, nsub):
        for c in range(8):
            bank = 4 + (c % 2)
            for j in range(nsub):
                S.op("pe", lambda e, c=c, j=j, bank=bank: e.transpose(
                    psum[:, bank, j * 128:(j + 1) * 128], h[:, j, c * 128:(c + 1) * 128], ident_f[:]),
                     reads=[hkey, "ident_f"], writes=PS(bank))
            eng = "act" if c % 2 == 0 else "dve"
            if eng == "act":
                S.op("act", lambda e, c=c, bank=bank: e.activation(
                    out=uT[:, c, 0:nsub * 128], in_=psum[:, bank, 0:nsub * 128], func=AF.Identity,
                    scale=modc[:, l, sc_c0 + c, which:which + 1], bias=modc[:, l, sh_c0 + c, which:which + 1]),
                     reads=PS(bank) + ["modc"], writes=[ukey])
            else:
                S.op("dve", lambda e, c=c, bank=bank: e.tensor_scalar(
                    out=uT[:, c, 0:nsub * 128], in0=psum[:, bank, 0:nsub * 128],
                    scalar1=modc[:, l, sc_c0 + c, which:which + 1], scalar2=modc[:, l, sh_c0 + c, which:which + 1],
                    op0=ALU.mult, op1=ALU.add),
                     reads=PS(bank) + ["modc"], writes=[ukey])

    def resid_ln(h, hkey, j, yb, ybkey, g_bc, gkey, lng, lngkey, lnb, lnbkey, stat, tmp):
        S.op("dve", lambda e: e.tensor_tensor(out=yb, in0=yb, in1=g_bc[:, :], op=ALU.mult),
             reads=[ybkey, gkey], writes=[ybkey])
        S.op("dve", lambda e: e.scalar_tensor_tensor(out=yb, in0=h[:, j, :], scalar=float(ALPHA), in1=yb,
                                                     op0=ALU.mult, op1=ALU.add),
             reads=[ybkey, hkey], writes=[ybkey])
        for q in range(2):
            S.op("dve", lambda e, q=q: e.bn_stats(out=stat[:, q * 6:(q + 1) * 6], in_=yb[:, q * 512:(q + 1) * 512]),
                 reads=[ybkey], writes=["ln_stat"])
        S.op("dve", lambda e: e.bn_aggr(out=stat[:, 12:14], in_=stat[:, 0:12]), reads=["ln_stat"], writes=["ln_mv"])
        S.op("act", lambda e: e.activation(out=stat[:, 14:15], in_=stat[:, 13:14], func=AF.Sqrt, bias=eps_t[:, 0:1], scale=1.0),
             reads=["ln_mv", "eps_t"], writes=["ln_sd"])
        S.op("dve", lambda e: e.reciprocal(out=stat[:, 15:16], in_=stat[:, 14:15]), reads=["ln_sd"], writes=["ln_rs"])
        S.op("dve", lambda e: e.tensor_scalar(out=yb, in0=yb, scalar1=stat[:, 12:13], scalar2=stat[:, 15:16],
                                              op0=ALU.subtract, op1=ALU.mult),
             reads=[ybkey, "ln_mv", "ln_rs"], writes=[ybkey])
        S.op("pool", lambda e: e.tensor_tensor(out=yb, in0=yb, in1=lng[:, :], op=ALU.mult),
             reads=[ybkey, lngkey], writes=[ybkey])
        S.op("pool", lambda e: e.tensor_tensor(out=h[:, j, :], in0=yb, in1=lnb[:, :], op=ALU.add),
             reads=[ybkey, lnbkey], writes=[hkey])

    eps_t = sb("eps_t", [128, 1], F32)
    S.op("pool", lambda e: e.memset(eps_t[:], LN_EPS), writes=["eps_t"])

    if "p1" in cfg.phases:
        p1 = ExitStack()

        def sb1(name, shape, dt):
            return p1.enter_context(nc.sbuf_tensor(name, list(shape), dt))

        bc = {}
        for nm in ("g1l", "g1c", "g2l", "g2c", "ln1g", "ln1b", "ln2g", "ln2b", "bpw2"):
            bc[nm] = sb1("bc_" + nm, [128, D], F32)
        gtmp = sb1("gtmp", [128, 128], F32)
        hbuf = sb1("hbuf", [128, 2, NS, D], F32)
        uT = sb1("uT", [128, 8, TT], BF16)
        sig = sb1("sig", [128, 8, TT], F32)
        glu = sb1("glu", [128, 8, TT], F32)
        acc = sb1("acc", [128, 8, TT], F32)
        sqt = sb1("sqt", [128, 2, TT], F32)
        lnt = sb1("lnt", [128, 4, TT], F32)
        sT = sb1("sT", [128, 8, TT], BF16)
        ybuf = sb1("ybuf", [128, 2, D], F32)
        stat = sb1("stat", [128, 16], F32)
        hidr = sb1("hidr", [128, 2, TT], BF16)
        hidT = sb1("hidT", [128, 32, TT], BF16)
        xst = sb1("xst", [128, 2, 4, TT], BF16)
        zst = sb1("zst", [128, 2, 4, TT], BF16)

        gate_bcast(0, 16, 0, bc["g1l"], "bc_g1l", gtmp)
        gate_bcast(0, 16, 1, bc["g1c"], "bc_g1c", gtmp)
        gate_bcast(0, 40, 0, bc["g2l"], "bc_g2l", gtmp)
        gate_bcast(0, 40, 1, bc["g2c"], "bc_g2c", gtmp)
        bcast_row("ln1_g0", bc["ln1g"], "bc_ln1g")
        bcast_row("ln1_b0", bc["ln1b"], "bc_ln1b")
        bcast_row("ln2_g0", bc["ln2g"], "bc_ln2g")
        bcast_row("ln2_b0", bc["ln2b"], "bc_ln2b")
        bcast_row("b_pw2", bc["bpw2"], "bc_bpw2")

        wdw0 = COLS["w_dw"][0]
        tiles = [("ctx", 0)] + [("lat", i) for i in range(NA)]
        if cfg.p1_tiles is not None:
            tiles = tiles[:cfg.p1_tiles]
        for ti, (kind, idx) in enumerate(tiles):
            which = 1 if kind == "ctx" else 0
            hs = ti % 2
            h = hbuf[:, hs]
            hkey = ("h", hs)
            src = cx[:, :] if kind == "ctx" else xl[idx * TT:(idx + 1) * TT, :]
            S.op("sp", lambda e, hs=hs, src=src: e.dma_start(out=hbuf[:, hs], in_=src.rearrange("(j p) d -> p j d", p=128)),
                 writes=[hkey], dma=("h", hs))
            if cfg.stop <= 0:
                continue
            modulate_T(h, hkey, 0, 8, 0, which, uT, "uT", NS)

            def ev_pw1(ch, ps, pk):
                if ch >= 8:
                    c = ch - 8
                    if getattr(cfg, 'no_ev', None) == "act":
                        return
                    S.op("act", lambda e, c=c, ps=ps: e.activation(out=sig[:, c, :], in_=ps, func=(AF.Identity if getattr(cfg, "nosig", False) else AF.Sigmoid),
                                                                   bias=(0.0 if getattr(cfg, "nobias", False) else col("b_pw1", 8 + c)), scale=1.0),
                         reads=(pk if getattr(cfg, "nobias", False) else pk + ["colt"]), writes=[("sig", c)])
                else:
                    c = ch
                    if getattr(cfg, 'no_ev', None) == "dve":
                        return
                    S.op("dve", lambda e, c=c, ps=ps: e.scalar_tensor_tensor(
                        out=glu[:, c, :], in0=ps, scalar=col("b_pw1", c), in1=sig[:, c, :], op0=ALU.add, op1=ALU.mult),
                         reads=pk + ["colt", ("sig", c)], writes=[("glu", c)])

            if ti == 0:
                dump("modc", modc[:], ["modc"])
                dump("uT1", uT[:], ["uT"])
            if cfg.stop <= 1:
                continue
            _, _, K, _, N = STAGE["pw1"]
            for nb in (2, 0, 3, 1):
                slot = load_unit("pw1", nb, 0)
                for j in range(4):
                    bank = j
                    pk = PS(bank)
                    for kc in range(8):
                        S.op("pe", lambda e, bank=bank, slot=slot, kc=kc, j=j: e.matmul(
                            psum[:, bank, 0:TT], wring[:, slot, kc, j * 128:(j + 1) * 128],
                            uT[:, kc, :], start=(kc == 0), stop=(kc == 7)),
                             reads=[("wr", slot), "uT"], writes=pk)
                    if getattr(cfg, 'no_ev', None) is not True:
                        ev_pw1(nb * 4 + j, psum[:, bank, 0:TT], pk)
            if cfg.stop <= 2:
                continue
            RL = TT if kind == "ctx" else 64
            NR = TT // RL
            for c in range(8):
                gv = glu[:, c, :].rearrange("p (r t) -> p r t", t=RL)
                av = acc[:, c, :].rearrange("p (r t) -> p r t", t=RL)
                eng = "dve"
                S.op(eng, lambda e, c=c, gv=gv, av=av: e.tensor_scalar(
                    out=av, in0=gv, scalar1=colt[:, wdw0 + c * 31 + 15:wdw0 + c * 31 + 16], scalar2=col("b_dw", c),
                    op0=ALU.mult, op1=ALU.add), reads=[("glu", c), "colt"], writes=[("acc", c)])
                for k in range(31):
                    dlt = k - 15
                    if dlt == 0 or abs(dlt) >= RL:
                        continue
                    lo_o = max(0, -dlt)
                    hi_o = RL - max(0, dlt)
                    S.op(eng, lambda e, c=c, gv=gv, av=av, k=k, dlt=dlt, lo_o=lo_o, hi_o=hi_o: e.scalar_tensor_tensor(
                        out=av[:, :, lo_o:hi_o], in0=gv[:, :, lo_o + dlt:hi_o + dlt],
                        scalar=colt[:, wdw0 + c * 31 + k:wdw0 + c * 31 + k + 1], in1=av[:, :, lo_o:hi_o],
                        op0=ALU.mult, op1=ALU.add), reads=[("glu", c), ("acc", c), "colt"], writes=[("acc", c)])
            if ti == 0:
                dump("glu", glu[:], [("glu", c) for c in range(8)])
            if cfg.stop <= 3:
                continue
            for c in range(8):
                S.op("pe", lambda e, c=c: e.matmul(psum[:, 6, 0:TT], ones_f[:], acc[:, c, :], start=(c == 0), stop=(c == 7)),
                     reads=["ones_f", ("acc", c)], writes=PS(6))
            for c in range(8):
                S.op("act", lambda e, c=c: e.activation(out=sqt[:, c % 2, :], in_=acc[:, c, :], func=AF.Square),
                     reads=[("acc", c)], writes=[("sqt", c % 2)])
                S.op("pe", lambda e, c=c: e.matmul(psum[:, 7, 0:TT], ones_f[:], sqt[:, c % 2, :], start=(c == 0), stop=(c == 7)),
                     reads=["ones_f", ("sqt", c % 2)], writes=PS(7))
            S.op("act", lambda e: e.activation(out=lnt[:, 0, :], in_=psum[:, 6, 0:TT], func=AF.Copy, scale=1.0 / D),
                 reads=PS(6), writes=["lnt0"])
            S.op("dve", lambda e: e.tensor_tensor(out=lnt[:, 2, :], in0=lnt[:, 0, :], in1=lnt[:, 0, :], op=ALU.mult),
                 reads=["lnt0"], writes=["lnt2"])
            S.op("dve", lambda e: e.scalar_tensor_tensor(out=lnt[:, 2, :], in0=psum[:, 7, 0:TT], scalar=1.0 / D,
                                                         in1=lnt[:, 2, :], op0=ALU.mult, op1=ALU.subtract),
                 reads=PS(7) + ["lnt2"], writes=["lnt2"])
            S.op("act", lambda e: e.activation(out=lnt[:, 3, :], in_=lnt[:, 2, :], func=AF.Sqrt, bias=eps_t[:, 0:1], scale=1.0),
                 reads=["lnt2", "eps_t"], writes=["lnt3"])
            S.op("dve", lambda e: e.reciprocal(out=lnt[:, 1, :], in_=lnt[:, 3, :]), reads=["lnt3"], writes=["lnt1"])
            for c in range(8):
                eng = "dve" if c % 2 == 0 else "pool"
                S.op(eng, lambda e, c=c: e.tensor_tensor(out=acc[:, c, :], in0=acc[:, c, :], in1=lnt[:, 0, :], op=ALU.subtract),
                     reads=[("acc", c), "lnt0"], writes=[("acc", c)])
                S.op(eng, lambda e, c=c: e.tensor_tensor(out=acc[:, c, :], in0=acc[:, c, :], in1=lnt[:, 1, :], op=ALU.mult),
                     reads=[("acc", c), "lnt1"], writes=[("acc", c)])
                S.op("act", lambda e, c=c: e.activation(out=sT[:, c, :], in_=acc[:, c, :], func=AF.Silu,
                                                        scale=col("cln_g", c), bias=col("cln_b", c)),
                     reads=[("acc", c), "colt"], writes=["sT"])
            if ti == 0:
                dump("acc", acc[:], [("acc", c) for c in range(8)])
            if cfg.stop <= 4:
                continue
            g1 = ("g1c" if which else "g1l")
            g2 = ("g2c" if which else "g2l")

            def ev_tm(gname, lg, lb, bias_bc=None):
                def ev(nb, j, ps, pk):
                    if bias_bc is None:
                        S.op("act", lambda e, j=j, nb=nb, ps=ps: e.activation(out=ybuf[:, j, nb * 512:(nb + 1) * 512], in_=ps, func=AF.Copy),
                             reads=pk, writes=[("yb", j)])
                    else:
                        S.op("dve", lambda e, j=j, nb=nb, ps=ps: e.tensor_tensor(out=ybuf[:, j, nb * 512:(nb + 1) * 512], in0=ps,
                                                                                in1=bc[bias_bc][:, nb * 512:(nb + 1) * 512], op=ALU.add),
                             reads=pk + ["bc_" + bias_bc], writes=[("yb", j)])
                    if nb == 1:
                        resid_ln(h, hkey, j, ybuf[:, j, :], ("yb", j), bc[gname], "bc_" + gname,
                                 bc[lg], "bc_" + lg, bc[lb], "bc_" + lb, stat, None)
                return ev

            linear_tm("pw2", sT, "sT", NS, ev_tm(g1, "ln1g", "ln1b", "bpw2"))
            if ti == 0:
                dump("sT", sT[:], ["sT"])
                dump("h1mid", hbuf[:, hs], [hkey])
                dump("stat", stat[:], ["ln_stat", "ln_mv", "ln_sd", "ln_rs"])
                dump("yb", ybuf[:], [("yb", 0), ("yb", 1)])
            if cfg.stop <= 5:
                continue
            modulate_T(h, hkey, 0, 32, 24, which, uT, "uT", NS)

            def ev_w1(ch, ps, pk):
                S.op("act", lambda e, ch=ch, ps=ps: e.activation(out=hidr[:, ch % 2, :], in_=ps, func=AF.Relu),
                     reads=pk, writes=[("hidr", ch % 2)])
                S.op("pool", lambda e, ch=ch: e.tensor_tensor(out=hidT[:, ch, :], in0=hidr[:, ch % 2, :], in1=hidr[:, ch % 2, :], op=ALU.mult),
                     reads=[("hidr", ch % 2)], writes=["hidT"])

            linear_fm("w1a", uT, "uT", NS, ev_w1)
            linear_tm("w2a", hidT, "hidT", NS, ev_tm(g2, "ln2g", "ln2b"))
            if ti == 0:
                dump("hmid", hbuf[:, hs], [hkey])
            if cfg.stop <= 6:
                continue
            row0 = NTOK if kind == "ctx" else idx * TT
            if kind == "lat" and idx < NL:
                S.op("pool", lambda e, hs=hs, row0=row0: e.dma_start(
                    out=H1[row0:row0 + TT, :].rearrange("(j p) d -> p j d", p=128), in_=hbuf[:, hs]),
                     reads=[hkey], writes=[("H1", kind, idx)], dma=("hst", hs))
            if cfg.stop <= 7:
                continue
            modulate_T(h, hkey, 1, 8, 0, which, uT, "uT", NS)
            colbase = (NTOK if kind == "ctx" else idx * TT)

            def ev_upx(ch, ps, pk):
                s = (ch // 4) % 2
                S.op("act" if ch % 2 else "dve",
                     (lambda e, ch=ch, ps=ps, s=s: e.activation(out=xst[:, s, ch % 4, :], in_=ps, func=AF.Copy)) if ch % 2 else
                     (lambda e, ch=ch, ps=ps, s=s: e.tensor_copy(out=xst[:, s, ch % 4, :], in_=ps)),
                     reads=pk, writes=[("xst", s)])
                if ch % 4 == 3:
                    nb = ch // 4
                    S.op("pool", lambda e, s=s, nb=nb, colbase=colbase: e.dma_start(
                        out=XM[nb * 512:(nb + 1) * 512, colbase:colbase + TT].rearrange("(c p) t -> p c t", p=128),
                        in_=xst[:, s]), reads=[("xst", s)], writes=["XM"], dma=("xst", s))

            linear_fm("upx", uT, "uT", NS, ev_upx)
            if kind == "lat" and idx < NL:
                def ev_upz(ch, ps, pk):
                    sl = (ch // 4) % 2
                    S.op("act", lambda e, ch=ch, ps=ps, sl=sl: e.activation(out=zst[:, sl, ch % 4, :], in_=ps, func=AF.Silu),
                         reads=pk, writes=[("zst", sl)])
                    if ch % 4 == 3:
                        nb = ch // 4
                        S.op("pool", lambda e, sl=sl, nb=nb, idx=idx: e.dma_start(
                            out=SZT[nb * 512:(nb + 1) * 512, idx * TT:(idx + 1) * TT].rearrange("(c p) t -> p c t", p=128),
                            in_=zst[:, sl]), reads=[("zst", sl)], writes=[("SZT", idx)], dma=("zst", sl))

                linear_fm("upz", uT, "uT", NS, ev_upz)
        S.barrier()
        p1.close()


    maskA = sb("maskA", [128, 128], F32)
    maskB = sb("maskB", [128, 128], F32)
    ones_b = sb("ones_b", [128, 4], BF16)
    cst = sb("cst", [128, 4], F32)
    S.op("pool", lambda e: e.memset(maskA[:], 1.0), writes=["maskA"])
    S.op("pool", lambda e: e.affine_select(out=maskA[:], in_=maskA[:], pattern=[[1, 128]], compare_op=ALU.is_ge,
                                           fill=0.0, base=0, channel_multiplier=-1), reads=["maskA"], writes=["maskA"])
    S.op("pool", lambda e: e.memset(maskB[:], 1.0), writes=["maskB"])
    S.op("pool", lambda e: e.affine_select(out=maskB[:], in_=maskB[:], pattern=[[-1, 128]], compare_op=ALU.is_ge,
                                           fill=0.0, base=0, channel_multiplier=1), reads=["maskB"], writes=["maskB"])
    S.op("pool", lambda e: e.memset(ones_b[:], 1.0), writes=["ones_b"])
    S.op("pool", lambda e: e.memset(cst[:, 0:1], LN_EPS), writes=["cst"])
    S.op("pool", lambda e: e.memset(cst[:, 1:2], float(np.log(KSCALE))), writes=["cst"])
    S.op("pool", lambda e: e.memset(cst[:, 2:3], 1.0), writes=["cst"])
    S.op("pool", lambda e: e.memset(cst[:, 3:4], 0.0), writes=["cst"])

    ftiles = [("ctx", 0)] + [("lat", i) for i in range(NA)]

    if "p2" in cfg.phases:
        p2 = ExitStack()

        def sb2(name, shape, dt):
            return p2.enter_context(nc.sbuf_tensor(name, list(shape), dt))

        xmT = sb2("xmT", [128, 16, TT + 4], BF16)
        cacc = sb2("cacc", [128, 2, TT], F32)
        xcT = sb2("xcT", [128, 16, TT], BF16)
        qT = sb2("qT", [128, 16, TT], BF16)
        kT = sb2("kT", [128, 16, TT], BF16)
        vT = sb2("vT", [128, 16, TT], BF16)
        wg_f = sb2("wg_f", [128, 48, 16], F32)
        wg = sb2("wg", [128, 48, 16], BF16)
        bg_bc = sb2("bg_bc", [128, 16], F32)
        bo_bc = sb2("bo_bc", [128, 2 * E], F32)
        tst = sb2("tst", [128, 2, E], BF16)
        ost = sb2("ost", [128, 2, 2 * E], BF16)
        otmp = sb2("otmp", [128, 2, 512], F32)
        gsb = sb2("gsb", [128, 16], F32)
        gl1 = sb2("gl1", [128, 8], F32)
        gcum = sb2("gcum", [128, 16], F32)
        gtt = sb2("gtt", [128, 2, 8], F32)
        gso = sb2("gso", [128, 2, 32], F32)

        S.op("sp", lambda e: e.dma_start(out=wg_f[:], in_=w_gate_in.rearrange("(c p) n -> p c n", p=128)), writes=["wg_f"], dma="const2")
        S.op("dve", lambda e: e.tensor_copy(out=wg[:], in_=wg_f[:]), reads=["wg_f"], writes=["wg"])
        bcast_row("b_gate", bg_bc, "bg_bc")
        bcast_row("b_o", bo_bc, "bo_bc")
        wmc0 = COLS["w_mc"][0]

        for f, (kind, idx) in enumerate(ftiles):
            is_ctx = kind == "ctx"
            own = (kind == "lat" and idx < NL)
            S.op("pool", lambda e: e.memset(xmT[:, :, 0:2], 0.0), writes=["xmT"])
            S.op("pool", lambda e: e.memset(xmT[:, :, TT + 2:TT + 4], 0.0), writes=["xmT"])
            if is_ctx:
                c0, c1, d0 = NTOK, NTOK + TT, 2
            else:
                c0 = max(0, idx * TT - 2)
                c1 = min(idx * TT + TT + 2, NTOK)
                d0 = 2 - (idx * TT - c0)
            S.op("sp", lambda e, c0=c0, c1=c1, d0=d0: e.dma_start(
                out=xmT[:, :, d0:d0 + (c1 - c0)], in_=XM[:, c0:c1].rearrange("(c p) t -> p c t", p=128)),
                 reads=["XM"], writes=["xmT"], dma="xmT")
            for c in range(16):
                cs = c % 2
                S.op("dve", lambda e, c=c, cs=cs: e.tensor_scalar(
                    out=cacc[:, cs, :], in0=xmT[:, c, 2:2 + TT], scalar1=colt[:, wmc0 + c * 5 + 2:wmc0 + c * 5 + 3],
                    scalar2=col("b_mc", c), op0=ALU.mult, op1=ALU.add), reads=["xmT", "colt"], writes=[("cacc", cs)])
                for k in (0, 1, 3, 4):
                    S.op("dve", lambda e, c=c, cs=cs, k=k: e.scalar_tensor_tensor(
                        out=cacc[:, cs, :], in0=xmT[:, c, k:k + TT], scalar=colt[:, wmc0 + c * 5 + k:wmc0 + c * 5 + k + 1],
                        in1=cacc[:, cs, :], op0=ALU.mult, op1=ALU.add), reads=["xmT", "colt", ("cacc", cs)], writes=[("cacc", cs)])
                S.op("act", lambda e, c=c, cs=cs: e.activation(out=xcT[:, c, :], in_=cacc[:, cs, :], func=AF.Silu),
                     reads=[("cacc", cs)], writes=["xcT"])
            if own:
                S.op("pool", lambda e, f=f: e.dma_start(out=XC[:, f * TT:(f + 1) * TT].rearrange("(c p) t -> p c t", p=128), in_=xcT[:]),
                     reads=["xcT"], writes=[("XC", f)], dma="xcst")

            def ev_copy(dst, dkey):
                def ev(ch, ps, pk):
                    if ch % 2:
                        S.op("act", lambda e, ch=ch, ps=ps: e.activation(out=dst[:, ch, :], in_=ps, func=AF.Copy), reads=pk, writes=[dkey])
                    else:
                        S.op("dve", lambda e, ch=ch, ps=ps: e.tensor_copy(out=dst[:, ch, :], in_=ps), reads=pk, writes=[dkey])
                return ev

            linear_fm("wq", xcT, "xcT", NS, ev_copy(qT, "qT"))
            if own:
                S.op("pool", lambda e, f=f: e.dma_start(out=QT[:, f * TT:(f + 1) * TT].rearrange("(c p) t -> p c t", p=128), in_=qT[:]),
                     reads=["qT"], writes=[("QT", f)], dma="qst")
            linear_fm("wk", xcT, "xcT", NS, ev_copy(kT, "kT"))
            if own:
                S.op("pool", lambda e, f=f: e.dma_start(out=KT[:, f * TT:(f + 1) * TT].rearrange("(c p) t -> p c t", p=128), in_=kT[:]),
                     reads=["kT"], writes=[("KT", f)], dma="kst")
            linear_fm("wv", xmT, "xmT", NS, ev_copy(vT, "vT"), xoff=2)

            for j in range(NS):
                n = 0
                for (src, skey, base) in ((qT, "qT", 0), (kT, "kT", 16), (vT, "vT", 32)):
                    for c in range(16):
                        S.op("pe", lambda e, src=src, c=c, j=j, base=base, n=n: e.matmul(
                            psum[:, 6, 0:16], src[:, c, j * 128:(j + 1) * 128], wg[:, base + c, :], start=(n == 0), stop=(n == 47)),
                             reads=[skey, "wg"], writes=PS(6))
                        n += 1
                S.op("dve", lambda e: e.tensor_tensor(out=gsb[:], in0=psum[:, 6, 0:16], in1=bg_bc[:], op=ALU.add),
                     reads=PS(6) + ["bg_bc"], writes=["gsb"])
                gv = gsb[:].rearrange("p (d g h) -> p d g h", d=2, g=2)
                S.op("act", lambda e, gv=gv: e.activation(out=gl1[:].rearrange("p (d h) -> p d h", d=2), in_=gv[:, :, 1, :], func=AF.Exp, scale=-1.0),
                     reads=["gsb"], writes=["gl1"])
                S.op("act", lambda e: e.activation(out=gl1[:], in_=gl1[:], func=AF.Ln, bias=cst[:, 2:3], scale=1.0),
                     reads=["gl1", "cst"], writes=["gl1"])
                S.op("pe", lambda e: e.matmul(psum[:, 7, 0:4], maskA[:], gl1[:, 0:4], start=True, stop=True), reads=["maskA", "gl1"], writes=PS(7))
                S.op("pe", lambda e: e.matmul(psum[:, 7, 4:8], maskB[:], gl1[:, 4:8], start=True, stop=True), reads=["maskB", "gl1"], writes=PS(7))
                S.op("pe", lambda e: e.matmul(psum[:, 7, 8:16], ones_f[:], gl1[:, 0:8], start=True, stop=True), reads=["ones_f", "gl1"], writes=PS(7))
                S.op("dve", lambda e: e.tensor_copy(out=gcum[:], in_=psum[:, 7, 0:16]), reads=PS(7), writes=["gcum"])
                ncum = gcum[:, 0:8].rearrange("p (d h) -> p d h", d=2)
                ntot = gcum[:, 8:16].rearrange("p (d h) -> p d h", d=2)
                go = gso[:, j, :].rearrange("p (d k h) -> p d k h", d=2, k=4)
                S.op("act", lambda e, go=go, ncum=ncum: e.activation(out=go[:, :, 0, :], in_=ncum, func=AF.Exp, scale=-1.0),
                     reads=["gcum"], writes=[("gso", j)])
                S.op("dve", lambda e, gv=gv, ncum=ncum: e.tensor_tensor(out=gtt[:, :, 0:4], in0=gv[:, :, 0, :], in1=ncum, op=ALU.add),
                     reads=["gsb", "gcum"], writes=["gtt"])
                S.op("dve", lambda e, ntot=ntot: e.tensor_tensor(out=gtt[:, :, 4:8], in0=gtt[:, :, 0:4], in1=ntot, op=ALU.subtract),
                     reads=["gtt", "gcum"], writes=["gtt"])
                S.op("act", lambda e, go=go: e.activation(out=go[:, :, 1, :], in_=gtt[:, :, 0:4], func=AF.Exp, bias=cst[:, 1:2], scale=1.0),
                     reads=["gtt", "cst"], writes=[("gso", j)])
                S.op("act", lambda e, go=go: e.activation(out=go[:, :, 2, :], in_=gtt[:, :, 4:8], func=AF.Exp, bias=cst[:, 1:2], scale=1.0),
                     reads=["gtt", "cst"], writes=[("gso", j)])
                S.op("act", lambda e, go=go, ntot=ntot: e.activation(out=go[:, :, 3, :], in_=ntot, func=AF.Exp, scale=-1.0),
                     reads=["gcum"], writes=[("gso", j)])
                S.op("pool", lambda e, f=f, j=j: e.dma_start(out=GS[f * TT + j * 128:f * TT + (j + 1) * 128, :], in_=gso[:, j, :]),
                     reads=[("gso", j)], writes=[("GS", f)], dma=("gso", j))

            def ev_tok(DST, dname):
                def ev(nb, j, ps, pk):
                    if nb % 2:
                        S.op("act", lambda e, nb=nb, j=j, ps=ps: e.activation(out=tst[:, j, nb * 512:(nb + 1) * 512], in_=ps, func=AF.Copy),
                             reads=pk, writes=[("tst", j)])
                    else:
                        S.op("dve", lambda e, nb=nb, j=j, ps=ps: e.tensor_copy(out=tst[:, j, nb * 512:(nb + 1) * 512], in_=ps),
                             reads=pk, writes=[("tst", j)])
                    if nb == 3:
                        S.op("pool", lambda e, j=j, f=f: e.dma_start(out=DST[f * TT + j * 128:f * TT + (j + 1) * 128, :], in_=tst[:, j, :]),
                             reads=[("tst", j)], writes=[(dname, f)], dma=("tst", j))
                return ev

            linear_tm("wk", xcT, "xcT", NS, ev_tok(KTOK, "KTOK"))
            linear_tm("wv", xmT, "xmT", NS, ev_tok(VTOK, "VTOK"), xoff=2)
            if own:
                def ev_o(nb, j, ps, pk):
                    S.op("dve", lambda e, nb=nb, j=j, ps=ps: e.tensor_tensor(out=otmp[:, j, :], in0=ps, in1=bo_bc[:, nb * 512:(nb + 1) * 512], op=ALU.add),
                         reads=pk + ["bo_bc"], writes=[("otmp", j)])
                    S.op("act", lambda e, nb=nb, j=j: e.activation(out=ost[:, j, nb * 512:(nb + 1) * 512], in_=otmp[:, j, :], func=AF.Sigmoid),
                         reads=[("otmp", j)], writes=[("ost", j)])
                    if nb == 7:
                        S.op("pool", lambda e, j=j, f=f: e.dma_start(out=OAB[f * TT + j * 128:f * TT + (j + 1) * 128, :], in_=ost[:, j, :]),
                             reads=[("ost", j)], writes=[("OAB", f)], dma=("ost", j))

                linear_tm("wo", xmT, "xmT", NS, ev_o, xoff=2)
        S.barrier()
        p2.close()

    if "p3" in cfg.phases:
        p3 = ExitStack()

        def sb3(name, shape, dt):
            return p3.enter_context(nc.sbuf_tensor(name, list(shape), dt))

        Cst = sb3("Cst", [128, 16, 512], F32)
        Cbf = sb3("Cbf", [128, 16, 512], BF16)
        nst = sb3("nst", [128, 16], F32)
        nbf = sb3("nbf", [128, 16], BF16)
        qS = sb3("qS", [128, 16, TT], BF16)
        kS = sb3("kS", [128, 16, TT], BF16)
        ktS = sb3("ktS", [128, 2, E], BF16)
        vtS = sb3("vtS", [128, 2, E], BF16)
        oS = sb3("oS", [128, 2, E], BF16)
        gS = sb3("gS", [128, 2, 32], F32)
        stS = sb3("stS", [128, 128], BF16)
        kgS = sb3("kgS", [128, 512], BF16)
        rS = sb3("rS", [128, 4], F32)
        hob = sb3("hob", [128, E], F32)
        hoa = sb3("hoa", [128, E], F32)
        xcS = sb3("xcS", [128, 16, TT], BF16)
        szS = sb3("szS", [128, 16, TT], BF16)
        yTs = sb3("yTs", [128, 16, TT], BF16)
        gst = sb3("gst", [128, 16], F32)
        ytmp = sb3("ytmp", [128, 2, 128], F32)
        ytm2 = sb3("ytm2", [128, 2, 128], F32)

        def state_init_zero():
            S.op("pool", lambda e: e.memset(Cst[:], 0.0), writes=["Cst"])
            S.op("pool", lambda e: e.memset(Cbf[:], 0.0), writes=["Cbf"])
            S.op("pool", lambda e: e.memset(nst[:], 0.0), writes=["nst"])
            S.op("pool", lambda e: e.memset(nbf[:], 0.0), writes=["nbf"])

        def scan_tile(dirn, f, is_ctx, lat_idx):
            do = dirn * 16
            if not is_ctx:
                S.op("sp", lambda e: e.dma_start(out=qS[:], in_=QT[:, f * TT:(f + 1) * TT].rearrange("(c p) t -> p c t", p=128)),
                     reads=[("QT", f)], writes=["qS"], dma="qS")
                S.op("sp", lambda e: e.dma_start(out=kS[:], in_=KT[:, f * TT:(f + 1) * TT].rearrange("(c p) t -> p c t", p=128)),
                     reads=[("KT", f)], writes=["kS"], dma="kS")
            S.op("sp", lambda e: e.dma_start(out=ktS[:], in_=KTOK[f * TT:(f + 1) * TT, :].rearrange("(j p) d -> p j d", p=128)),
                 reads=[("KTOK", f)], writes=["ktS"], dma="ktS")
            S.op("sp", lambda e: e.dma_start(out=vtS[:], in_=VTOK[f * TT:(f + 1) * TT, :].rearrange("(j p) d -> p j d", p=128)),
                 reads=[("VTOK", f)], writes=["vtS"], dma="vtS")
            S.op("sp", lambda e: e.dma_start(out=gS[:], in_=GS[f * TT:(f + 1) * TT, :].rearrange("(j p) d -> p j d", p=128)),
                 reads=[("GS", f)], writes=["gS"], dma="gS")
            if not is_ctx:
                S.op("sp", lambda e: e.dma_start(out=oS[:], in_=OAB[f * TT:(f + 1) * TT, dirn * E:(dirn + 1) * E].rearrange("(j p) d -> p j d", p=128)),
                     reads=[("OAB", f)], writes=["oS"], dma="oS")
                if dirn == 1:
                    S.op("sp", lambda e: e.dma_start(out=xcS[:], in_=XC[:, f * TT:(f + 1) * TT].rearrange("(c p) t -> p c t", p=128)),
                         reads=[("XC", f)], writes=["xcS"], dma="xcS")
                    S.op("sp", lambda e: e.dma_start(out=szS[:], in_=SZT[:, lat_idx * TT:(lat_idx + 1) * TT].rearrange("(c p) t -> p c t", p=128)),
                         reads=[("SZT", lat_idx)], writes=["szS"], dma="szS")
            mask = maskA if dirn == 0 else maskB
            mkey = "maskA" if dirn == 0 else "maskB"
            for j in ((0, 1) if dirn == 0 else (1, 0)):
                tok0 = lat_idx * TT + j * 128
                if (not is_ctx) and dirn == 1:
                    S.op("sp", lambda e, tok0=tok0: e.dma_start(out=hoa[:], in_=HOA[tok0:tok0 + 128, :]),
                         reads=[("HOA", lat_idx, j)], writes=["hoa"], dma="hoa")
                for h in range(NH):
                    tsl = slice(j * 128, (j + 1) * 128)
                    hsl = slice(h * 512, (h + 1) * 512)
                    gcol = lambda kind, h=h, j=j: gS[:, j, do + kind * 4 + h:do + kind * 4 + h + 1]
                    if not is_ctx:
                        for dc in range(4):
                            S.op("pe", lambda e, h=h, dc=dc, tsl=tsl: e.matmul(psum[:, 0, 0:128], kS[:, h * 4 + dc, tsl], qS[:, h * 4 + dc, tsl],
                                                                               start=(dc == 0), stop=(dc == 3)),
                                 reads=["kS", "qS"], writes=PS(0))
                        S.op("dve", lambda e, gcol=gcol: e.scalar_tensor_tensor(out=stS[:], in0=psum[:, 0, 0:128], scalar=gcol(1), in1=mask[:],
                                                                                op0=ALU.mult, op1=ALU.mult),
                             reads=PS(0) + ["gS", mkey], writes=["stS"])
                        S.op("pe", lambda e, j=j, hsl=hsl: e.matmul(psum[:, 1, :], stS[:], vtS[:, j, hsl], start=True, stop=False),
                             reads=["stS", "vtS"], writes=PS(1))
                        for dc in range(4):
                            S.op("pe", lambda e, h=h, dc=dc, tsl=tsl: e.matmul(psum[:, 1, :], qS[:, h * 4 + dc, tsl], Cbf[:, h * 4 + dc, :],
                                                                               start=False, stop=(dc == 3)),
                                 reads=["qS", "Cbf"], writes=PS(1))
                        S.op("pe", lambda e: e.matmul(psum[:, 2, 0:1], stS[:], ones_b[:, 0:1], start=True, stop=False),
                             reads=["stS", "ones_b"], writes=PS(2))
                        for dc in range(4):
                            S.op("pe", lambda e, h=h, dc=dc, tsl=tsl: e.matmul(psum[:, 2, 0:1], qS[:, h * 4 + dc, tsl], nbf[:, h * 4 + dc:h * 4 + dc + 1],
                                                                               start=False, stop=(dc == 3)),
                                 reads=["qS", "nbf"], writes=PS(2))
                        S.op("dve", lambda e, gcol=gcol: e.tensor_tensor(out=rS[:, 0:1], in0=psum[:, 2, 0:1], in1=gcol(0), op=ALU.mult),
                             reads=PS(2) + ["gS"], writes=["rS0"])
                        S.op("act", lambda e: e.activation(out=rS[:, 1:2], in_=rS[:, 0:1], func=AF.Abs),
                             reads=["rS0"], writes=["rS1"])
                        S.op("dve", lambda e: e.tensor_scalar_max(out=rS[:, 1:2], in0=rS[:, 1:2], scalar1=1.0),
                             reads=["rS1"], writes=["rS1"])
                        S.op("dve", lambda e: e.reciprocal(out=rS[:, 2:3], in_=rS[:, 1:2]), reads=["rS1"], writes=["rS2"])
                        S.op("dve", lambda e, gcol=gcol: e.tensor_tensor(out=rS[:, 3:4], in0=rS[:, 2:3], in1=gcol(0), op=ALU.mult),
                             reads=["rS2", "gS"], writes=["rS3"])
                        S.op("dve", lambda e, j=j, hsl=hsl: e.scalar_tensor_tensor(out=hob[:, hsl], in0=psum[:, 1, :], scalar=rS[:, 3:4],
                                                                                   in1=oS[:, j, hsl], op0=ALU.mult, op1=ALU.mult),
                             reads=PS(1) + ["rS3", "oS"], writes=[("hob", h)])
                        if dirn == 1:
                            S.op("pool", lambda e, hsl=hsl: e.tensor_tensor(out=hob[:, hsl], in0=hob[:, hsl], in1=hoa[:, hsl], op=ALU.add),
                                 reads=[("hob", h), "hoa"], writes=[("hob", h)])
                    S.op("pool", lambda e, j=j, hsl=hsl, gcol=gcol: e.tensor_scalar_mul(out=kgS[:], in0=ktS[:, j, hsl], scalar1=gcol(2)),
                         reads=["ktS", "gS"], writes=["kgS"])
                    for dc in range(4):
                        S.op("pe", lambda e, j=j, hsl=hsl, dc=dc: e.matmul(psum[:, 3 + dc, :], kgS[:, dc * 128:(dc + 1) * 128], vtS[:, j, hsl],
                                                                           start=True, stop=True),
                             reads=["kgS", "vtS"], writes=PS(3 + dc))
                        S.op("dve", lambda e, h=h, dc=dc, gcol=gcol: e.scalar_tensor_tensor(
                            out=Cst[:, h * 4 + dc, :], in0=Cst[:, h * 4 + dc, :], scalar=gcol(3), in1=psum[:, 3 + dc, :],
                            op0=ALU.mult, op1=ALU.add), reads=PS(3 + dc) + ["gS", "Cst"], writes=["Cst"])
                        S.op("act", lambda e, h=h, dc=dc: e.activation(out=Cbf[:, h * 4 + dc, :], in_=Cst[:, h * 4 + dc, :], func=AF.Copy),
                             reads=["Cst"], writes=["Cbf"])
                    for dc in range(4):
                        S.op("pe", lambda e, dc=dc: e.matmul(psum[:, 7, dc:dc + 1], kgS[:, dc * 128:(dc + 1) * 128], ones_b[:, 0:1],
                                                             start=True, stop=True),
                             reads=["kgS", "ones_b"], writes=PS(7))
                    S.op("dve", lambda e, h=h, gcol=gcol: e.scalar_tensor_tensor(
                        out=nst[:, h * 4:h * 4 + 4], in0=nst[:, h * 4:h * 4 + 4], scalar=gcol(3), in1=psum[:, 7, 0:4],
                        op0=ALU.mult, op1=ALU.add), reads=PS(7) + ["gS", "nst"], writes=["nst"])
                    S.op("dve", lambda e, h=h: e.tensor_copy(out=nbf[:, h * 4:h * 4 + 4], in_=nst[:, h * 4:h * 4 + 4]),
                         reads=["nst"], writes=["nbf"])
                if is_ctx:
                    continue
                if dirn == 0:
                    S.op("pool", lambda e, tok0=tok0: e.dma_start(out=HOA[tok0:tok0 + 128, :], in_=hob[:]),
                         reads=[("hob", h) for h in range(NH)], writes=[("HOA", lat_idx, j)], dma="hob")
                else:
                    for h in range(NH):
                        hsl = slice(h * 512, (h + 1) * 512)
                        S.op("dve", lambda e, hsl=hsl: e.bn_stats(out=gst[:, 0:6], in_=hob[:, hsl]), reads=[("hob", h)], writes=["gst0"])
                        S.op("dve", lambda e: e.bn_aggr(out=gst[:, 6:8], in_=gst[:, 0:6]), reads=["gst0"], writes=["gst1"])
                        S.op("act", lambda e: e.activation(out=gst[:, 8:9], in_=gst[:, 7:8], func=AF.Sqrt, bias=cst[:, 0:1], scale=1.0),
                             reads=["gst1", "cst"], writes=["gst2"])
                        S.op("dve", lambda e: e.reciprocal(out=gst[:, 9:10], in_=gst[:, 8:9]), reads=["gst2"], writes=["gst3"])
                        S.op("dve", lambda e, hsl=hsl: e.tensor_scalar(out=hob[:, hsl], in0=hob[:, hsl], scalar1=gst[:, 6:7], scalar2=gst[:, 9:10],
                                                                       op0=ALU.subtract, op1=ALU.mult),
                             reads=[("hob", h), "gst1", "gst3"], writes=[("hob", h)])
                    for c in range(16):
                        pb = c % 4
                        S.op("pe", lambda e, c=c, pb=pb: e.transpose(psum[:, 7, pb * 128:(pb + 1) * 128], hob[:, c * 128:(c + 1) * 128], ident_f[:]),
                             reads=[("hob", c // 4), "ident_f"], writes=PS(7))
                        ys = c % 2
                        S.op("pool", lambda e, c=c, ys=ys, tsl=tsl: e.tensor_scalar_mul(out=ytmp[:, ys, :], in0=xcS[:, c, tsl], scalar1=col("skip", c)),
                             reads=["xcS", "colt"], writes=[("ytmp", ys)])
                        S.op("dve", lambda e, c=c, ys=ys, pb=pb: e.scalar_tensor_tensor(
                            out=ytm2[:, ys, :], in0=psum[:, 7, pb * 128:(pb + 1) * 128], scalar=col("gn_g", c), in1=ytmp[:, ys, :],
                            op0=ALU.mult, op1=ALU.add), reads=PS(7) + ["colt", ("ytmp", ys)], writes=[("ytm2", ys)])
                        S.op("pool", lambda e, c=c, ys=ys, tsl=tsl: e.tensor_tensor(out=yTs[:, c, tsl], in0=ytm2[:, ys, :], in1=szS[:, c, tsl], op=ALU.mult),
                             reads=[("ytm2", ys), "szS"], writes=["yTs"])
            if (not is_ctx) and dirn == 1:
                S.op("pool", lambda e: e.dma_start(out=YT[:, lat_idx * TT:(lat_idx + 1) * TT].rearrange("(c p) t -> p c t", p=128), in_=yTs[:]),
                     reads=["yTs"], writes=[("YT", lat_idx)], dma="yst")

        state_init_zero()
        for f in range(0, NL + 1):
            scan_tile(0, f, f == 0, ftiles[f][1])
        state_init_zero()
        scan_tile(1, 0, True, 0)
        for f in range(NA, NL, -1):
            scan_tile(1, f, True, ftiles[f][1])
        for f in range(NL, 0, -1):
            scan_tile(1, f, False, ftiles[f][1])
        S.barrier()
        p3.close()

        p4 = ExitStack()

        def sb4(name, shape, dt):
            return p4.enter_context(nc.sbuf_tensor(name, list(shape), dt))

        bd = {}
        for nm in ("g1", "g2", "ln1g", "ln1b", "ln2g", "ln2b"):
            bd[nm] = sb4("bd_" + nm, [128, D], F32)
        gtmp4v = sb4("gtmp4", [128, 128], F32)
        hbuf4v = sb4("hbuf4", [128, 2, NS, D], F32)
        yin = sb4("yin", [128, 2, 16, TT], BF16)
        uT4v = sb4("uT4", [128, 8, TT], BF16)
        ybuf4v = sb4("ybuf4", [128, 2, D], F32)
        stat4v = sb4("stat4", [128, 16], F32)
        hidr4v = sb4("hidr4", [128, 2, TT], BF16)
        hidT4v = sb4("hidT4", [128, 32, TT], BF16)
        gate_bcast(1, 16, 0, bd["g1"], "bd_g1", gtmp4v)
        gate_bcast(1, 40, 0, bd["g2"], "bd_g2", gtmp4v)
        bcast_row("ln1_g1", bd["ln1g"], "bd_ln1g")
        bcast_row("ln1_b1", bd["ln1b"], "bd_ln1b")
        bcast_row("ln2_g1", bd["ln2g"], "bd_ln2g")
        bcast_row("ln2_b1", bd["ln2b"], "bd_ln2b")
        for i in range(NL):
            hs = i % 2
            h = hbuf4v[:, hs]
            hkey = ("h4", hs)
            S.op("sp", lambda e, hs=hs, i=i: e.dma_start(out=hbuf4v[:, hs], in_=H1[i * TT:(i + 1) * TT, :].rearrange("(j p) d -> p j d", p=128)),
                 reads=[("H1", "lat", i)], writes=[hkey], dma=("h4", hs))
            S.op("sp", lambda e, hs=hs, i=i: e.dma_start(out=yin[:, hs], in_=YT[:, i * TT:(i + 1) * TT].rearrange("(c p) t -> p c t", p=128)),
                 reads=[("YT", i)], writes=[("yin", hs)], dma=("yin", hs))

            def ev_tm4(gname, lg, lb):
                def ev(nb, j, ps, pk):
                    S.op("act", lambda e, j=j, nb=nb, ps=ps: e.activation(out=ybuf4v[:, j, nb * 512:(nb + 1) * 512], in_=ps, func=AF.Copy),
                         reads=pk, writes=[("yb4", j)])
                    if nb == 1:
                        resid_ln(h, hkey, j, ybuf4v[:, j, :], ("yb4", j), bd[gname], "bd_" + gname,
                                 bd[lg], "bd_" + lg, bd[lb], "bd_" + lb, stat4v, None)
                return ev

            linear_tm("down", yin[:, hs], ("yin", hs), NS, ev_tm4("g1", "ln1g", "ln1b"))
            modulate_T(h, hkey, 1, 32, 24, 0, uT4v, "uT4", NS)

            def ev_w14(ch, ps, pk):
                S.op("act", lambda e, ch=ch, ps=ps: e.activation(out=hidr4v[:, ch % 2, :], in_=ps, func=AF.Relu),
                     reads=pk, writes=[("hidr4", ch % 2)])
                S.op("pool", lambda e, ch=ch: e.tensor_tensor(out=hidT4v[:, ch, :], in0=hidr4v[:, ch % 2, :], in1=hidr4v[:, ch % 2, :], op=ALU.mult),
                     reads=[("hidr4", ch % 2)], writes=["hidT4"])

            linear_fm("w1b", uT4v, "uT4", NS, ev_w14)
            linear_tm("w2b", hidT4v, "hidT4", NS, ev_tm4("g2", "ln2g", "ln2b"))
            S.op("pool", lambda e, hs=hs, i=i: e.dma_start(out=out_d[i * TT:(i + 1) * TT, :].rearrange("(j p) d -> p j d", p=128), in_=hbuf4v[:, hs]),
                 reads=[hkey], writes=[("out", i)], dma=("ost4", hs))
        p4.close()

    S.analyze()
    dma_keys = list(S.dma_cnt.keys())
    with ExitStack() as ss:
        sems = {e: ss.enter_context(nc.semaphore("sem_" + e)) for e in Sched.ENGS}
        dma_sems = {k: ss.enter_context(nc.semaphore("dsem%d" % i)) for i, k in enumerate(dma_keys)}
        with nc.Block() as block:
            S.emit(nc, block, sems, dma_sems)
    es.close()
    return nc


def _colmajor(v, nchunk):
    return np.ascontiguousarray(v.reshape(nchunk, 128).T)


def make_core_inputs(inp, b, s, cfg):
    NTOK = cfg.n_tok
    own = cfg.n_lat * TT
    seq = inp["x"].shape[1]
    half = seq // 2
    x = inp["x"][b]
    ctx = inp["ctx"][b]
    if s == 0:
        xl = x[0:half + TT][:NTOK] if NTOK <= half + TT else None
        xl = x[0:NTOK]
        cxx = ctx
    else:
        xr = x[::-1]
        xl = xr[0:NTOK]
        cxx = ctx[::-1]
    flip = (s == 1)
    w_dw = inp["conv_w_dw"][0]
    w_mc = inp["ml_w_conv"][0]
    if flip:
        w_dw = w_dw[::-1]
        w_mc = w_mc[::-1]
    colv = np.zeros((128, NCOL), np.float32)

    def put(name, arr):
        o, w = COLS[name]
        assert arr.shape == (128, w), (name, arr.shape, w)
        colv[:, o:o + w] = arr

    put("ada_b0", _colmajor(inp["ada_b"][0], 48))
    put("ada_b1", _colmajor(inp["ada_b"][1], 48))
    put("b_pw1", _colmajor(inp["conv_b_pw1"][0], 16))
    put("b_dw", _colmajor(inp["conv_b_dw"][0], 8))
    put("cln_g", _colmajor(inp["conv_ln_g"][0], 8))
    put("cln_b", _colmajor(inp["conv_ln_b"][0], 8))
    put("w_dw", np.ascontiguousarray(w_dw.T.reshape(8, 128, 31).transpose(1, 0, 2)).reshape(128, 8 * 31))
    put("w_mc", np.ascontiguousarray(w_mc.T.reshape(16, 128, 5).transpose(1, 0, 2)).reshape(128, 16 * 5))
    put("b_mc", _colmajor(inp["ml_b_conv"][0], 16))
    cT = np.stack([_colmajor(inp["c"][b], 8), _colmajor(inp["c_ctx"], 8)], axis=-1)
    put("cT", cT.reshape(128, 16))
    put("gn_g", _colmajor(inp["ml_gn_g"][0], 16))
    put("skip", _colmajor(inp["ml_skip"][0], 16))

    rowv = np.zeros((1, NROW), np.float32)

    def putr(name, arr):
        o, w = ROWS[name]
        assert arr.shape == (w,), (name, arr.shape)
        rowv[0, o:o + w] = arr

    for l in range(2):
        putr("ln1_g%d" % l, inp["ln1_g"][l])
        putr("ln1_b%d" % l, inp["ln1_b"][l])
        putr("ln2_g%d" % l, inp["ln2_g"][l])
        putr("ln2_b%d" % l, inp["ln2_b"][l])
    putr("b_pw2", inp["conv_b_pw2"][0])
    w_o = inp["ml_w_o"][0]
    b_o = inp["ml_b_o"][0]
    w_g = inp["ml_w_gate"][0]
    b_g = inp["ml_b_gate"][0]
    if flip:
        w_o = np.concatenate([w_o[:, E:], w_o[:, :E]], axis=1)
        b_o = np.concatenate([b_o[E:], b_o[:E]])
        perm = np.r_[8:16, 0:8]
        w_g = w_g[:, perm]
        b_g = b_g[perm]
    putr("b_o", b_o)
    putr("gn_g", inp["ml_gn_g"][0])
    putr("skip", inp["ml_skip"][0])
    putr("b_gate", b_g)
    m = {
        "xl": np.ascontiguousarray(xl), "cx": np.ascontiguousarray(cxx), "colv": colv, "rowv": rowv,
        "ada_w": inp["ada_w"],
        "conv_w_pw1": inp["conv_w_pw1"][0], "conv_w_pw2": inp["conv_w_pw2"][0],
        "mlp_w1_0": inp["mlp_w1"][0], "mlp_w2_0": inp["mlp_w2"][0],
        "mlp_w1_1": inp["mlp_w1"][1], "mlp_w2_1": inp["mlp_w2"][1],
        "ml_w_up": inp["ml_w_up"][0], "ml_w_q": inp["ml_w_q"][0], "ml_w_k": inp["ml_w_k"][0],
        "ml_w_v": inp["ml_w_v"][0], "ml_w_o": np.ascontiguousarray(w_o), "ml_w_down": inp["ml_w_down"][0],
        "w_gate": np.ascontiguousarray(w_g),
    }
    return {k: np.ascontiguousarray(v, dtype=np.float32) for k, v in m.items()}


def kernel(**inputs):
    inp = {k: np.asarray(v) for k, v in inputs.items()}
    B, SEQ = inp["x"].shape[0], inp["x"].shape[1]
    half = SEQ // 2
    cfg = Cfg(n_lat_tiles=half // TT)
    nc = build_program(cfg)
    cores = [(b, s) for b in range(B) for s in range(2)]
    ids = list(range(len(cores)))
    in_maps = [make_core_inputs(inp, b, s, cfg) for b, s in cores]
    res2 = run_bass_kernel_spmd(nc, in_maps, core_ids=ids)
    out = np.zeros((B, SEQ, D), np.float32)
    for c, (b, s) in enumerate(cores):
        o = np.asarray(res2.results[c]["out"])
        if s == 0:
            out[b, :half] = o
        else:
            out[b, half:] = o[::-1]
    return out
```
